# Optimizing a Trainium2 kernel written in Bass

```python
import math
import jax, jax.numpy as jnp
from jax import lax
import numpy as np

D_MODEL = 2048
BATCH = 1
SEQ = 8192
DEPTH = 1

CHUNK = 64
EPS = 1e-6
A_HEADS = 8
A_HEAD_DIM = 128
A_WIDTH = A_HEADS * A_HEAD_DIM
B_HEADS = 8
B_HEAD_DIM = 128
B_WIDTH = B_HEADS * B_HEAD_DIM
Q_RANK = 512
KV_RANK = 256
IDX_HEADS = 16
IDX_DIM = 128
TOPK_MAX = 256
Q_BLOCK = 128
REL_BUCKETS = 32
REL_MAX_DIST = 128

SPLIT_SIZES = (A_WIDTH, A_WIDTH, A_WIDTH, A_WIDTH,
               Q_RANK, KV_RANK, IDX_DIM, IDX_HEADS, B_WIDTH,
               D_MODEL, D_MODEL)
SPLIT_POINTS = tuple(sum(SPLIT_SIZES[:i + 1]) for i in range(len(SPLIT_SIZES) - 1))
IN_WIDTH = sum(SPLIT_SIZES)

kernel_name = 'hgrn2_dsa_gated_hybrid_block'

F32 = jnp.float32


def rmsnorm(x, w):
    x32 = x.astype(F32)
    y = x32 * lax.rsqrt(jnp.mean(x32 * x32, axis=-1, keepdims=True) + EPS) * w.astype(F32)
    return y.astype(x.dtype)


def layernorm(x, w, b):
    x32 = x.astype(F32)
    mu = jnp.mean(x32, axis=-1, keepdims=True)
    var = jnp.mean(jnp.square(x32 - mu), axis=-1, keepdims=True)
    y = (x32 - mu) * lax.rsqrt(var + EPS) * w.astype(F32) + b.astype(F32)
    return y.astype(x.dtype)


def t5_bucket(rel):
    half = REL_BUCKETS // 2
    max_exact = half // 2
    base = jnp.where(rel > 0, half, 0)
    n = jnp.abs(rel)
    large = max_exact + (jnp.log(jnp.maximum(n, 1).astype(F32) / max_exact)
                         / math.log(REL_MAX_DIST / max_exact) * (half - max_exact)).astype(jnp.int32)
    large = jnp.minimum(large, half - 1)
    return base + jnp.where(n < max_exact, n, large)


def hgrn2_recurrence(q, k, log_f, v):
    b_, s_, h_, dk = q.shape
    dv = v.shape[-1]
    nc = s_ // CHUNK

    def to_chunks(t):
        return jnp.moveaxis(t.reshape(b_, nc, CHUNK, h_, t.shape[-1]), 1, 0)

    causal = jnp.tril(jnp.ones((CHUNK, CHUNK), bool))

    def step(state, inp):
        q_c, k_c, g_c, v_c = inp
        cum = jnp.cumsum(g_c, axis=1)
        inter = jnp.einsum('bthk,bhkv->bthv', q_c * jnp.exp(cum), state)
        diff = cum[:, :, None] - cum[:, None, :]
        decay = jnp.exp(jnp.where(causal[None, :, :, None, None], diff, -jnp.inf))
        scores = jnp.einsum('bthk,bshk,btshk->bhts', q_c, k_c, decay)
        intra = jnp.einsum('bhts,bshv->bthv', scores, v_c)
        last = cum[:, -1]
        new_state = (jnp.exp(last)[..., None] * state
                     + jnp.einsum('bshk,bshv->bhkv', k_c * jnp.exp(last[:, None] - cum), v_c))
        return new_state, inter + intra

    init = jnp.zeros((b_, h_, dk, dv), F32)
    _, out = lax.scan(step, init, (to_chunks(q), to_chunks(k), to_chunks(log_f), to_chunks(v)))
    return jnp.moveaxis(out, 0, 1).reshape(b_, s_, h_, dv)


def hgrn2_branch(a_q, a_f, a_i, a_g, lb, gnorm_w):
    b_, s_, _ = a_q.shape
    shp = (b_, s_, A_HEADS, A_HEAD_DIM)
    q = (jax.nn.silu(a_q.astype(F32)) * A_HEAD_DIM ** -0.5).reshape(shp)
    f = lb + (1.0 - lb) * jax.nn.sigmoid(a_f.astype(F32))
    k = (1.0 - f).reshape(shp)
    log_f = jnp.log(f).reshape(shp)
    v = a_i.astype(F32).reshape(shp)
    o = hgrn2_recurrence(q, k, log_f, v)
    o = rmsnorm(o, gnorm_w).reshape(b_, s_, A_WIDTH)
    return (o * jax.nn.silu(a_g.astype(F32))).astype(a_q.dtype)


def dsa_branch(c_q, c_kv, k_idx_raw, w_idx_raw, b_g, q_norm_w, kv_norm_w, w_uq, w_qidx,
               w_ukv, kidx_norm_w, kidx_norm_b, rel_bias, topk):
    b_, s_, _ = c_q.shape
    nb = s_ // Q_BLOCK
    cq = rmsnorm(c_q, q_norm_w)
    q = (cq @ w_uq).reshape(b_, s_, B_HEADS, B_HEAD_DIM)
    q_idx = (cq @ w_qidx).reshape(b_, s_, IDX_HEADS, IDX_DIM)
    kv = (rmsnorm(c_kv, kv_norm_w) @ w_ukv).reshape(b_, s_, B_HEADS, 2 * B_HEAD_DIM)
    k, v = kv[..., :B_HEAD_DIM], kv[..., B_HEAD_DIM:]
    k_idx = layernorm(k_idx_raw, kidx_norm_w, kidx_norm_b).astype(F32)
    w_idx = w_idx_raw.astype(F32) * (IDX_HEADS ** -0.5 * IDX_DIM ** -0.5)
    key_chunk = jnp.arange(s_) // CHUNK
    bias_table = rel_bias.astype(F32)

    def to_blocks(t):
        return jnp.moveaxis(t.reshape((b_, nb, Q_BLOCK) + t.shape[2:]), 1, 0)

    def attend_block(inp):
        q_blk, qi_blk, wi_blk, t0 = inp
        t_pos = t0 + jnp.arange(Q_BLOCK)
        rel_scores = jax.nn.relu(jnp.einsum('bthd,bsd->bths', qi_blk.astype(F32), k_idx))
        score = jnp.einsum('bths,bth->bts', rel_scores, wi_blk)
        visible = key_chunk[None, :] <= (t_pos // CHUNK)[:, None]
        score = jnp.where(visible[None], score, -jnp.inf)
        top_val, top_idx = lax.top_k(score, topk)
        valid = jnp.isfinite(top_val)
        k_sel = jax.vmap(lambda kk, ii: kk[ii])(k, top_idx)
        v_sel = jax.vmap(lambda vv, ii: vv[ii])(v, top_idx)
        logits = jnp.einsum('bthd,btkhd->bthk', q_blk, k_sel).astype(F32) * B_HEAD_DIM ** -0.5
        bias = jnp.moveaxis(bias_table[t5_bucket(top_idx - t_pos[None, :, None])], -1, 2)
        logits = jnp.where(valid[:, :, None, :], logits + bias, -jnp.inf)
        probs = jax.nn.softmax(logits, axis=-1).astype(v.dtype)
        return jnp.einsum('bthk,btkhd->bthd', probs, v_sel)

    out = lax.map(attend_block, (to_blocks(q), to_blocks(q_idx), to_blocks(w_idx),
                                 jnp.arange(nb, dtype=jnp.int32) * Q_BLOCK))
    out = jnp.moveaxis(out, 0, 1).reshape(b_, s_, B_WIDTH)
    return out * jax.nn.silu(b_g)


def setup_inputs(seed: int = 0) -> dict:
    key = jax.random.key(seed)
    ks = jax.random.split(key, 20)
    nrm = lambda k, shape, scale: jax.random.normal(k, shape, F32) * scale
    return {
        'x': nrm(ks[0], (BATCH, SEQ, D_MODEL), 1.0),
        'norm_w': 1.0 + nrm(ks[1], (DEPTH, D_MODEL), 0.02),
        'w_in': nrm(ks[2], (DEPTH, D_MODEL, IN_WIDTH), D_MODEL ** -0.5),
        'lb_table': nrm(ks[3], (DEPTH + 1, A_WIDTH), 0.1),
        'gnorm_a': 1.0 + nrm(ks[4], (DEPTH, A_HEAD_DIM), 0.02),
        'q_norm_w': 1.0 + nrm(ks[5], (DEPTH, Q_RANK), 0.02),
        'kv_norm_w': 1.0 + nrm(ks[6], (DEPTH, KV_RANK), 0.02),
        'w_uq': nrm(ks[7], (DEPTH, Q_RANK, B_WIDTH), Q_RANK ** -0.5),
        'w_qidx': nrm(ks[8], (DEPTH, Q_RANK, IDX_HEADS * IDX_DIM), Q_RANK ** -0.5),
        'w_ukv': nrm(ks[9], (DEPTH, KV_RANK, 2 * B_WIDTH), KV_RANK ** -0.5),
        'kidx_norm_w': 1.0 + nrm(ks[10], (DEPTH, IDX_DIM), 0.02),
        'kidx_norm_b': nrm(ks[11], (DEPTH, IDX_DIM), 0.02),
        'w_pa': nrm(ks[12], (DEPTH, A_WIDTH, D_MODEL), A_WIDTH ** -0.5),
        'w_pb': nrm(ks[13], (DEPTH, B_WIDTH, D_MODEL), B_WIDTH ** -0.5),
        'w_out': nrm(ks[14], (DEPTH, D_MODEL, D_MODEL), D_MODEL ** -0.5),
        'rel_bias': nrm(ks[15], (REL_BUCKETS, B_HEADS), 0.5),
        'final_norm_w': 1.0 + nrm(ks[16], (D_MODEL,), 0.02),
    }


def reference(x, norm_w, w_in, lb_table, gnorm_a, q_norm_w, kv_norm_w, w_uq, w_qidx, w_ukv,
              kidx_norm_w, kidx_norm_b, w_pa, w_pb, w_out, rel_bias, final_norm_w):
    s_ = x.shape[1]
    topk = min(TOPK_MAX, s_ // 4)
    lb_all = jnp.cumsum(jax.nn.softmax(lb_table.astype(F32), axis=0), axis=0)
    for layer in range(DEPTH):
        h = rmsnorm(x, norm_w[layer])
        proj = h @ w_in[layer]
        (a_q, a_f, a_i, a_g, c_q, c_kv, k_idx_raw, w_idx_raw, b_g, m_a, m_b) = jnp.split(
            proj, SPLIT_POINTS, axis=-1)
        y_a = hgrn2_branch(a_q, a_f, a_i, a_g, lb_all[layer], gnorm_a[layer])
        y_b = dsa_branch(c_q, c_kv, k_idx_raw, w_idx_raw, b_g, q_norm_w[layer], kv_norm_w[layer],
                         w_uq[layer], w_qidx[layer], w_ukv[layer], kidx_norm_w[layer],
                         kidx_norm_b[layer], rel_bias, topk)
        merged = (jax.nn.sigmoid(m_a) * (y_a @ w_pa[layer])
                  + jax.nn.sigmoid(m_b) * (y_b @ w_pb[layer]))
        x = x + merged @ w_out[layer]
    return rmsnorm(x, final_norm_w)
```

```python
import math
from contextlib import ExitStack
import numpy as np
import ml_dtypes
import jax
import jax.numpy as jnp
import concourse.bass as bass
import concourse.mybir as mybir
from concourse.bass_utils import run_bass_kernel_spmd

F32 = mybir.dt.float32
BF16 = mybir.dt.bfloat16
ALU = mybir.AluOpType
AF = mybir.ActivationFunctionType
AX = mybir.AxisListType

NCORE = 8
NB = 8
T = 128
DM = 2048
INW = 10128
EPS = 1e-6
SCALE_A = 128 ** -0.5
SCALE_B = 128 ** -0.5
WIDX_C = 16 ** -0.5 * 128 ** -0.5
NEG = -1.0e30
NBIS = 18
TOPK = 256.0


class Prog:
    def __init__(self, engines):
        self.sem = {}
        self.nins = {n: 0 for n in engines}
        self.known = {n: {} for n in engines}
        self.lastw = {}
        self.reads = {}
        self.dsem = {}
        self.dcnt = {}
        self.dnext = {}
        self.acts = {n: [] for n in engines}
        self.needed = {n: set() for n in engines}
        self.n_inst = 0

    def finalize(self):
        self.val = {}
        for en, need in self.needed.items():
            v = 0
            for idx in sorted(need):
                v += 1
                self.val[(en, idx)] = v

    def emit(self, en, e):
        for a in self.acts[en]:
            if a[0] == "w":
                ev = a[1]
                if ev[0] == "e":
                    e.wait_ge(self.sem[ev[1]], self.val[(ev[1], ev[2])])
                else:
                    e.wait_ge(ev[1], ev[2])
            elif a[0] == "c":
                a[1](e).then_inc(a[2])
            elif a[0] == "d":
                a[1](e).then_inc(a[2], 16)
            else:
                inst = a[1](e)
                if a[2] in self.needed[en]:
                    inst.then_inc(self.sem[en], 1)

    def add_engine_sem(self, name, sem):
        self.sem[name] = sem

    def add_dma_sems(self, name, sems):
        self.dsem[name] = list(sems)
        self.dcnt[name] = [0] * len(sems)
        self.dnext[name] = 0

    def _deps(self, reads, writes):
        deps = []
        for k in list(reads) + list(writes):
            ev = self.lastw.get(k)
            if ev is not None:
                deps.append(ev)
        for k in writes:
            deps.extend(self.reads.get(k, []))
        return deps

    @staticmethod
    def _kv(ev):
        if ev[0] == "e":
            return ("e", ev[1]), ev[2] + 1
        return ("s", id(ev[1])), ev[2]

    def _wait(self, en, deps):
        best = {}
        for ev in deps:
            if ev[0] == "e" and ev[1] == en and en == "pe":
                continue
            k, v = self._kv(ev)
            if v > best.get(k, (0, None))[0]:
                best[k] = (v, ev)
        for k, (v, ev) in best.items():
            if self.known[en].get(k, 0) >= v:
                continue
            self.acts[en].append(("w", ev))
            if ev[0] == "e":
                self.needed[ev[1]].add(ev[2])
            self.known[en][k] = v

    def _commit(self, ev, reads, writes):
        for k in writes:
            self.lastw[k] = ev
            self.reads[k] = []
        for k in reads:
            if k in writes:
                continue
            lst = self.reads.setdefault(k, [])
            lst.append(ev)
            if len(lst) > 48:
                best = {}
                for e2 in lst:
                    kk, v = self._kv(e2)
                    if v > best.get(kk, (0, None))[0]:
                        best[kk] = (v, e2)
                self.reads[k] = [x[1] for x in best.values()]

    def op(self, en, fn, reads=(), writes=()):
        self._wait(en, self._deps(reads, writes))
        idx = self.nins[en]
        self.nins[en] += 1
        self.acts[en].append(("i", fn, idx))
        ev = ("e", en, idx)
        self._commit(ev, reads, writes)
        self.n_inst += 1
        return ev

    def dma(self, en, stream, out, in_, reads=(), writes=()):
        i = self.dnext[stream]
        self.dnext[stream] = (i + 1) % len(self.dsem[stream])
        sem = self.dsem[stream][i]
        deps = self._deps(reads, writes)
        if self.dcnt[stream][i] > 0 and en != "pool":
            deps.append(("d", sem, self.dcnt[stream][i]))
        self._wait(en, deps)
        self.dcnt[stream][i] += 16
        self.acts[en].append(("d", (lambda e, out=out, in_=in_: e.dma_start(out=out, in_=in_)), sem))
        ev = ("d", sem, self.dcnt[stream][i])
        self._commit(ev, reads, writes)
        self.n_inst += 1
        return ev

    def alias_last(self, keys):
        evs = [self.lastw[k] for k in keys]
        newest = max(evs, key=lambda ev: self._kv(ev)[1])
        for k in keys:
            self.lastw[k] = newest

    def collective(self, fn, sem, reads=(), writes=()):
        en = "pool"
        self._wait(en, self._deps(reads, writes))
        self.acts[en].append(("c", fn, sem))
        ev = ("d", sem, 1)
        self.acts[en].append(("w", ev))
        self.known[en][("s", id(sem))] = 1
        self._commit(ev, reads, writes)

    def wait_all(self, en):
        deps = []
        for n in self.nins:
            if self.nins[n] > 0:
                deps.append(("e", n, self.nins[n] - 1))
        for n, ss in self.dsem.items():
            for s_, c in zip(ss, self.dcnt[n]):
                if c > 0:
                    deps.append(("d", s_, c))
        self._wait(en, deps)

    def barrier(self):
        for en in self.acts:
            self.wait_all(en)


def _t5_bucket_np(rel):
    rel = jnp.asarray(rel, jnp.int32)
    half = 16
    max_exact = 8
    base = jnp.where(rel > 0, half, 0)
    n = jnp.abs(rel)
    large = max_exact + (jnp.log(jnp.maximum(n, 1).astype(jnp.float32) / max_exact)
                         / math.log(128 / max_exact) * (half - max_exact)).astype(jnp.int32)
    large = jnp.minimum(large, half - 1)
    return np.asarray(base + jnp.where(n < max_exact, n, large))


def build_nc(dbg=False):
    nc = bass.Bass("TRN2", target_bir_lowering=False)

    def din(name, shape, dt=F32):
        return nc.dram_tensor(name, shape, dt, kind="ExternalInput").ap()

    x_in = din("x", [NB, T, DM])
    w_in = din("w_in", [DM, INW])
    normw = din("normw", [1, DM])
    lbt = din("lbt", [2, 1024])
    gnorm = din("gnorm", [1, 128])
    qnw = din("qnw", [1, 512])
    kvnw = din("kvnw", [1, 256])
    w_uq = din("w_uq", [512, 1024])
    w_qidx = din("w_qidx", [512, 2048])
    w_ukv = din("w_ukv", [256, 2048])
    kiw = din("kiw", [1, 128])
    kib = din("kib", [1, 128])
    w_pa = din("w_pa", [1024, DM])
    w_pb = din("w_pb", [1024, DM])
    w_out = din("w_out", [DM, DM])
    relb = din("relb", [32, 8])
    fnw = din("fnw", [1, DM])
    c_ident = din("c_ident", [128, 128])
    c_LT = din("c_LT", [128, 128])
    c_ind2 = din("c_ind2", [128, 2])
    c_cmask = din("c_cmask", [128, 128])
    c_J = din("c_J", [128, 128])
    c_mbias = din("c_mbias", [128, 1024])
    c_OH = din("c_OH", [32, 1280])
    c_sel = din("c_sel", [128, 64])
    y_out = nc.dram_tensor("y", [NB, T, DM], F32, kind="ExternalOutput").ap()

    proj = nc.dram_tensor("proj", [NB, T, INW], F32).ap()
    oint = nc.dram_tensor("oint", [NB, T, 1024], F32).ap()
    sendA = nc.dram_tensor("sendA", [128, 9216], BF16)
    gathA = nc.dram_tensor("gathA", [NCORE * 128, 9216], BF16)
    sendB = nc.dram_tensor("sendB", [1024, 2064], BF16)
    gathB = nc.dram_tensor("gathB", [NCORE * 1024, 2064], BF16)
    bvec = nc.dram_tensor("bvec", [8, 1280], F32)
    fl_in = nc.dram_tensor("fl_in", [128, 64], BF16)
    fl_out = nc.dram_tensor("fl_out", [NCORE * 128, 64], BF16)
    fl_out2 = nc.dram_tensor("fl_out2", [NCORE * 128, 64], BF16)
    qiT_scr = nc.dram_tensor("qiT_scr", [NB, 128, 2048], BF16).ap()
    yaT_scr = nc.dram_tensor("yaT_scr", [128, 8192], BF16).ap()
    ybT_scr = nc.dram_tensor("ybT_scr", [128, 8192], BF16).ap()
    dbg_out = {}
    if dbg:
        for nm, shp in [("d_proj0", [128, 512]), ("d_ya", [128, 1024]), ("d_sel", [128, 1024]),
                        ("d_attn", [128, 1024]), ("d_acc", [128, 1024]), ("d_S", [128, 1024]),
                        ("d_S0", [128, 1024]), ("d_ya0", [128, 1024]), ("d_attn0", [128, 1024]), ("d_sel0", [128, 1024]),
                        ("d_oi0", [128, 1024]), ("d_den0", [128, 8])]:
            dbg_out[nm] = nc.dram_tensor(nm, shp, F32, kind="ExternalOutput").ap()

    ENG = ["pe", "act", "dve", "pool", "sp"]
    P = Prog(ENG)

    with ExitStack() as es:
        sems = {n: es.enter_context(nc.semaphore("s_" + n)) for n in ENG}
        for n in ENG:
            P.add_engine_sem(n, sems[n])
        for nm, k in [("c", 1), ("cp", 1), ("x", 2), ("w", 2), ("st", 3), ("ld", 3), ("g", 4), ("kv", 4), ("o", 1)]:
            P.add_dma_sems(nm, [es.enter_context(nc.semaphore("d_%s%d" % (nm, i))) for i in range(k)])
        cc1 = es.enter_context(nc.semaphore("cc1"))
        cc2 = es.enter_context(nc.semaphore("cc2"))
        cc3 = es.enter_context(nc.semaphore("cc3"))
        cc4 = es.enter_context(nc.semaphore("cc4"))
        PA = es.enter_context(nc.psum_tensor("PA", [128, 1024], F32))
        PB = es.enter_context(nc.psum_tensor("PB", [128, 1024], F32))
        PC = es.enter_context(nc.psum_tensor("PC", [128, 1024], F32))
        PT0 = es.enter_context(nc.psum_tensor("PT0", [128, 1024], BF16))
        PT1 = es.enter_context(nc.psum_tensor("PT1", [128, 1024], BF16))
        block = es.enter_context(nc.Block())

        def MM(out, lhsT, rhs, st, sp, r, w):
            P.op("pe", lambda e: e.matmul(out, lhsT=lhsT, rhs=rhs, start=st, stop=sp), r, w)

        def ACT(out, in_, func, r, w, **kw):
            P.op("act", lambda e: e.activation(out=out, in_=in_, func=func, **kw), r, w)

        def TT(out, a, b, op, r, w, en="dve"):
            P.op(en, lambda e: e.tensor_tensor(out=out, in0=a, in1=b, op=op), r, w)

        def TS(out, a, s1, op0, r, w, s2=None, op1=None, en="dve", acc=None):
            if op1 is None:
                P.op(en, lambda e: e.tensor_scalar(out=out, in0=a, scalar1=s1, scalar2=None, op0=op0), r, w)
            elif acc is None:
                P.op(en, lambda e: e.tensor_scalar(out=out, in0=a, scalar1=s1, scalar2=s2, op0=op0, op1=op1), r, w)
            else:
                P.op(en, lambda e: e.tensor_scalar(out=out, in0=a, scalar1=s1, scalar2=s2, op0=op0, op1=op1, accum_out=acc), r, w)

        def STT(out, a, s, b, op0, op1, r, w):
            P.op("dve", lambda e: e.scalar_tensor_tensor(out=out, in0=a, scalar=s, in1=b, op0=op0, op1=op1), r, w)

        def CP(en, out, in_, r, w):
            if en == "act":
                P.op("act", lambda e: e.copy(out=out, in_=in_), r, w)
            else:
                P.op(en, lambda e: e.tensor_copy(out=out, in_=in_), r, w)

        def RED(out, in_, op, r, w, ax=AX.X):
            P.op("dve", lambda e: e.tensor_reduce(out=out, in_=in_, axis=ax, op=op), r, w)

        def MEMSET(en, ap, v, w):
            P.op(en, lambda e: e.memset(ap, v), [], w)

        def DMA(en, stream, out, in_, r, w):
            P.dma(en, stream, out, in_, r, w)

        def sb(stack, name, shape, dt):
            return stack.enter_context(nc.sbuf_tensor(name, shape, dt))

        def rstd_from_ss(out, ss, tmp, n, r, w, tk):
            ACT(tmp, ss, AF.Ln, r, [tk], scale=1.0 / n, bias=EPS)
            ACT(out, tmp, AF.Exp, [tk], w, scale=-0.5)

        identf = sb(es, "identf", [128, 128], F32)
        identb = sb(es, "identb", [128, 128], BF16)
        DMA("sp", "c", identf[:], c_ident, [], ["identf"])
        CP("dve", identb[:], identf[:], ["identf"], ["identb"])

        def TR(out, in_, r, w):
            P.op("pe", lambda e: e.transpose(out=out, in_=in_, identity=identb[:]), list(r) + ["identb"], w)

        with ExitStack() as ph:
            hT = sb(ph, "hT", [128, 16, 1024], BF16)
            normw_bc = sb(ph, "normw_bc", [128, DM], F32)
            xt = [sb(ph, "xt%d" % i, [128, DM], F32) for i in range(2)]
            hb = [sb(ph, "hb%d" % i, [128, DM], BF16) for i in range(2)]
            junk = sb(ph, "junk1", [128, DM], F32)
            st1 = sb(ph, "st1", [128, 8], F32)
            DMA("sp", "c", normw_bc[:], normw.partition_broadcast(128), [], ["normw_bc"])
            for j in range(NB):
                b = j % 2
                DMA("sp", "x", xt[b][:], x_in[j], [], [("xt", b)])
                ACT(junk[:], xt[b][:], AF.Square, [("xt", b)], ["junk1", "st1a"], accum_out=st1[:, 0:1])
                rstd_from_ss(st1[:, 2:3], st1[:, 0:1], st1[:, 1:2], DM, ["st1a"], ["st1c"], "st1b")
                STT(hb[b][:], xt[b][:], st1[:, 2:3], normw_bc[:], ALU.mult, ALU.mult,
                    [("xt", b), "st1c", "normw_bc"], [("hb", b)])
                for half in range(2):
                    pt = PT0 if half == 0 else PT1
                    pk = "PT%d" % half
                    for k in range(8):
                        kc = half * 8 + k
                        TR(pt[:, k * 128:(k + 1) * 128], hb[b][:, kc * 128:(kc + 1) * 128], [("hb", b)], [pk])
                    CP("act" if half == 0 else "dve",
                       hT[:, half * 8:(half + 1) * 8, j * 128:(j + 1) * 128],
                       pt[:].rearrange("p (k t) -> p k t", k=8), [pk], [("hT", j)])
            wb = [sb(ph, "wb%d" % i, [128, 16, 512], BF16) for i in range(2)]
            stg = [sb(ph, "stg%d" % i, [128, 512], F32) for i in range(4)]
            banks = [(PA, 0), (PA, 1), (PB, 0), (PB, 1), (PC, 0), (PC, 1)]
            nct = (INW + 511) // 512
            it = 0
            for ct in range(nct):
                c0 = ct * 512
                cw = min(512, INW - c0)
                wbuf = wb[ct % 2]
                wk = ("wb", ct % 2)
                DMA("pool", "w", wbuf[:, :, 0:cw], w_in[:, c0:c0 + cw].rearrange("(k p) c -> p k c", p=128), [], [wk])
                for j in range(NB):
                    pst, hf = banks[it % 6]
                    pk = (id(pst), hf)
                    po = pst[:, hf * 512: hf * 512 + cw]
                    for k in range(16):
                        MM(po, hT[:, k, j * 128:(j + 1) * 128], wbuf[:, k, 0:cw], k == 0, k == 15,
                           [("hT", j), wk], [pk])
                    sg = stg[it % 4]
                    sk = ("stg", it % 4)
                    CP("act" if it % 2 == 0 else "dve", sg[:, 0:cw], po, [pk], [sk])
                    DMA("sp", "st", proj[j, :, c0:c0 + cw], sg[:, 0:cw], [sk], [("proj", j, ct)])
                    if dbg and ct == 0 and j == 0:
                        DMA("sp", "o", dbg_out["d_proj0"], sg[:, 0:512], [sk], [])
                    it += 1
            P.barrier()
        PROJ_ALL = lambda j: [("proj", j, ct) for ct in range(nct)]

        with ExitStack() as ph3:
            qeT_all = sb(ph3, "qeT_all", [128, NB, 8, 128], BF16)
            emid_all = sb(ph3, "emid_all", [128, NB, 8], F32)
            with ExitStack() as ph:
                LT = sb(ph, "LT", [128, 128], F32)
                ind2 = sb(ph, "ind2", [128, 2], F32)
                cmask8 = sb(ph, "cmask8", [128, 8, 128], F32)
                lb_bc = sb(ph, "lb_bc", [128, 1024], F32)
                oml_bc = sb(ph, "oml_bc", [128, 1024], F32)
                lt1 = sb(ph, "lt1", [128, 1024], F32)
                kvn_bc = sb(ph, "kvn_bc", [128, 256], F32)
                kiw_bc = sb(ph, "kiw_bc", [128, 128], F32)
                kib_bc = sb(ph, "kib_bc", [128, 128], F32)
                wukv = sb(ph, "wukv", [128, 2, 2048], BF16)
                DMA("sp", "c", LT[:], c_LT, [], ["LT"])
                DMA("sp", "c", ind2[:], c_ind2, [], ["ind2"])
                for h in range(8):
                    DMA("sp", "c", cmask8[:, h, :], c_cmask, [], ["cmask8"])
                DMA("sp", "c", lb_bc[:], lbt[0:1, :].partition_broadcast(128), [], ["lb_bc"])
                DMA("sp", "c", lt1[:], lbt[1:2, :].partition_broadcast(128), [], ["lt1"])
                DMA("sp", "c", kvn_bc[:], kvnw.partition_broadcast(128), [], ["kvn_bc"])
                DMA("sp", "c", kiw_bc[:], kiw.partition_broadcast(128), [], ["kiw_bc"])
                DMA("sp", "c", kib_bc[:], kib.partition_broadcast(128), [], ["kib_bc"])
                DMA("pool", "cp", wukv[:], w_ukv.rearrange("(k p) c -> p k c", p=128), [], ["wukv"])
                TT(lt1[:], lb_bc[:], lt1[:], ALU.subtract, ["lb_bc", "lt1"], ["lt1"])
                ACT(lb_bc[:], lt1[:], AF.Sigmoid, ["lt1"], ["lb_bc"])
                ACT(oml_bc[:], lt1[:], AF.Sigmoid, ["lt1"], ["oml_bc"], scale=-1.0)

                a3 = [sb(ph, "a3_%d" % i, [128, 3072], F32) for i in range(2)]
                ck = [sb(ph, "ck_%d" % i, [128, 400], F32) for i in range(2)]
                t1 = sb(ph, "t1", [128, 1024], F32)
                gt = sb(ph, "gt", [128, 1024], F32)
                e1 = sb(ph, "e1", [128, 1024], F32)
                e2 = sb(ph, "e2", [128, 1024], F32)
                t2 = sb(ph, "t2", [128, 1024], F32)
                qe_b = sb(ph, "qe_b", [128, 1024], BF16)
                ke_b = sb(ph, "ke_b", [128, 1024], BF16)
                v_b = sb(ph, "v_b", [128, 1024], BF16)
                keT = sb(ph, "keT", [128, 8, 128], BF16)
                AT_b = sb(ph, "AT_b", [128, 8, 128], BF16)
                oi = sb(ph, "oi", [128, 1024], F32)
                sm = sb(ph, "sm", [128, 64], F32)
                sendb = sb(ph, "sendb", [128, 2064], BF16)
                sendk = sb(ph, "sendk", [128, 8, 128], BF16)
                sendi = sb(ph, "sendi", [128, 128], BF16)
                cn = sb(ph, "cn", [128, 256], BF16)
                ckvT = sb(ph, "ckvT", [128, 2, 128], BF16)
                kn = sb(ph, "kn", [128, 128], F32)
                knb = sb(ph, "knb", [128, 128], BF16)
                junk3 = sb(ph, "junk3", [128, 256], F32)
                MEMSET("pool", sendb[:, 1032:2064], 1.0, ["sendV"])
                for j in range(NB):
                    b = j % 2
                    DMA("sp", "ld", a3[b][:], proj[j, :, 0:3072], PROJ_ALL(j), [("a3", b)])
                    DMA("sp", "ld", ck[b][:], proj[j, :, 4608:5008], PROJ_ALL(j), [("ck", b)])
                    aq = a3[b][:, 0:1024]
                    af = a3[b][:, 1024:2048]
                    ai = a3[b][:, 2048:3072]
                    ACT(t1[:], af, AF.Sigmoid, [("a3", b)], ["t1"])
                    TT(t1[:], t1[:], oml_bc[:], ALU.mult, ["t1", "oml_bc"], ["t1"])
                    TT(t1[:], t1[:], lb_bc[:], ALU.add, ["t1", "lb_bc"], ["t1"])
                    ACT(gt[:], t1[:], AF.Ln, ["t1"], ["gt"])
                    TS(t1[:], t1[:], -1.0, ALU.mult, ["t1"], ["t1"], s2=1.0, op1=ALU.add)
                    for hf in range(2):
                        MM(PA[:, hf * 512:(hf + 1) * 512], LT[:], gt[:, hf * 512:(hf + 1) * 512], True, True,
                           ["LT", "gt"], [(id(PA), hf)])
                    for h in range(8):
                        MM(PB[:, 2 * h:2 * h + 2], gt[:, h * 128:(h + 1) * 128], ind2[:], True, True,
                           ["gt", "ind2"], [(id(PB), 0)])
                    CP("dve", sm[:, 0:16], PB[:, 0:16], [(id(PB), 0)], ["sm"])
                    smv = sm[:, 0:16].rearrange("p (h two) -> p h two", two=2)
                    TT(sm[:, 16:24], smv[:, :, 1], smv[:, :, 0], ALU.subtract, ["sm"], ["sm2"])
                    ACT(sm[:, 24:32], sm[:, 16:24], AF.Exp, ["sm2"], ["elm"])
                    ACT(emid_all[:, j, :], smv[:, :, 0], AF.Exp, ["sm"], [("emid", j)])
                    ACT(sendb[:, 1024:1032], smv[:, :, 1], AF.Exp, ["sm"], ["sendD"])
                    for hf in range(2):
                        sl = slice(hf * 512, (hf + 1) * 512)
                        ACT(e1[:, sl], PA[:, sl], AF.Exp, [(id(PA), hf)], [("e1", hf)])
                        ACT(e2[:, sl], PA[:, sl], AF.Exp, [(id(PA), hf)], [("e2", hf)], scale=-1.0)
                    ACT(t2[:], aq, AF.Silu, [("a3", b)], ["t2"])
                    STT(qe_b[:], t2[:], SCALE_A, e1[:], ALU.mult, ALU.mult, ["t2", ("e1", 0), ("e1", 1)], ["qe_b"])
                    TT(ke_b[:], t1[:], e2[:], ALU.mult, ["t1", ("e2", 0), ("e2", 1)], ["ke_b"])
                    CP("pool", v_b[:], ai, [("a3", b)], ["v_b"])
                    for h in range(8):
                        TR(PT0[:, h * 128:(h + 1) * 128], qe_b[:, h * 128:(h + 1) * 128], ["qe_b"], ["PT0"])
                    for h in range(8):
                        TR(PT1[:, h * 128:(h + 1) * 128], ke_b[:, h * 128:(h + 1) * 128], ["ke_b"], ["PT1"])
                    CP("act", qeT_all[:, j, :, :], PT0[:].rearrange("p (h t) -> p h t", h=8), ["PT0"], [("qeT", j)])
                    CP("dve", keT[:], PT1[:].rearrange("p (h t) -> p h t", h=8), ["PT1"], ["keT"])
                    for h in range(8):
                        MM(PB[:, h * 128:(h + 1) * 128], keT[:, h, :], qeT_all[:, j, h, :], True, True,
                           ["keT", ("qeT", j)], [(id(PB), h // 4)])
                    for hf in range(2):
                        TT(AT_b[:, hf * 4:(hf + 1) * 4, :],
                           PB[:, hf * 512:(hf + 1) * 512].rearrange("p (h t) -> p h t", h=4),
                           cmask8[:, hf * 4:(hf + 1) * 4, :], ALU.mult, [(id(PB), hf), "cmask8"], [("AT", hf)])
                    for h in range(8):
                        MM(PA[:, h * 128:(h + 1) * 128], AT_b[:, h, :], v_b[:, h * 128:(h + 1) * 128], True, True,
                           [("AT", h // 4), "v_b"], [(id(PA), h // 4)])
                    for hf in range(2):
                        sl = slice(hf * 512, (hf + 1) * 512)
                        CP("act", oi[:, sl], PA[:, sl], [(id(PA), hf)], ["oi"])
                    DMA("sp", "st", oint[j], oi[:], ["oi"], [("oint", j)])
                    for h in range(8):
                        MM(PC[:, h * 128:(h + 1) * 128], ke_b[:, h * 128:(h + 1) * 128], v_b[:, h * 128:(h + 1) * 128],
                           True, True, ["ke_b", "v_b"], [(id(PC), h // 4)])
                    for h in range(8):
                        TS(sendb[:, h * 128:(h + 1) * 128], PC[:, h * 128:(h + 1) * 128], sm[:, 24 + h:25 + h], ALU.mult,
                           [(id(PC), h // 4), "elm"], ["sendS"])
                    ckv = ck[b][:, 0:256]
                    kraw = ck[b][:, 256:384]
                    ACT(junk3[:], ckv, AF.Square, [("ck", b)], ["junk3", "sm3a"], accum_out=sm[:, 32:33])
                    rstd_from_ss(sm[:, 34:35], sm[:, 32:33], sm[:, 33:34], 256, ["sm3a"], ["sm3c"], "sm3b")
                    STT(cn[:], ckv, sm[:, 34:35], kvn_bc[:], ALU.mult, ALU.mult, [("ck", b), "sm3c", "kvn_bc"], ["cn"])
                    for k in range(2):
                        TR(PT0[:, k * 128:(k + 1) * 128], cn[:, k * 128:(k + 1) * 128], ["cn"], ["PT0"])
                    CP("dve", ckvT[:], PT0[:, 0:256].rearrange("p (k t) -> p k t", k=2), ["PT0"], ["ckvT"])
                    wv = wukv[:].rearrange("p k (h c) -> p k h c", h=8)
                    for h in range(8):
                        for k in range(2):
                            MM(PB[:, h * 128:(h + 1) * 128], wv[:, k, h, 0:128], ckvT[:, k, :], k == 0, k == 1,
                               ["wukv", "ckvT"], [(id(PB), h // 4)])
                    for hf in range(2):
                        CP("act", sendk[:, hf * 4:(hf + 1) * 4, :],
                           PB[:, hf * 512:(hf + 1) * 512].rearrange("p (h t) -> p h t", h=4), [(id(PB), hf)], ["sendk"])
                    for hf in range(2):
                        for k in range(2):
                            MM(PC[:, hf * 512:(hf + 1) * 512], ckvT[:, k, :], wv[:, k, hf * 4:(hf + 1) * 4, 128:256],
                               k == 0, k == 1, ["wukv", "ckvT"], [(id(PC), hf)])
                    sv = sendb[:, 1032:2064].rearrange("p (h c) -> p h c", c=129)
                    for hf in range(2):
                        CP("dve", sv[:, hf * 4:(hf + 1) * 4, 0:128],
                           PC[:, hf * 512:(hf + 1) * 512].rearrange("p (h c) -> p h c", h=4), [(id(PC), hf)], ["sendV"])
                    RED(sm[:, 36:37], kraw, ALU.add, [("ck", b)], ["sm4a"])
                    TS(sm[:, 37:38], sm[:, 36:37], -1.0 / 128, ALU.mult, ["sm4a"], ["sm4b"])
                    TS(kn[:], kraw, sm[:, 37:38], ALU.add, [("ck", b), "sm4b"], ["kn"])
                    ACT(junk3[:, 0:128], kn[:], AF.Square, ["kn"], ["junk3", "sm4c"], accum_out=sm[:, 38:39])
                    rstd_from_ss(sm[:, 40:41], sm[:, 38:39], sm[:, 39:40], 128, ["sm4c"], ["sm4e"], "sm4d")
                    STT(kn[:], kn[:], sm[:, 40:41], kiw_bc[:], ALU.mult, ALU.mult, ["kn", "sm4e", "kiw_bc"], ["kn"])
                    TT(knb[:], kn[:], kib_bc[:], ALU.add, ["kn", "kib_bc"], ["knb"])
                    TR(PT1[:, 0:128], knb[:], ["knb"], ["PT1"])
                    CP("act", sendi[:], PT1[:, 0:128], ["PT1"], ["sendi"])
                    sA = sendA.ap()
                    DMA("sp", "st", sendB.ap()[j * 128:(j + 1) * 128, :], sendb[:], ["sendS", "sendD", "sendV"], ["sendB"])
                    DMA("sp", "st", sA[:, 0:8192].rearrange("p (h j t) -> p h j t", h=8, j=8)[:, :, j, :], sendk[:],
                        ["sendk"], ["sendA"])
                    DMA("sp", "st", sA[:, 8192 + j * 128: 8192 + (j + 1) * 128], sendi[:], ["sendi"], ["sendA"])
                P.barrier()
            P.collective(lambda e: e.collective_compute("AllGather", ALU.bypass, replica_groups=[list(range(NCORE))],
                                                        ins=[sendA.ap().opt()], outs=[gathA.ap().opt()]),
                         cc1, ["sendA"], ["gathA"])
            P.collective(lambda e: e.collective_compute("AllGather", ALU.bypass, replica_groups=[list(range(NCORE))],
                                                        ins=[sendB.ap().opt()], outs=[gathB.ap().opt()]),
                         cc2, ["sendB"], ["gathB"])
            gA = gathA.ap()
            gB = gathB.ap()

            with ExitStack() as ph:
                S = sb(ph, "S", [128, 1024], F32)
                save = sb(ph, "save", [128, NB, 1024], F32)
                selc = sb(ph, "selc", [128, 64], F32)
                slb = [sb(ph, "slb%d" % i, [128, 1032], BF16) for i in range(4)]
                dfl = [sb(ph, "dfl%d" % i, [128, 8], F32) for i in range(2)]
                DMA("sp", "c", selc[:], c_sel, [], ["selc"])
                MEMSET("dve", S[:], 0.0, ["S"])
                for bb in range(64):
                    r, jj = bb % 8, bb // 8
                    sl = slb[bb % 4]
                    slk = ("slb", bb % 4)
                    row0 = r * 1024 + jj * 128
                    DMA("pool", "g", sl[:], gB[row0:row0 + 128, 0:1032], ["gathB"], [slk])
                    if r == 0:
                        TS(save[:, jj, :], S[:], selc[:, bb:bb + 1], ALU.mult, ["S", "selc"], [("save", jj)])
                    else:
                        STT(save[:, jj, :], S[:], selc[:, bb:bb + 1], save[:, jj, :], ALU.mult, ALU.add,
                            ["S", "selc", ("save", jj)], [("save", jj)])
                    df = dfl[bb % 2]
                    CP("act", df[:], sl[:, 1024:1032], [slk], [("dfl", bb % 2)])
                    for h in range(8):
                        hs = slice(h * 128, (h + 1) * 128)
                        STT(S[:, hs], S[:, hs], df[:, h:h + 1], sl[:, hs], ALU.mult, ALU.add,
                            ["S", ("dfl", bb % 2), slk], ["S"])
                if dbg:
                    DMA("sp", "o", dbg_out["d_S"], save[:, 1, :], [("save", 1)], [])
                    DMA("sp", "o", dbg_out["d_S0"], save[:, 0, :], [("save", 0)], [])

                gn_bc = sb(ph, "gn_bc", [128, 128], F32)
                sinb = sb(ph, "sinb", [128, 1024], BF16)
                oi2 = [sb(ph, "oi2_%d" % i, [128, 1024], F32) for i in range(2)]
                ag2 = [sb(ph, "ag2_%d" % i, [128, 1024], F32) for i in range(2)]
                o5 = sb(ph, "o5", [128, 1024], F32)
                j5 = sb(ph, "j5", [128, 1024], F32)
                sg5 = sb(ph, "sg5", [128, 1024], F32)
                y5 = sb(ph, "y5", [128, 1024], F32)
                yab = sb(ph, "yab", [128, 1024], BF16)
                yaT = sb(ph, "yaT", [128, 8, 128], BF16)
                s5 = sb(ph, "s5", [128, 32], F32)
                DMA("sp", "c", gn_bc[:], gnorm.partition_broadcast(128), [], ["gn_bc"])
                for j in range(NB):
                    b = j % 2
                    DMA("sp", "ld", oi2[b][:], oint[j], [("oint", j)], [("oi2", b)])
                    DMA("sp", "ld", ag2[b][:], proj[j, :, 3072:4096], PROJ_ALL(j), [("ag2", b)])
                    for h in range(8):
                        hs = slice(h * 128, (h + 1) * 128)
                        TS(sinb[:, hs], save[:, j, hs], emid_all[:, j, h:h + 1], ALU.mult,
                           [("save", j), ("emid", j)], ["sinb"])
                    for h in range(8):
                        hs = slice(h * 128, (h + 1) * 128)
                        MM(PA[:, hs], qeT_all[:, j, h, :], sinb[:, hs], True, True, [("qeT", j), "sinb"], [(id(PA), h // 4)])
                    for hf in range(2):
                        sl_ = slice(hf * 512, (hf + 1) * 512)
                        TT(o5[:, sl_], PA[:, sl_], oi2[b][:, sl_], ALU.add, [(id(PA), hf), ("oi2", b)], ["o5"])
                    ACT(j5[:], o5[:], AF.Square, ["o5"], ["j5"])
                    RED(s5[:, 0:8], j5[:].rearrange("p (h v) -> p h v", h=8), ALU.add, ["j5"], ["s5a"])
                    rstd_from_ss(s5[:, 16:24], s5[:, 0:8], s5[:, 8:16], 128, ["s5a"], ["s5c"], "s5b")
                    ACT(sg5[:], ag2[b][:], AF.Silu, [("ag2", b)], ["sg5"])
                    for h in range(8):
                        hs = slice(h * 128, (h + 1) * 128)
                        STT(y5[:, hs], o5[:, hs], s5[:, 16 + h:17 + h], gn_bc[:], ALU.mult, ALU.mult,
                            ["o5", "s5c", "gn_bc"], ["y5"])
                    TT(yab[:], y5[:], sg5[:], ALU.mult, ["y5", "sg5"], ["yab"])
                    if dbg and j == 1:
                        DMA("sp", "o", dbg_out["d_ya"], y5[:], ["y5"], [])
                    if dbg and j == 0:
                        DMA("sp", "o", dbg_out["d_ya0"], y5[:], ["y5"], [])
                        DMA("sp", "o", dbg_out["d_oi0"], oi2[b][:], [("oi2", b)], [])
                    for h in range(8):
                        TR(PT0[:, h * 128:(h + 1) * 128], yab[:, h * 128:(h + 1) * 128], ["yab"], ["PT0"])
                    CP("act", yaT[:], PT0[:].rearrange("p (h t) -> p h t", h=8), ["PT0"], ["yaT"])
                    DMA("sp", "st", yaT_scr.rearrange("p (k t) -> p k t", k=8)[:, :, j * 128:(j + 1) * 128], yaT[:],
                        ["yaT"], ["yaT_scr"])
                P.barrier()

        with ExitStack() as ph:
            kidxT = sb(ph, "kidxT", [128, 8, 1024], BF16)
            qT_all = sb(ph, "qT_all", [128, 8, 1024], BF16)
            wabs = sb(ph, "wabs", [128, NB, 16], F32)
            wsgn = sb(ph, "wsgn", [128, NB, 16], F32)
            EB = sb(ph, "EB", [128, 9, 8, 128], BF16)
            mb = sb(ph, "mb", [128, 8, 128], F32)
            DMA("pool", "cp", kidxT[:], gA[:, 8192:9216].rearrange("(r p) t -> p r t", p=128), ["gathA"], ["kidxT"])
            DMA("sp", "c", mb[:], c_mbias.rearrange("p (r t) -> p r t", r=8), [], ["mb"])
            with ExitStack() as pp:
                relb_s = sb(pp, "relb_s", [32, 8], F32)
                oh_s = sb(pp, "oh_s", [32, 1280], F32)
                bv_s = sb(pp, "bv_s", [8, 1280], F32)
                Jm = sb(pp, "Jm", [128, 128], F32)
                hk = [sb(pp, "hk%d" % i, [128, 128], F32) for i in range(4)]
                DMA("sp", "c", relb_s[:], relb, [], ["relb_s"])
                DMA("sp", "c", oh_s[:], c_OH, [], ["oh_s"])
                DMA("sp", "c", Jm[:], c_J, [], ["Jm"])
                for i, (v0, vw) in enumerate([(0, 512), (512, 512), (1024, 256)]):
                    MM(PA[0:8, 0:vw], relb_s[:], oh_s[:, v0:v0 + vw], True, True, ["relb_s", "oh_s"], [(id(PA), 0)])
                    CP("act", bv_s[:, v0:v0 + vw], PA[0:8, 0:vw], [(id(PA), 0)], ["bv_s"])
                DMA("sp", "st", bvec.ap(), bv_s[:], ["bv_s"], ["bvec"])
                it = 0
                for kb in range(9):
                    for h in range(8):
                        hb_ = hk[it % 4]
                        hkk = ("hk", it % 4)
                        DMA("sp", "ld", hb_[:], bass.AP(bvec, h * 1280 + kb * 128, [[1, 128], [1, 128]]), ["bvec"], [hkk])
                        pst, hf = [(PB, 0), (PB, 1), (PC, 0), (PC, 1)][it % 4]
                        MM(pst[:, hf * 512: hf * 512 + 128], hb_[:], Jm[:], True, True, [hkk, "Jm"], [(id(pst), hf)])
                        ACT(EB[:, kb, h, :], pst[:, hf * 512: hf * 512 + 128], AF.Exp, [(id(pst), hf)], ["EB"])
                        it += 1
                qn_bc = sb(pp, "qn_bc", [128, 512], F32)
                wuq = sb(pp, "wuq", [128, 4, 1024], BF16)
                wqi = sb(pp, "wqi", [128, 4, 2048], BF16)
                cqT = sb(pp, "cqT", [128, 4, 1024], BF16)
                cq = [sb(pp, "cq%d" % i, [128, 912], F32) for i in range(2)]
                cqn = sb(pp, "cqn", [128, 512], BF16)
                j6 = sb(pp, "j6", [128, 512], F32)
                s6 = sb(pp, "s6", [128, 8], F32)
                qiS = [sb(pp, "qiS%d" % i, [128, 1024], BF16) for i in range(2)]
                DMA("sp", "c", qn_bc[:], qnw.partition_broadcast(128), [], ["qn_bc"])
                DMA("pool", "cp", wuq[:], w_uq.rearrange("(k p) c -> p k c", p=128), [], ["wuq"])
                DMA("pool", "cp", wqi[:], w_qidx.rearrange("(k p) c -> p k c", p=128), [], ["wqi"])
                P.alias_last(["kidxT", "wuq", "wqi"])
                for j in range(NB):
                    b = j % 2
                    DMA("sp", "ld", cq[b][:], proj[j, :, 4096:5008], PROJ_ALL(j), [("cq", b)])
                    ACT(j6[:], cq[b][:, 0:512], AF.Square, [("cq", b)], ["j6", "s6a"], accum_out=s6[:, 0:1])
                    rstd_from_ss(s6[:, 2:3], s6[:, 0:1], s6[:, 1:2], 512, ["s6a"], ["s6c"], "s6b")
                    STT(cqn[:], cq[b][:, 0:512], s6[:, 2:3], qn_bc[:], ALU.mult, ALU.mult, [("cq", b), "s6c", "qn_bc"], ["cqn"])
                    for k in range(4):
                        TR(PT0[:, k * 128:(k + 1) * 128], cqn[:, k * 128:(k + 1) * 128], ["cqn"], ["PT0"])
                    CP("act", cqT[:, :, j * 128:(j + 1) * 128], PT0[:, 0:512].rearrange("p (k t) -> p k t", k=4), ["PT0"], ["cqT"])
                    ACT(wabs[:, j, :], cq[b][:, 896:912], AF.Abs, [("cq", b)], ["wabs"], scale=WIDX_C)
                    ACT(wsgn[:, j, :], cq[b][:, 896:912], AF.Sign, [("cq", b)], ["wsgn"])
                banks = [(PA, 0), (PA, 1), (PB, 0), (PB, 1), (PC, 0), (PC, 1)]
                it = 0
                for h in range(8):
                    for half in range(2):
                        pst, hf = banks[it % 6]
                        po = pst[:, hf * 512:(hf + 1) * 512]
                        for k in range(4):
                            MM(po, wuq[:, k, h * 128:(h + 1) * 128], cqT[:, k, half * 512:(half + 1) * 512], k == 0, k == 3,
                               ["wuq", "cqT"], [(id(pst), hf)])
                        ACT(qT_all[:, h, half * 512:(half + 1) * 512], po, AF.Copy, [(id(pst), hf)], ["qT_all"], scale=SCALE_B)
                        it += 1
                for h in range(16):
                    qb = qiS[h % 2]
                    qk = ("qiS", h % 2)
                    for half in range(2):
                        pst, hf = banks[it % 6]
                        po = pst[:, hf * 512:(hf + 1) * 512]
                        for k in range(4):
                            MM(po, wqi[:, k, h * 128:(h + 1) * 128], cqT[:, k, half * 512:(half + 1) * 512], k == 0, k == 3,
                               ["wqi", "cqT"], [(id(pst), hf)])
                        CP("act" if it % 2 == 0 else "dve", qb[:, half * 512:(half + 1) * 512], po, [(id(pst), hf)], [qk])
                        it += 1
                    DMA("sp", "st", qiT_scr.rearrange("j d (h t) -> d h j t", h=16)[:, h, :, :],
                        qb[:].rearrange("p (j t) -> p j t", j=8), [qk], ["qiT_scr"])
                P.barrier()

            acc = sb(ph, "acc", [128, 8, 8, 128], F32)
            selb = sb(ph, "selb", [128, 8, 8, 128], BF16)
            selT2 = [sb(ph, "selT%d" % i, [128, 8, 8, 128], BF16) for i in range(2)]
            qiT = [sb(ph, "qiT%d" % i, [128, 16, 128], BF16) for i in range(2)]
            rbuf = [sb(ph, "rbuf%d" % i, [128, 512], BF16) for i in range(4)]
            bs2 = [sb(ph, "bs%d" % i, [128, 16], F32) for i in range(2)]
            halfs2 = [sb(ph, "halfs%d" % i, [128, NBIS + 2], F32) for i in range(2)]
            pw = sb(ph, "pw", [128, NBIS + 2], F32)
            KTp = [sb(ph, "KTp%d" % i, [128, 8, 512], BF16) for i in range(2)]
            Vp = [sb(ph, "Vp%d" % i, [128, 4, 1032], BF16) for i in range(2)]
            et = [sb(ph, "et%d" % i, [128, 512], BF16) for i in range(3)]
            ptt = [sb(ph, "ptt%d" % i, [128, 4, 128], BF16) for i in range(3)]
            att = sb(ph, "att", [128, 1024], F32)
            bg2 = [sb(ph, "bg%d" % i, [128, 1024], F32) for i in range(2)]
            ybb = sb(ph, "ybb", [128, 1024], BF16)
            ybT = sb(ph, "ybT", [128, 8, 128], BF16)
            rd = sb(ph, "rd", [128, 8], F32)
            oacc = sb(ph, "oacc", [128, 8, 129], F32)
            for i in range(NBIS + 2):
                MEMSET("pool", pw[:, i:i + 1], 2.0 ** (-(i + 1)), ["pw"])
            banks_i = [(PA, 0), (PA, 1), (PB, 0), (PB, 1)]

            def key_tiles(nb_):
                tl = []
                for r in range(8):
                    s0 = 0
                    while s0 < nb_:
                        n = min(4, nb_ - s0)
                        tl.append((r, s0, n))
                        s0 += n
                return tl

            cnt_i = [0]

            def gen_indexer(j):
                nb_ = j + 1
                qb = qiT[j % 2]
                qk = ("qiT", j % 2)
                DMA("sp", "ld", qb[:], qiT_scr[j].rearrange("d (h t) -> d h t", h=16), ["qiT_scr"], [qk])
                DMA("sp", "ld", bg2[j % 2][:], proj[j, :, 5008:6032], PROJ_ALL(j), [("bg", j % 2)])
                tiles = key_tiles(nb_)
                for h in range(16):
                    for (r, s0, n) in tiles:
                        it = cnt_i[0]
                        cnt_i[0] += 1
                        ak = ("acc", r, s0)
                        pst, hf = banks_i[it % 4]
                        po = pst[:, hf * 512: hf * 512 + n * 128]
                        MM(po, qb[:, h, :], kidxT[:, r, s0 * 128:(s0 + n) * 128], True, True, [qk, "kidxT"], [(id(pst), hf)])
                        rb = rbuf[it % 4]
                        rk = ("rbuf", it % 4)
                        ACT(rb[:, 0:n * 128], po, AF.Relu, [(id(pst), hf), "wabs"], [rk], scale=wabs[:, j, h:h + 1])
                        av = acc[:, r, s0:s0 + n, :]
                        rv = rb[:, 0:n * 128].rearrange("p (n t) -> p n t", n=n)
                        if h == 0:
                            TS(av, rv, wsgn[:, j, h:h + 1], ALU.mult, [rk, "wsgn"], [ak])
                        else:
                            STT(av, rv, wsgn[:, j, h:h + 1], av, ALU.mult, ALU.add, [rk, "wsgn", ak], [ak])
                        yield

            def gen_bisect(j):
                nb_ = j + 1
                bs = bs2[j % 2]
                halfs = halfs2[j % 2]
                bk = "bs%d" % (j % 2)
                AK = [("acc", r, s0) for (r, s0, n) in key_tiles(nb_)]
                av = acc[:, :, 0:nb_, :]
                RED(bs[:, 0:1], av, ALU.max, AK, [bk + "0"], ax=AX.XYZ)
                yield
                RED(bs[:, 1:2], av, ALU.min, AK, [bk + "1"], ax=AX.XYZ)
                yield
                TT(acc[:, :, j, :], acc[:, :, j, :], mb[:], ALU.add, AK + ["mb"], AK + ["accm"])
                yield
                AKM = AK + ["accm"]
                TT(bs[:, 2:3], bs[:, 0:1], bs[:, 1:2], ALU.subtract, [bk + "0", bk + "1"], [bk + "w"])
                yield
                TS(bs[:, 2:3], bs[:, 2:3], 1.0001, ALU.mult, [bk + "w"], [bk + "w"], s2=1e-6, op1=ALU.add)
                yield
                TS(halfs[:], pw[:], bs[:, 2:3], ALU.mult, ["pw", bk + "w"], [bk + "h"])
                yield
                TT(bs[:, 3:4], bs[:, 1:2], halfs[:, 0:1], ALU.add, [bk + "1", bk + "h"], [bk + "thr"])
                yield
                for i in range(NBIS):
                    TS(selb[:, :, 0:nb_, :], av, bs[:, 3:4], ALU.is_ge, AKM + [bk + "thr"], ["selb", bk + "cnt"],
                       s2=0.0, op1=ALU.add, acc=bs[:, 4:5])
                    yield
                    TT(bs[:, 7:8], bs[:, 3:4], halfs[:, i + 1:i + 2], ALU.subtract, [bk + "thr", bk + "h"], [bk + "tm"])
                    yield
                    STT(bs[:, 5:6], bs[:, 4:5], TOPK, halfs[:, i:i + 1], ALU.is_ge, ALU.mult, [bk + "cnt", bk + "h"], [bk + "g"])
                    yield
                    TT(bs[:, 3:4], bs[:, 5:6], bs[:, 7:8], ALU.add, [bk + "g", bk + "tm"], [bk + "thr"])
                    yield
                TT(bs[:, 6:7], bs[:, 3:4], halfs[:, NBIS:NBIS + 1], ALU.subtract, [bk + "thr", bk + "h"], [bk + "lo"])
                yield
                TS(selb[:, :, 0:nb_, :], av, bs[:, 6:7], ALU.is_ge, AKM + [bk + "lo"], ["selb"])
                yield
                if dbg and j == 1:
                    DMA("sp", "o", dbg_out["d_acc"].rearrange("p (r t) -> p r t", r=8), acc[:, :, 1, :], AKM, [])
                    CP("pool", att[:].rearrange("p (r t) -> p r t", r=8), selb[:, :, 1, :], ["selb"], ["att"])
                    DMA("sp", "o", dbg_out["d_sel"], att[:], ["att"], [])
                    yield

            def gen_transposes(j):
                nb_ = j + 1
                selT = selT2[j % 2]
                it = 0
                for r in range(8):
                    s0 = 0
                    while s0 < nb_:
                        n = min(8, nb_ - s0)
                        pt = PT0 if it % 2 == 0 else PT1
                        pk = "PT%d" % (it % 2)
                        for q in range(n):
                            TR(pt[:, q * 128:(q + 1) * 128], selb[:, r, s0 + q, :], ["selb"], [pk])
                        CP("act", selT[:, r, s0:s0 + n, :],
                           pt[:, 0:n * 128].rearrange("p (n t) -> p n t", n=n), [pk], [("selT", j % 2, r)])
                        s0 += n
                        it += 1
                        yield

            cnt_a = [0]

            def gen_attention(j):
                nb_ = j + 1
                selT = selT2[j % 2]
                bg = bg2[j % 2]
                pieces = key_tiles(nb_)
                npc = len(pieces)
                for pi, (r, s0, n) in enumerate(pieces):
                    kb_ = KTp[pi % 2]
                    vb_ = Vp[pi % 2]
                    kk = ("KTp", pi % 2)
                    vk = ("Vp", pi % 2)
                    DMA("pool", "kv", kb_[:, :, 0:n * 128],
                        gA[r * 128:(r + 1) * 128, 0:8192].rearrange("p (h t) -> p h t", h=8)[:, :, s0 * 128:(s0 + n) * 128],
                        ["gathA"], [kk])
                    DMA("pool", "kv", vb_[:, 0:n, :],
                        gB[r * 1024 + s0 * 128: r * 1024 + (s0 + n) * 128, 1032:2064].rearrange("(n p) c -> p n c", p=128),
                        ["gathB"], [vk])
                    for h in range(8):
                        ia = cnt_a[0]
                        cnt_a[0] += 1
                        lk = (id(PC), ia % 2)
                        pl = PC[:, (ia % 2) * 512:(ia % 2) * 512 + n * 128]
                        for q in range(n):
                            MM(PC[:, (ia % 2) * 512 + q * 128:(ia % 2) * 512 + (q + 1) * 128],
                               kb_[:, h, q * 128:(q + 1) * 128], qT_all[:, h, j * 128:(j + 1) * 128], True, True,
                               [kk, "qT_all"], [lk])
                        eb = et[ia % 3]
                        ek = ("et", ia % 3)
                        ACT(eb[:, 0:n * 128], pl, AF.Exp, [lk], [ek])
                        pb_ = ptt[ia % 3]
                        pk = ("ptt", ia % 3)
                        TT(pb_[:, 0:n, :], eb[:, 0:n * 128].rearrange("p (n t) -> p n t", n=n), selT[:, r, s0:s0 + n, :],
                           ALU.mult, [ek, ("selT", j % 2, r)], [pk])
                        for q in range(n):
                            jl = s0 + q
                            kbw = None
                            if jl == j:
                                kbw = 1 + r
                            elif jl == j - 1 and r == 7:
                                kbw = 0
                            if kbw is not None:
                                TT(pb_[:, q, :], pb_[:, q, :], EB[:, kbw, h, :], ALU.mult, [pk, "EB"], [pk])
                        po_t = PA if h < 4 else PB
                        ok_ = ("PO", h // 4)
                        for q in range(n):
                            MM(po_t[:, (h % 4) * 256:(h % 4) * 256 + 129], pb_[:, q, :], vb_[:, q, h * 129:(h + 1) * 129],
                               q == 0, q == n - 1, [pk, vk], [ok_, (id(po_t), 0), (id(po_t), 1)])
                        yield
                    for g2, po_t in enumerate((PA, PB)):
                        pv = po_t[:].rearrange("p (h c) -> p h c", h=4)[:, :, 0:129]
                        ov = oacc[:, g2 * 4:(g2 + 1) * 4, :]
                        if pi == 0:
                            CP("dve", ov, pv, [("PO", g2), (id(po_t), 0), (id(po_t), 1)], [("oacc", g2)])
                        else:
                            TT(ov, pv, ov, ALU.add, [("PO", g2), (id(po_t), 0), (id(po_t), 1), ("oacc", g2)], [("oacc", g2)])
                        yield
                for h in range(8):
                    ok_ = ("oacc", h // 4)
                    P.op("dve", lambda e, o=rd[:, h:h + 1], i_=oacc[:, h, 128:129]: e.reciprocal(out=o, in_=i_), [ok_], [("rd", h)])
                    TS(att[:, h * 128:(h + 1) * 128], oacc[:, h, 0:128], rd[:, h:h + 1], ALU.mult, [ok_, ("rd", h)], ["att"])
                    yield
                if dbg and j == 1:
                    DMA("sp", "o", dbg_out["d_attn"], att[:], ["att"], [])
                if dbg and j == 0:
                    DMA("sp", "o", dbg_out["d_attn0"], att[:], ["att"], [])
                ACT(bg[:], bg[:], AF.Silu, [("bg", j % 2)], [("bg", j % 2)])
                TT(ybb[:], att[:], bg[:], ALU.mult, ["att", ("bg", j % 2)], ["ybb"])
                yield
                for h in range(8):
                    TR(PT0[:, h * 128:(h + 1) * 128], ybb[:, h * 128:(h + 1) * 128], ["ybb"], ["PT0"])
                CP("act", ybT[:], PT0[:].rearrange("p (h t) -> p h t", h=8), ["PT0"], ["ybT"])
                DMA("sp", "st", ybT_scr.rearrange("p (k t) -> p k t", k=8)[:, :, j * 128:(j + 1) * 128], ybT[:],
                    ["ybT"], ["ybT_scr"])
                yield

            def run(g):
                for _ in g:
                    pass

            def interleave(ga, gb, ra, rb_):
                da = db = False
                while not (da and db):
                    for _ in range(ra):
                        if not da:
                            try:
                                next(ga)
                            except StopIteration:
                                da = True
                    for _ in range(rb_):
                        if not db:
                            try:
                                next(gb)
                            except StopIteration:
                                db = True

            run(gen_indexer(0))
            for j in range(NB):
                if j >= 1:
                    interleave(gen_bisect(j), gen_attention(j - 1), 1, 1)
                else:
                    run(gen_bisect(j))
                run(gen_transposes(j))
                if j + 1 < NB:
                    run(gen_indexer(j + 1))
            run(gen_attention(NB - 1))
            P.barrier()

        with ExitStack() as ph7:
            mT_all = sb(ph7, "mT_all", [128, 16, 1024], BF16)
            with ExitStack() as ph:
                wpa = sb(ph, "wpa", [128, 8, DM], BF16)
                wpb = sb(ph, "wpb", [128, 8, DM], BF16)
                yaT7 = [sb(ph, "yaT7_%d" % i, [128, 8, 128], BF16) for i in range(2)]
                ybT7 = [sb(ph, "ybT7_%d" % i, [128, 8, 128], BF16) for i in range(2)]
                mg = sb(ph, "mg", [128, 4096], F32)
                mrg = sb(ph, "mrg", [128, DM], F32)
                tmp7 = sb(ph, "tmp7", [128, DM], F32)
                mrb = sb(ph, "mrb", [128, DM], BF16)
                DMA("pool", "cp", wpa[:], w_pa.rearrange("(k p) c -> p k c", p=128), [], ["wpa"])
                DMA("pool", "cp", wpb[:], w_pb.rearrange("(k p) c -> p k c", p=128), [], ["wpb"])
                P.alias_last(["wpa", "wpb"])
                for j in range(NB):
                    b = j % 2
                    DMA("sp", "ld", yaT7[b][:], yaT_scr.rearrange("p (k t) -> p k t", k=8)[:, :, j * 128:(j + 1) * 128],
                        ["yaT_scr"], [("yaT7", b)])
                    DMA("sp", "ld", ybT7[b][:], ybT_scr.rearrange("p (k t) -> p k t", k=8)[:, :, j * 128:(j + 1) * 128],
                        ["ybT_scr"], [("ybT7", b)])
                    DMA("sp", "ld", mg[:], proj[j, :, 6032:10128], PROJ_ALL(j), ["mg"])
                    ACT(mg[:], mg[:], AF.Sigmoid, ["mg"], ["mg"])
                    for (wt, yT, wkey, ykey, goff, first) in [(wpa, yaT7[b], "wpa", ("yaT7", b), 0, True),
                                                              (wpb, ybT7[b], "wpb", ("ybT7", b), 2048, False)]:
                        for q4 in range(4):
                            pst, hf = [(PA, 0), (PA, 1), (PB, 0), (PB, 1)][q4]
                            po = pst[:, hf * 512:(hf + 1) * 512]
                            for k in range(8):
                                MM(po, yT[:, k, :], wt[:, k, q4 * 512:(q4 + 1) * 512], k == 0, k == 7,
                                   [wkey, ykey], [(id(pst), hf)])
                            cs = slice(q4 * 512, (q4 + 1) * 512)
                            gs = slice(goff + q4 * 512, goff + (q4 + 1) * 512)
                            if first:
                                TT(mrg[:, cs], po, mg[:, gs], ALU.mult, [(id(pst), hf), "mg"], [("mrg", q4)])
                            else:
                                TT(tmp7[:, cs], po, mg[:, gs], ALU.mult, [(id(pst), hf), "mg"], [("tmp7", q4)])
                                TT(mrb[:, cs], mrg[:, cs], tmp7[:, cs], ALU.add, [("mrg", q4), ("tmp7", q4)], [("mrb", q4)])
                    for half in range(2):
                        pt = PT0 if half == 0 else PT1
                        pk = "PT%d" % half
                        for k in range(8):
                            kc = half * 8 + k
                            TR(pt[:, k * 128:(k + 1) * 128], mrb[:, kc * 128:(kc + 1) * 128], [("mrb", kc // 4)], [pk])
                        CP("act" if half == 0 else "dve", mT_all[:, half * 8:(half + 1) * 8, j * 128:(j + 1) * 128],
                           pt[:].rearrange("p (k t) -> p k t", k=8), [pk], [("mT", j)])
                P.barrier()
            with ExitStack() as ph:
                wo = sb(ph, "wo", [128, 16, DM], BF16)
                fn_bc = sb(ph, "fn_bc", [128, DM], F32)
                xr = [sb(ph, "xr%d" % i, [128, DM], F32) for i in range(2)]
                tmp8 = sb(ph, "tmp8", [128, DM], F32)
                jk8 = sb(ph, "jk8", [128, DM], F32)
                s7 = sb(ph, "s7", [128, 8], F32)
                for k4 in range(4):
                    DMA("pool", "cp", wo[:, k4 * 4:(k4 + 1) * 4, :],
                        w_out[k4 * 512:(k4 + 1) * 512, :].rearrange("(k p) c -> p k c", p=128), [], ["wo"])
                DMA("sp", "c", fn_bc[:], fnw.partition_broadcast(128), [], ["fn_bc"])
                for j in range(NB):
                    b = j % 2
                    DMA("sp", "x", xr[b][:], x_in[j], [], [("xr", b)])
                    for q4 in range(4):
                        pst, hf = [(PC, 0), (PC, 1), (PA, 0), (PA, 1)][q4]
                        po = pst[:, hf * 512:(hf + 1) * 512]
                        for k in range(16):
                            MM(po, mT_all[:, k, j * 128:(j + 1) * 128], wo[:, k, q4 * 512:(q4 + 1) * 512], k == 0, k == 15,
                               [("mT", j), "wo"], [(id(pst), hf)])
                        cs = slice(q4 * 512, (q4 + 1) * 512)
                        TT(tmp8[:, cs], po, xr[b][:, cs], ALU.add, [(id(pst), hf), ("xr", b)], [("tmp8", q4)])
                    T7 = [("tmp8", q) for q in range(4)]
                    ACT(jk8[:], tmp8[:], AF.Square, T7, ["jk8", "s7a"], accum_out=s7[:, 0:1])
                    rstd_from_ss(s7[:, 2:3], s7[:, 0:1], s7[:, 1:2], DM, ["s7a"], ["s7c"], "s7b")
                    STT(xr[b][:], tmp8[:], s7[:, 2:3], fn_bc[:], ALU.mult, ALU.mult, T7 + ["s7c", "fn_bc", ("xr", b)], [("xr", b)])
                    DMA("sp", "o", y_out[j], xr[b][:], [("xr", b)], [("y", j)])
                P.barrier()

        P.finalize()
        for en, meth in [("pe", block.tensor), ("act", block.scalar), ("dve", block.vector),
                         ("pool", block.gpsimd), ("sp", block.sync)]:
            meth(lambda e, en=en: P.emit(en, e))
    return nc, P


def _consts(c):
    ident = np.eye(128, dtype=np.float32)
    s = np.arange(128)[:, None]
    t = np.arange(128)[None, :]
    LT = ((s <= t).astype(np.float32) - (s <= 63).astype(np.float32))
    ind2 = np.stack([(np.arange(128) <= 63).astype(np.float32), np.ones(128, np.float32)], axis=1)
    cmask = (s <= t).astype(np.float32)
    J = np.zeros((128, 128), np.float32)
    J[np.arange(128), 127 - np.arange(128)] = 1.0
    i = np.arange(128)[:, None, None]
    r = np.arange(8)[None, :, None]
    ip = np.arange(128)[None, None, :]
    vis = (2 * r + ip // 64) <= (2 * c + i // 64)
    mbias = np.where(vis, 0.0, NEG).astype(np.float32).reshape(128, 1024)
    v = np.arange(1280)
    bk = _t5_bucket_np(v - 255 - 128 * c)
    OH = np.zeros((32, 1280), np.float32)
    OH[bk, v] = 1.0
    OH[15, :] -= 1.0
    sel = np.zeros((128, 64), np.float32)
    sel[:, np.arange(64) % 8 == c] = 1.0
    return dict(c_ident=ident, c_LT=LT, c_ind2=ind2, c_cmask=cmask, c_J=J, c_mbias=mbias, c_OH=OH, c_sel=sel)


_DBG = False


def kernel(x, norm_w, w_in, lb_table, gnorm_a, q_norm_w, kv_norm_w, w_uq, w_qidx, w_ukv,
           kidx_norm_w, kidx_norm_b, w_pa, w_pb, w_out, rel_bias, final_norm_w):
    f = lambda a: np.ascontiguousarray(np.asarray(a, dtype=np.float32))
    x = f(x)
    xb = x.reshape(8, 8, 128, DM)
    shared = dict(
        w_in=f(w_in)[0], normw=f(norm_w).reshape(1, DM), lbt=f(lb_table), gnorm=f(gnorm_a).reshape(1, 128),
        qnw=f(q_norm_w).reshape(1, 512), kvnw=f(kv_norm_w).reshape(1, 256), w_uq=f(w_uq)[0], w_qidx=f(w_qidx)[0],
        w_ukv=f(w_ukv)[0], kiw=f(kidx_norm_w).reshape(1, 128), kib=f(kidx_norm_b).reshape(1, 128),
        w_pa=f(w_pa)[0], w_pb=f(w_pb)[0], w_out=f(w_out)[0], relb=f(rel_bias), fnw=f(final_norm_w).reshape(1, DM))
    in_maps = []
    for c in range(NCORE):
        m = dict(shared)
        m["x"] = np.ascontiguousarray(xb[:, c])
        m.update(_consts(c))
        in_maps.append(m)
    nc, _ = build_nc(_DBG)
    res = run_bass_kernel_spmd(nc, in_maps, core_ids=list(range(NCORE)))
    out = np.zeros((8, 8, 128, DM), np.float32)
    for c in range(NCORE):
        out[:, c] = res.results[c]["y"]
    if _DBG:
        kernel.dbg = res.results
    return out.reshape(1, 8192, DM)
```

```python
import math
from contextlib import ExitStack
import numpy as np
import ml_dtypes
import jax
import jax.numpy as jnp
import concourse.bass as bass
import concourse.mybir as mybir
from concourse.bass_utils import run_bass_kernel_spmd

F32 = mybir.dt.float32
BF16 = mybir.dt.bfloat16
ALU = mybir.AluOpType
AF = mybir.ActivationFunctionType
AX = mybir.AxisListType

NCORE = 8
NB = 8
T = 128
DM = 2048
INW = 10128
EPS = 1e-6
SCALE_A = 128 ** -0.5
SCALE_B = 128 ** -0.5
WIDX_C = 16 ** -0.5 * 128 ** -0.5
NEG = -1.0e30
NBIS = 18
TOPK = 256.0


class Prog:
    def __init__(self, engines):
        self.sem = {}
        self.nins = {n: 0 for n in engines}
        self.known = {n: {} for n in engines}
        self.lastw = {}
        self.reads = {}
        self.dsem = {}
        self.dcnt = {}
        self.dnext = {}
        self.acts = {n: [] for n in engines}
        self.needed = {n: set() for n in engines}
        self.n_inst = 0

    def finalize(self):
        self.val = {}
        for en, need in self.needed.items():
            v = 0
            for idx in sorted(need):
                v += 1
                self.val[(en, idx)] = v

    def emit(self, en, e):
        for a in self.acts[en]:
            if a[0] == "w":
                ev = a[1]
                if ev[0] == "e":
                    e.wait_ge(self.sem[ev[1]], self.val[(ev[1], ev[2])])
                else:
                    e.wait_ge(ev[1], ev[2])
            elif a[0] == "c":
                a[1](e).then_inc(a[2])
            elif a[0] == "d":
                a[1](e).then_inc(a[2], 16)
            else:
                inst = a[1](e)
                if a[2] in self.needed[en]:
                    inst.then_inc(self.sem[en], 1)

    def add_engine_sem(self, name, sem):
        self.sem[name] = sem

    def add_dma_sems(self, name, sems):
        self.dsem[name] = list(sems)
        self.dcnt[name] = [0] * len(sems)
        self.dnext[name] = 0

    def _deps(self, reads, writes):
        deps = []
        for k in list(reads) + list(writes):
            ev = self.lastw.get(k)
            if ev is not None:
                deps.append(ev)
        for k in writes:
            deps.extend(self.reads.get(k, []))
        return deps

    @staticmethod
    def _kv(ev):
        if ev[0] == "e":
            return ("e", ev[1]), ev[2] + 1
        return ("s", id(ev[1])), ev[2]

    def _wait(self, en, deps):
        best = {}
        for ev in deps:
            if ev[0] == "e" and ev[1] == en and en == "pe":
                continue
            k, v = self._kv(ev)
            if v > best.get(k, (0, None))[0]:
                best[k] = (v, ev)
        for k, (v, ev) in best.items():
            if self.known[en].get(k, 0) >= v:
                continue
            self.acts[en].append(("w", ev))
            if ev[0] == "e":
                self.needed[ev[1]].add(ev[2])
            self.known[en][k] = v

    def _commit(self, ev, reads, writes):
        for k in writes:
            self.lastw[k] = ev
            self.reads[k] = []
        for k in reads:
            if k in writes:
                continue
            lst = self.reads.setdefault(k, [])
            lst.append(ev)
            if len(lst) > 48:
                best = {}
                for e2 in lst:
                    kk, v = self._kv(e2)
                    if v > best.get(kk, (0, None))[0]:
                        best[kk] = (v, e2)
                self.reads[k] = [x[1] for x in best.values()]

    def op(self, en, fn, reads=(), writes=()):
        self._wait(en, self._deps(reads, writes))
        idx = self.nins[en]
        self.nins[en] += 1
        self.acts[en].append(("i", fn, idx))
        ev = ("e", en, idx)
        self._commit(ev, reads, writes)
        self.n_inst += 1
        return ev

    def dma(self, en, stream, out, in_, reads=(), writes=()):
        i = self.dnext[stream]
        self.dnext[stream] = (i + 1) % len(self.dsem[stream])
        sem = self.dsem[stream][i]
        deps = self._deps(reads, writes)
        if self.dcnt[stream][i] > 0 and en != "pool":
            deps.append(("d", sem, self.dcnt[stream][i]))
        self._wait(en, deps)
        self.dcnt[stream][i] += 16
        self.acts[en].append(("d", (lambda e, out=out, in_=in_: e.dma_start(out=out, in_=in_)), sem))
        ev = ("d", sem, self.dcnt[stream][i])
        self._commit(ev, reads, writes)
        self.n_inst += 1
        return ev

    def alias_last(self, keys):
        evs = [self.lastw[k] for k in keys]
        newest = max(evs, key=lambda ev: self._kv(ev)[1])
        for k in keys:
            self.lastw[k] = newest

    def collective(self, fn, sem, reads=(), writes=()):
        en = "pool"
        self._wait(en, self._deps(reads, writes))
        self.acts[en].append(("c", fn, sem))
        ev = ("d", sem, 1)
        self.acts[en].append(("w", ev))
        self.known[en][("s", id(sem))] = 1
        self._commit(ev, reads, writes)

    def wait_all(self, en):
        deps = []
        for n in self.nins:
            if self.nins[n] > 0:
                deps.append(("e", n, self.nins[n] - 1))
        for n, ss in self.dsem.items():
            for s_, c in zip(ss, self.dcnt[n]):
                if c > 0:
                    deps.append(("d", s_, c))
        self._wait(en, deps)

    def barrier(self):
        for en in self.acts:
            self.wait_all(en)


def _t5_bucket_np(rel):
    rel = jnp.asarray(rel, jnp.int32)
    half = 16
    max_exact = 8
    base = jnp.where(rel > 0, half, 0)
    n = jnp.abs(rel)
    large = max_exact + (jnp.log(jnp.maximum(n, 1).astype(jnp.float32) / max_exact)
                         / math.log(128 / max_exact) * (half - max_exact)).astype(jnp.int32)
    large = jnp.minimum(large, half - 1)
    return np.asarray(base + jnp.where(n < max_exact, n, large))


def build_nc(dbg=False):
    nc = bass.Bass("TRN2", target_bir_lowering=False)

    def din(name, shape, dt=F32):
        return nc.dram_tensor(name, shape, dt, kind="ExternalInput").ap()

    x_in = din("x", [NB, T, DM])
    w_in = din("w_in", [DM, INW])
    normw = din("normw", [1, DM])
    lbt = din("lbt", [2, 1024])
    gnorm = din("gnorm", [1, 128])
    qnw = din("qnw", [1, 512])
    kvnw = din("kvnw", [1, 256])
    w_uq = din("w_uq", [512, 1024])
    w_qidx = din("w_qidx", [512, 2048])
    w_ukv = din("w_ukv", [256, 2048])
    kiw = din("kiw", [1, 128])
    kib = din("kib", [1, 128])
    w_pa = din("w_pa", [1024, DM])
    w_pb = din("w_pb", [1024, DM])
    w_out = din("w_out", [DM, DM])
    relb = din("relb", [32, 8])
    fnw = din("fnw", [1, DM])
    c_ident = din("c_ident", [128, 128])
    c_LT = din("c_LT", [128, 128])
    c_ind2 = din("c_ind2", [128, 2])
    c_cmask = din("c_cmask", [128, 128])
    c_J = din("c_J", [128, 128])
    c_mbias = din("c_mbias", [128, 1024])
    c_OH = din("c_OH", [32, 1280])
    c_sel = din("c_sel", [128, 64])
    y_out = nc.dram_tensor("y", [NB, T, DM], F32, kind="ExternalOutput").ap()

    proj = nc.dram_tensor("proj", [NB, T, INW], F32).ap()
    oint = nc.dram_tensor("oint", [NB, T, 1024], F32).ap()
    sendA = nc.dram_tensor("sendA", [128, 9216], BF16)
    gathA = nc.dram_tensor("gathA", [NCORE * 128, 9216], BF16)
    sendB = nc.dram_tensor("sendB", [1024, 2064], BF16)
    gathB = nc.dram_tensor("gathB", [NCORE * 1024, 2064], BF16)
    bvec = nc.dram_tensor("bvec", [8, 1280], F32)
    fl_in = nc.dram_tensor("fl_in", [128, 64], BF16)
    fl_out = nc.dram_tensor("fl_out", [NCORE * 128, 64], BF16)
    fl_out2 = nc.dram_tensor("fl_out2", [NCORE * 128, 64], BF16)
    qiT_scr = nc.dram_tensor("qiT_scr", [NB, 128, 2048], BF16).ap()
    yaT_scr = nc.dram_tensor("yaT_scr", [128, 8192], BF16).ap()
    ybT_scr = nc.dram_tensor("ybT_scr", [128, 8192], BF16).ap()
    dbg_out = {}
    if dbg:
        for nm, shp in [("d_proj0", [128, 512]), ("d_ya", [128, 1024]), ("d_sel", [128, 1024]),
                        ("d_attn", [128, 1024]), ("d_acc", [128, 1024]), ("d_S", [128, 1024]),
                        ("d_S0", [128, 1024]), ("d_ya0", [128, 1024]), ("d_attn0", [128, 1024]), ("d_sel0", [128, 1024]),
                        ("d_oi0", [128, 1024]), ("d_den0", [128, 8])]:
            dbg_out[nm] = nc.dram_tensor(nm, shp, F32, kind="ExternalOutput").ap()

    ENG = ["pe", "act", "dve", "pool", "sp"]
    P = Prog(ENG)

    with ExitStack() as es:
        sems = {n: es.enter_context(nc.semaphore("s_" + n)) for n in ENG}
        for n in ENG:
            P.add_engine_sem(n, sems[n])
        for nm, k in [("c", 1), ("cp", 1), ("x", 2), ("w", 2), ("st", 3), ("ld", 3), ("g", 4), ("kv", 4), ("o", 1)]:
            P.add_dma_sems(nm, [es.enter_context(nc.semaphore("d_%s%d" % (nm, i))) for i in range(k)])
        cc1 = es.enter_context(nc.semaphore("cc1"))
        cc2 = es.enter_context(nc.semaphore("cc2"))
        cc3 = es.enter_context(nc.semaphore("cc3"))
        cc4 = es.enter_context(nc.semaphore("cc4"))
        PA = es.enter_context(nc.psum_tensor("PA", [128, 1024], F32))
        PB = es.enter_context(nc.psum_tensor("PB", [128, 1024], F32))
        PC = es.enter_context(nc.psum_tensor("PC", [128, 1024], F32))
        PT0 = es.enter_context(nc.psum_tensor("PT0", [128, 1024], BF16))
        PT1 = es.enter_context(nc.psum_tensor("PT1", [128, 1024], BF16))
        block = es.enter_context(nc.Block())

        def MM(out, lhsT, rhs, st, sp, r, w):
            P.op("pe", lambda e: e.matmul(out, lhsT=lhsT, rhs=rhs, start=st, stop=sp), r, w)

        def ACT(out, in_, func, r, w, **kw):
            P.op("act", lambda e: e.activation(out=out, in_=in_, func=func, **kw), r, w)

        def TT(out, a, b, op, r, w, en="dve"):
            P.op(en, lambda e: e.tensor_tensor(out=out, in0=a, in1=b, op=op), r, w)

        def TS(out, a, s1, op0, r, w, s2=None, op1=None, en="dve", acc=None):
            if op1 is None:
                P.op(en, lambda e: e.tensor_scalar(out=out, in0=a, scalar1=s1, scalar2=None, op0=op0), r, w)
            elif acc is None:
                P.op(en, lambda e: e.tensor_scalar(out=out, in0=a, scalar1=s1, scalar2=s2, op0=op0, op1=op1), r, w)
            else:
                P.op(en, lambda e: e.tensor_scalar(out=out, in0=a, scalar1=s1, scalar2=s2, op0=op0, op1=op1, accum_out=acc), r, w)

        def STT(out, a, s, b, op0, op1, r, w):
            P.op("dve", lambda e: e.scalar_tensor_tensor(out=out, in0=a, scalar=s, in1=b, op0=op0, op1=op1), r, w)

        def CP(en, out, in_, r, w):
            if en == "act":
                P.op("act", lambda e: e.copy(out=out, in_=in_), r, w)
            else:
                P.op(en, lambda e: e.tensor_copy(out=out, in_=in_), r, w)

        def RED(out, in_, op, r, w, ax=AX.X):
            P.op("dve", lambda e: e.tensor_reduce(out=out, in_=in_, axis=ax, op=op), r, w)

        def MEMSET(en, ap, v, w):
            P.op(en, lambda e: e.memset(ap, v), [], w)

        def DMA(en, stream, out, in_, r, w):
            P.dma(en, stream, out, in_, r, w)

        def sb(stack, name, shape, dt):
            return stack.enter_context(nc.sbuf_tensor(name, shape, dt))

        def rstd_from_ss(out, ss, tmp, n, r, w, tk):
            ACT(tmp, ss, AF.Ln, r, [tk], scale=1.0 / n, bias=EPS)
            ACT(out, tmp, AF.Exp, [tk], w, scale=-0.5)

        identf = sb(es, "identf", [128, 128], F32)
        identb = sb(es, "identb", [128, 128], BF16)
        DMA("sp", "c", identf[:], c_ident, [], ["identf"])
        CP("dve", identb[:], identf[:], ["identf"], ["identb"])

        def TR(out, in_, r, w):
            P.op("pe", lambda e: e.transpose(out=out, in_=in_, identity=identb[:]), list(r) + ["identb"], w)

        with ExitStack() as ph:
            hT = sb(ph, "hT", [128, 16, 1024], BF16)
            normw_bc = sb(ph, "normw_bc", [128, DM], F32)
            xt = [sb(ph, "xt%d" % i, [128, DM], F32) for i in range(2)]
            hb = [sb(ph, "hb%d" % i, [128, DM], BF16) for i in range(2)]
            junk = sb(ph, "junk1", [128, DM], F32)
            st1 = sb(ph, "st1", [128, 8], F32)
            DMA("sp", "c", normw_bc[:], normw.partition_broadcast(128), [], ["normw_bc"])
            for j in range(NB):
                b = j % 2
                DMA("sp", "x", xt[b][:], x_in[j], [], [("xt", b)])
                ACT(junk[:], xt[b][:], AF.Square, [("xt", b)], ["junk1", "st1a"], accum_out=st1[:, 0:1])
                rstd_from_ss(st1[:, 2:3], st1[:, 0:1], st1[:, 1:2], DM, ["st1a"], ["st1c"], "st1b")
                STT(hb[b][:], xt[b][:], st1[:, 2:3], normw_bc[:], ALU.mult, ALU.mult,
                    [("xt", b), "st1c", "normw_bc"], [("hb", b)])
                for half in range(2):
                    pt = PT0 if half == 0 else PT1
                    pk = "PT%d" % half
                    for k in range(8):
                        kc = half * 8 + k
                        TR(pt[:, k * 128:(k + 1) * 128], hb[b][:, kc * 128:(kc + 1) * 128], [("hb", b)], [pk])
                    CP("act" if half == 0 else "dve",
                       hT[:, half * 8:(half + 1) * 8, j * 128:(j + 1) * 128],
                       pt[:].rearrange("p (k t) -> p k t", k=8), [pk], [("hT", j)])
            wb = [sb(ph, "wb%d" % i, [128, 16, 512], BF16) for i in range(2)]
            stg = [sb(ph, "stg%d" % i, [128, 512], F32) for i in range(4)]
            banks = [(PA, 0), (PA, 1), (PB, 0), (PB, 1), (PC, 0), (PC, 1)]
            nct = (INW + 511) // 512
            it = 0
            for ct in range(nct):
                c0 = ct * 512
                cw = min(512, INW - c0)
                wbuf = wb[ct % 2]
                wk = ("wb", ct % 2)
                DMA("pool", "w", wbuf[:, :, 0:cw], w_in[:, c0:c0 + cw].rearrange("(k p) c -> p k c", p=128), [], [wk])
                for j in range(NB):
                    pst, hf = banks[it % 6]
                    pk = (id(pst), hf)
                    po = pst[:, hf * 512: hf * 512 + cw]
                    for k in range(16):
                        MM(po, hT[:, k, j * 128:(j + 1) * 128], wbuf[:, k, 0:cw], k == 0, k == 15,
                           [("hT", j), wk], [pk])
                    sg = stg[it % 4]
                    sk = ("stg", it % 4)
                    CP("act" if it % 2 == 0 else "dve", sg[:, 0:cw], po, [pk], [sk])
                    DMA("sp", "st", proj[j, :, c0:c0 + cw], sg[:, 0:cw], [sk], [("proj", j, ct)])
                    if dbg and ct == 0 and j == 0:
                        DMA("sp", "o", dbg_out["d_proj0"], sg[:, 0:512], [sk], [])
                    it += 1
            P.barrier()
        PROJ_ALL = lambda j: [("proj", j, ct) for ct in range(nct)]

        with ExitStack() as ph3:
            qeT_all = sb(ph3, "qeT_all", [128, NB, 8, 128], BF16)
            emid_all = sb(ph3, "emid_all", [128, NB, 8], F32)
            with ExitStack() as ph:
                LT = sb(ph, "LT", [128, 128], F32)
                ind2 = sb(ph, "ind2", [128, 2], F32)
                cmask8 = sb(ph, "cmask8", [128, 8, 128], F32)
                lb_bc = sb(ph, "lb_bc", [128, 1024], F32)
                oml_bc = sb(ph, "oml_bc", [128, 1024], F32)
                lt1 = sb(ph, "lt1", [128, 1024], F32)
                kvn_bc = sb(ph, "kvn_bc", [128, 256], F32)
                kiw_bc = sb(ph, "kiw_bc", [128, 128], F32)
                kib_bc = sb(ph, "kib_bc", [128, 128], F32)
                wukv = sb(ph, "wukv", [128, 2, 2048], BF16)
                DMA("sp", "c", LT[:], c_LT, [], ["LT"])
                DMA("sp", "c", ind2[:], c_ind2, [], ["ind2"])
                for h in range(8):
                    DMA("sp", "c", cmask8[:, h, :], c_cmask, [], ["cmask8"])
                DMA("sp", "c", lb_bc[:], lbt[0:1, :].partition_broadcast(128), [], ["lb_bc"])
                DMA("sp", "c", lt1[:], lbt[1:2, :].partition_broadcast(128), [], ["lt1"])
                DMA("sp", "c", kvn_bc[:], kvnw.partition_broadcast(128), [], ["kvn_bc"])
                DMA("sp", "c", kiw_bc[:], kiw.partition_broadcast(128), [], ["kiw_bc"])
                DMA("sp", "c", kib_bc[:], kib.partition_broadcast(128), [], ["kib_bc"])
                DMA("pool", "cp", wukv[:], w_ukv.rearrange("(k p) c -> p k c", p=128), [], ["wukv"])
                TT(lt1[:], lb_bc[:], lt1[:], ALU.subtract, ["lb_bc", "lt1"], ["lt1"])
                ACT(lb_bc[:], lt1[:], AF.Sigmoid, ["lt1"], ["lb_bc"])
                ACT(oml_bc[:], lt1[:], AF.Sigmoid, ["lt1"], ["oml_bc"], scale=-1.0)

                a3 = [sb(ph, "a3_%d" % i, [128, 3072], F32) for i in range(2)]
                ck = [sb(ph, "ck_%d" % i, [128, 400], F32) for i in range(2)]
                t1 = sb(ph, "t1", [128, 1024], F32)
                gt = sb(ph, "gt", [128, 1024], F32)
                e1 = sb(ph, "e1", [128, 1024], F32)
                e2 = sb(ph, "e2", [128, 1024], F32)
                t2 = sb(ph, "t2", [128, 1024], F32)
                qe_b = sb(ph, "qe_b", [128, 1024], BF16)
                ke_b = sb(ph, "ke_b", [128, 1024], BF16)
                v_b = sb(ph, "v_b", [128, 1024], BF16)
                keT = sb(ph, "keT", [128, 8, 128], BF16)
                AT_b = sb(ph, "AT_b", [128, 8, 128], BF16)
                oi = sb(ph, "oi", [128, 1024], F32)
                sm = sb(ph, "sm", [128, 64], F32)
                sendb = sb(ph, "sendb", [128, 2064], BF16)
                sendk = sb(ph, "sendk", [128, 8, 128], BF16)
                sendi = sb(ph, "sendi", [128, 128], BF16)
                cn = sb(ph, "cn", [128, 256], BF16)
                ckvT = sb(ph, "ckvT", [128, 2, 128], BF16)
                kn = sb(ph, "kn", [128, 128], F32)
                knb = sb(ph, "knb", [128, 128], BF16)
                junk3 = sb(ph, "junk3", [128, 256], F32)
                MEMSET("pool", sendb[:, 1032:2064], 1.0, ["sendV"])
                for j in range(NB):
                    b = j % 2
                    DMA("sp", "ld", a3[b][:], proj[j, :, 0:3072], PROJ_ALL(j), [("a3", b)])
                    DMA("sp", "ld", ck[b][:], proj[j, :, 4608:5008], PROJ_ALL(j), [("ck", b)])
                    aq = a3[b][:, 0:1024]
                    af = a3[b][:, 1024:2048]
                    ai = a3[b][:, 2048:3072]
                    ACT(t1[:], af, AF.Sigmoid, [("a3", b)], ["t1"])
                    TT(t1[:], t1[:], oml_bc[:], ALU.mult, ["t1", "oml_bc"], ["t1"])
                    TT(t1[:], t1[:], lb_bc[:], ALU.add, ["t1", "lb_bc"], ["t1"])
                    ACT(gt[:], t1[:], AF.Ln, ["t1"], ["gt"])
                    TS(t1[:], t1[:], -1.0, ALU.mult, ["t1"], ["t1"], s2=1.0, op1=ALU.add)
                    for hf in range(2):
                        MM(PA[:, hf * 512:(hf + 1) * 512], LT[:], gt[:, hf * 512:(hf + 1) * 512], True, True,
                           ["LT", "gt"], [(id(PA), hf)])
                    for h in range(8):
                        MM(PB[:, 2 * h:2 * h + 2], gt[:, h * 128:(h + 1) * 128], ind2[:], True, True,
                           ["gt", "ind2"], [(id(PB), 0)])
                    CP("dve", sm[:, 0:16], PB[:, 0:16], [(id(PB), 0)], ["sm"])
                    smv = sm[:, 0:16].rearrange("p (h two) -> p h two", two=2)
                    TT(sm[:, 16:24], smv[:, :, 1], smv[:, :, 0], ALU.subtract, ["sm"], ["sm2"])
                    ACT(sm[:, 24:32], sm[:, 16:24], AF.Exp, ["sm2"], ["elm"])
                    ACT(emid_all[:, j, :], smv[:, :, 0], AF.Exp, ["sm"], [("emid", j)])
                    ACT(sendb[:, 1024:1032], smv[:, :, 1], AF.Exp, ["sm"], ["sendD"])
                    for hf in range(2):
                        sl = slice(hf * 512, (hf + 1) * 512)
                        ACT(e1[:, sl], PA[:, sl], AF.Exp, [(id(PA), hf)], [("e1", hf)])
                        ACT(e2[:, sl], PA[:, sl], AF.Exp, [(id(PA), hf)], [("e2", hf)], scale=-1.0)
                    ACT(t2[:], aq, AF.Silu, [("a3", b)], ["t2"])
                    STT(qe_b[:], t2[:], SCALE_A, e1[:], ALU.mult, ALU.mult, ["t2", ("e1", 0), ("e1", 1)], ["qe_b"])
                    TT(ke_b[:], t1[:], e2[:], ALU.mult, ["t1", ("e2", 0), ("e2", 1)], ["ke_b"])
                    CP("pool", v_b[:], ai, [("a3", b)], ["v_b"])
                    for h in range(8):
                        TR(PT0[:, h * 128:(h + 1) * 128], qe_b[:, h * 128:(h + 1) * 128], ["qe_b"], ["PT0"])
                    for h in range(8):
                        TR(PT1[:, h * 128:(h + 1) * 128], ke_b[:, h * 128:(h + 1) * 128], ["ke_b"], ["PT1"])
                    CP("act", qeT_all[:, j, :, :], PT0[:].rearrange("p (h t) -> p h t", h=8), ["PT0"], [("qeT", j)])
                    CP("dve", keT[:], PT1[:].rearrange("p (h t) -> p h t", h=8), ["PT1"], ["keT"])
                    for h in range(8):
                        MM(PB[:, h * 128:(h + 1) * 128], keT[:, h, :], qeT_all[:, j, h, :], True, True,
                           ["keT", ("qeT", j)], [(id(PB), h // 4)])
                    for hf in range(2):
                        TT(AT_b[:, hf * 4:(hf + 1) * 4, :],
                           PB[:, hf * 512:(hf + 1) * 512].rearrange("p (h t) -> p h t", h=4),
                           cmask8[:, hf * 4:(hf + 1) * 4, :], ALU.mult, [(id(PB), hf), "cmask8"], [("AT", hf)])
                    for h in range(8):
                        MM(PA[:, h * 128:(h + 1) * 128], AT_b[:, h, :], v_b[:, h * 128:(h + 1) * 128], True, True,
                           [("AT", h // 4), "v_b"], [(id(PA), h // 4)])
                    for hf in range(2):
                        sl = slice(hf * 512, (hf + 1) * 512)
                        CP("act", oi[:, sl], PA[:, sl], [(id(PA), hf)], ["oi"])
                    DMA("sp", "st", oint[j], oi[:], ["oi"], [("oint", j)])
                    for h in range(8):
                        MM(PC[:, h * 128:(h + 1) * 128], ke_b[:, h * 128:(h + 1) * 128], v_b[:, h * 128:(h + 1) * 128],
                           True, True, ["ke_b", "v_b"], [(id(PC), h // 4)])
                    for h in range(8):
                        TS(sendb[:, h * 128:(h + 1) * 128], PC[:, h * 128:(h + 1) * 128], sm[:, 24 + h:25 + h], ALU.mult,
                           [(id(PC), h // 4), "elm"], ["sendS"])
                    ckv = ck[b][:, 0:256]
                    kraw = ck[b][:, 256:384]
                    ACT(junk3[:], ckv, AF.Square, [("ck", b)], ["junk3", "sm3a"], accum_out=sm[:, 32:33])
                    rstd_from_ss(sm[:, 34:35], sm[:, 32:33], sm[:, 33:34], 256, ["sm3a"], ["sm3c"], "sm3b")
                    STT(cn[:], ckv, sm[:, 34:35], kvn_bc[:], ALU.mult, ALU.mult, [("ck", b), "sm3c", "kvn_bc"], ["cn"])
                    for k in range(2):
                        TR(PT0[:, k * 128:(k + 1) * 128], cn[:, k * 128:(k + 1) * 128], ["cn"], ["PT0"])
                    CP("dve", ckvT[:], PT0[:, 0:256].rearrange("p (k t) -> p k t", k=2), ["PT0"], ["ckvT"])
                    wv = wukv[:].rearrange("p k (h c) -> p k h c", h=8)
                    for h in range(8):
                        for k in range(2):
                            MM(PB[:, h * 128:(h + 1) * 128], wv[:, k, h, 0:128], ckvT[:, k, :], k == 0, k == 1,
                               ["wukv", "ckvT"], [(id(PB), h // 4)])
                    for hf in range(2):
                        CP("act", sendk[:, hf * 4:(hf + 1) * 4, :],
                           PB[:, hf * 512:(hf + 1) * 512].rearrange("p (h t) -> p h t", h=4), [(id(PB), hf)], ["sendk"])
                    for hf in range(2):
                        for k in range(2):
                            MM(PC[:, hf * 512:(hf + 1) * 512], ckvT[:, k, :], wv[:, k, hf * 4:(hf + 1) * 4, 128:256],
                               k == 0, k == 1, ["wukv", "ckvT"], [(id(PC), hf)])
                    sv = sendb[:, 1032:2064].rearrange("p (h c) -> p h c", c=129)
                    for hf in range(2):
                        CP("dve", sv[:, hf * 4:(hf + 1) * 4, 0:128],
                           PC[:, hf * 512:(hf + 1) * 512].rearrange("p (h c) -> p h c", h=4), [(id(PC), hf)], ["sendV"])
                    RED(sm[:, 36:37], kraw, ALU.add, [("ck", b)], ["sm4a"])
                    TS(sm[:, 37:38], sm[:, 36:37], -1.0 / 128, ALU.mult, ["sm4a"], ["sm4b"])
                    TS(kn[:], kraw, sm[:, 37:38], ALU.add, [("ck", b), "sm4b"], ["kn"])
                    ACT(junk3[:, 0:128], kn[:], AF.Square, ["kn"], ["junk3", "sm4c"], accum_out=sm[:, 38:39])
                    rstd_from_ss(sm[:, 40:41], sm[:, 38:39], sm[:, 39:40], 128, ["sm4c"], ["sm4e"], "sm4d")
                    STT(kn[:], kn[:], sm[:, 40:41], kiw_bc[:], ALU.mult, ALU.mult, ["kn", "sm4e", "kiw_bc"], ["kn"])
                    TT(knb[:], kn[:], kib_bc[:], ALU.add, ["kn", "kib_bc"], ["knb"])
                    TR(PT1[:, 0:128], knb[:], ["knb"], ["PT1"])
                    CP("act", sendi[:], PT1[:, 0:128], ["PT1"], ["sendi"])
                    sA = sendA.ap()
                    DMA("sp", "st", sendB.ap()[j * 128:(j + 1) * 128, :], sendb[:], ["sendS", "sendD", "sendV"], ["sendB"])
                    DMA("sp", "st", sA[:, 0:8192].rearrange("p (h j t) -> p h j t", h=8, j=8)[:, :, j, :], sendk[:],
                        ["sendk"], ["sendA"])
                    DMA("sp", "st", sA[:, 8192 + j * 128: 8192 + (j + 1) * 128], sendi[:], ["sendi"], ["sendA"])
                P.barrier()
            P.collective(lambda e: e.collective_compute("AllGather", ALU.bypass, replica_groups=[list(range(NCORE))],
                                                        ins=[sendA.ap().opt()], outs=[gathA.ap().opt()]),
                         cc1, ["sendA"], ["gathA"])
            P.collective(lambda e: e.collective_compute("AllGather", ALU.bypass, replica_groups=[list(range(NCORE))],
                                                        ins=[sendB.ap().opt()], outs=[gathB.ap().opt()]),
                         cc2, ["sendB"], ["gathB"])
            gA = gathA.ap()
            gB = gathB.ap()

            with ExitStack() as ph:
                S = sb(ph, "S", [128, 1024], F32)
                save = sb(ph, "save", [128, NB, 1024], F32)
                selc = sb(ph, "selc", [128, 64], F32)
                slb = [sb(ph, "slb%d" % i, [128, 1032], BF16) for i in range(4)]
                dfl = [sb(ph, "dfl%d" % i, [128, 8], F32) for i in range(2)]
                DMA("sp", "c", selc[:], c_sel, [], ["selc"])
                MEMSET("dve", S[:], 0.0, ["S"])
                for bb in range(64):
                    r, jj = bb % 8, bb // 8
                    sl = slb[bb % 4]
                    slk = ("slb", bb % 4)
                    row0 = r * 1024 + jj * 128
                    DMA("pool", "g", sl[:], gB[row0:row0 + 128, 0:1032], ["gathB"], [slk])
                    if r == 0:
                        TS(save[:, jj, :], S[:], selc[:, bb:bb + 1], ALU.mult, ["S", "selc"], [("save", jj)])
                    else:
                        STT(save[:, jj, :], S[:], selc[:, bb:bb + 1], save[:, jj, :], ALU.mult, ALU.add,
                            ["S", "selc", ("save", jj)], [("save", jj)])
                    df = dfl[bb % 2]
                    CP("act", df[:], sl[:, 1024:1032], [slk], [("dfl", bb % 2)])
                    for h in range(8):
                        hs = slice(h * 128, (h + 1) * 128)
                        STT(S[:, hs], S[:, hs], df[:, h:h + 1], sl[:, hs], ALU.mult, ALU.add,
                            ["S", ("dfl", bb % 2), slk], ["S"])
                if dbg:
                    DMA("sp", "o", dbg_out["d_S"], save[:, 1, :], [("save", 1)], [])
                    DMA("sp", "o", dbg_out["d_S0"], save[:, 0, :], [("save", 0)], [])

                gn_bc = sb(ph, "gn_bc", [128, 128], F32)
                sinb = sb(ph, "sinb", [128, 1024], BF16)
                oi2 = [sb(ph, "oi2_%d" % i, [128, 1024], F32) for i in range(2)]
                ag2 = [sb(ph, "ag2_%d" % i, [128, 1024], F32) for i in range(2)]
                o5 = sb(ph, "o5", [128, 1024], F32)
                j5 = sb(ph, "j5", [128, 1024], F32)
                sg5 = sb(ph, "sg5", [128, 1024], F32)
                y5 = sb(ph, "y5", [128, 1024], F32)
                yab = sb(ph, "yab", [128, 1024], BF16)
                yaT = sb(ph, "yaT", [128, 8, 128], BF16)
                s5 = sb(ph, "s5", [128, 32], F32)
                DMA("sp", "c", gn_bc[:], gnorm.partition_broadcast(128), [], ["gn_bc"])
                for j in range(NB):
                    b = j % 2
                    DMA("sp", "ld", oi2[b][:], oint[j], [("oint", j)], [("oi2", b)])
                    DMA("sp", "ld", ag2[b][:], proj[j, :, 3072:4096], PROJ_ALL(j), [("ag2", b)])
                    for h in range(8):
                        hs = slice(h * 128, (h + 1) * 128)
                        TS(sinb[:, hs], save[:, j, hs], emid_all[:, j, h:h + 1], ALU.mult,
                           [("save", j), ("emid", j)], ["sinb"])
                    for h in range(8):
                        hs = slice(h * 128, (h + 1) * 128)
                        MM(PA[:, hs], qeT_all[:, j, h, :], sinb[:, hs], True, True, [("qeT", j), "sinb"], [(id(PA), h // 4)])
                    for hf in range(2):
                        sl_ = slice(hf * 512, (hf + 1) * 512)
                        TT(o5[:, sl_], PA[:, sl_], oi2[b][:, sl_], ALU.add, [(id(PA), hf), ("oi2", b)], ["o5"])
                    ACT(j5[:], o5[:], AF.Square, ["o5"], ["j5"])
                    RED(s5[:, 0:8], j5[:].rearrange("p (h v) -> p h v", h=8), ALU.add, ["j5"], ["s5a"])
                    rstd_from_ss(s5[:, 16:24], s5[:, 0:8], s5[:, 8:16], 128, ["s5a"], ["s5c"], "s5b")
                    ACT(sg5[:], ag2[b][:], AF.Silu, [("ag2", b)], ["sg5"])
                    for h in range(8):
                        hs = slice(h * 128, (h + 1) * 128)
                        STT(y5[:, hs], o5[:, hs], s5[:, 16 + h:17 + h], gn_bc[:], ALU.mult, ALU.mult,
                            ["o5", "s5c", "gn_bc"], ["y5"])
                    TT(yab[:], y5[:], sg5[:], ALU.mult, ["y5", "sg5"], ["yab"])
                    if dbg and j == 1:
                        DMA("sp", "o", dbg_out["d_ya"], y5[:], ["y5"], [])
                    if dbg and j == 0:
                        DMA("sp", "o", dbg_out["d_ya0"], y5[:], ["y5"], [])
                        DMA("sp", "o", dbg_out["d_oi0"], oi2[b][:], [("oi2", b)], [])
                    for h in range(8):
                        TR(PT0[:, h * 128:(h + 1) * 128], yab[:, h * 128:(h + 1) * 128], ["yab"], ["PT0"])
                    CP("act", yaT[:], PT0[:].rearrange("p (h t) -> p h t", h=8), ["PT0"], ["yaT"])
                    DMA("sp", "st", yaT_scr.rearrange("p (k t) -> p k t", k=8)[:, :, j * 128:(j + 1) * 128], yaT[:],
                        ["yaT"], ["yaT_scr"])
                P.barrier()

        with ExitStack() as ph:
            kidxT = sb(ph, "kidxT", [128, 8, 1024], BF16)
            qT_all = sb(ph, "qT_all", [128, 8, 1024], BF16)
            wabs = sb(ph, "wabs", [128, NB, 16], F32)
            wsgn = sb(ph, "wsgn", [128, NB, 16], F32)
            EB = sb(ph, "EB", [128, 9, 8, 128], BF16)
            mb = sb(ph, "mb", [128, 8, 128], F32)
            DMA("pool", "cp", kidxT[:], gA[:, 8192:9216].rearrange("(r p) t -> p r t", p=128), ["gathA"], ["kidxT"])
            DMA("sp", "c", mb[:], c_mbias.rearrange("p (r t) -> p r t", r=8), [], ["mb"])
            with ExitStack() as pp:
                relb_s = sb(pp, "relb_s", [32, 8], F32)
                oh_s = sb(pp, "oh_s", [32, 1280], F32)
                bv_s = sb(pp, "bv_s", [8, 1280], F32)
                Jm = sb(pp, "Jm", [128, 128], F32)
                hk = [sb(pp, "hk%d" % i, [128, 128], F32) for i in range(4)]
                DMA("sp", "c", relb_s[:], relb, [], ["relb_s"])
                DMA("sp", "c", oh_s[:], c_OH, [], ["oh_s"])
                DMA("sp", "c", Jm[:], c_J, [], ["Jm"])
                for i, (v0, vw) in enumerate([(0, 512), (512, 512), (1024, 256)]):
                    MM(PA[0:8, 0:vw], relb_s[:], oh_s[:, v0:v0 + vw], True, True, ["relb_s", "oh_s"], [(id(PA), 0)])
                    CP("act", bv_s[:, v0:v0 + vw], PA[0:8, 0:vw], [(id(PA), 0)], ["bv_s"])
                DMA("sp", "st", bvec.ap(), bv_s[:], ["bv_s"], ["bvec"])
                it = 0
                for kb in range(9):
                    for h in range(8):
                        hb_ = hk[it % 4]
                        hkk = ("hk", it % 4)
                        DMA("sp", "ld", hb_[:], bass.AP(bvec, h * 1280 + kb * 128, [[1, 128], [1, 128]]), ["bvec"], [hkk])
                        pst, hf = [(PB, 0), (PB, 1), (PC, 0), (PC, 1)][it % 4]
                        MM(pst[:, hf * 512: hf * 512 + 128], hb_[:], Jm[:], True, True, [hkk, "Jm"], [(id(pst), hf)])
                        ACT(EB[:, kb, h, :], pst[:, hf * 512: hf * 512 + 128], AF.Exp, [(id(pst), hf)], ["EB"])
                        it += 1
                qn_bc = sb(pp, "qn_bc", [128, 512], F32)
                wuq = sb(pp, "wuq", [128, 4, 1024], BF16)
                wqi = sb(pp, "wqi", [128, 4, 2048], BF16)
                cqT = sb(pp, "cqT", [128, 4, 1024], BF16)
                cq = [sb(pp, "cq%d" % i, [128, 912], F32) for i in range(2)]
                cqn = sb(pp, "cqn", [128, 512], BF16)
                j6 = sb(pp, "j6", [128, 512], F32)
                s6 = sb(pp, "s6", [128, 8], F32)
                qiS = [sb(pp, "qiS%d" % i, [128, 1024], BF16) for i in range(2)]
                DMA("sp", "c", qn_bc[:], qnw.partition_broadcast(128), [], ["qn_bc"])
                DMA("pool", "cp", wuq[:], w_uq.rearrange("(k p) c -> p k c", p=128), [], ["wuq"])
                DMA("pool", "cp", wqi[:], w_qidx.rearrange("(k p) c -> p k c", p=128), [], ["wqi"])
                P.alias_last(["kidxT", "wuq", "wqi"])
                for j in range(NB):
                    b = j % 2
                    DMA("sp", "ld", cq[b][:], proj[j, :, 4096:5008], PROJ_ALL(j), [("cq", b)])
                    ACT(j6[:], cq[b][:, 0:512], AF.Square, [("cq", b)], ["j6", "s6a"], accum_out=s6[:, 0:1])
                    rstd_from_ss(s6[:, 2:3], s6[:, 0:1], s6[:, 1:2], 512, ["s6a"], ["s6c"], "s6b")
                    STT(cqn[:], cq[b][:, 0:512], s6[:, 2:3], qn_bc[:], ALU.mult, ALU.mult, [("cq", b), "s6c", "qn_bc"], ["cqn"])
                    for k in range(4):
                        TR(PT0[:, k * 128:(k + 1) * 128], cqn[:, k * 128:(k + 1) * 128], ["cqn"], ["PT0"])
                    CP("act", cqT[:, :, j * 128:(j + 1) * 128], PT0[:, 0:512].rearrange("p (k t) -> p k t", k=4), ["PT0"], ["cqT"])
                    ACT(wabs[:, j, :], cq[b][:, 896:912], AF.Abs, [("cq", b)], ["wabs"], scale=WIDX_C)
                    ACT(wsgn[:, j, :], cq[b][:, 896:912], AF.Sign, [("cq", b)], ["wsgn"])
                banks = [(PA, 0), (PA, 1), (PB, 0), (PB, 1), (PC, 0), (PC, 1)]
                it = 0
                for h in range(8):
                    for half in range(2):
                        pst, hf = banks[it % 6]
                        po = pst[:, hf * 512:(hf + 1) * 512]
                        for k in range(4):
                            MM(po, wuq[:, k, h * 128:(h + 1) * 128], cqT[:, k, half * 512:(half + 1) * 512], k == 0, k == 3,
                               ["wuq", "cqT"], [(id(pst), hf)])
                        ACT(qT_all[:, h, half * 512:(half + 1) * 512], po, AF.Copy, [(id(pst), hf)], ["qT_all"], scale=SCALE_B)
                        it += 1
                for h in range(16):
                    qb = qiS[h % 2]
                    qk = ("qiS", h % 2)
                    for half in range(2):
                        pst, hf = banks[it % 6]
                        po = pst[:, hf * 512:(hf + 1) * 512]
                        for k in range(4):
                            MM(po, wqi[:, k, h * 128:(h + 1) * 128], cqT[:, k, half * 512:(half + 1) * 512], k == 0, k == 3,
                               ["wqi", "cqT"], [(id(pst), hf)])
                        CP("act" if it % 2 == 0 else "dve", qb[:, half * 512:(half + 1) * 512], po, [(id(pst), hf)], [qk])
                        it += 1
                    DMA("sp", "st", qiT_scr.rearrange("j d (h t) -> d h j t", h=16)[:, h, :, :],
                        qb[:].rearrange("p (j t) -> p j t", j=8), [qk], ["qiT_scr"])
                P.barrier()

            acc = sb(ph, "acc", [128, 8, 8, 128], F32)
            selb = sb(ph, "selb", [128, 8, 8, 128], BF16)
            selT2 = [sb(ph, "selT%d" % i, [128, 8, 8, 128], BF16) for i in range(2)]
            qiT = [sb(ph, "qiT%d" % i, [128, 16, 128], BF16) for i in range(2)]
            rbuf = [sb(ph, "rbuf%d" % i, [128, 512], BF16) for i in range(4)]
            bs2 = [sb(ph, "bs%d" % i, [128, 16], F32) for i in range(2)]
            halfs2 = [sb(ph, "halfs%d" % i, [128, NBIS + 2], F32) for i in range(2)]
            pw = sb(ph, "pw", [128, NBIS + 2], F32)
            KTp = [sb(ph, "KTp%d" % i, [128, 8, 512], BF16) for i in range(2)]
            Vp = [sb(ph, "Vp%d" % i, [128, 4, 1032], BF16) for i in range(2)]
            et = [sb(ph, "et%d" % i, [128, 512], BF16) for i in range(3)]
            ptt = [sb(ph, "ptt%d" % i, [128, 4, 128], BF16) for i in range(3)]
            att = sb(ph, "att", [128, 1024], F32)
            bg2 = [sb(ph, "bg%d" % i, [128, 1024], F32) for i in range(2)]
            ybb = sb(ph, "ybb", [128, 1024], BF16)
            ybT = sb(ph, "ybT", [128, 8, 128], BF16)
            rd = sb(ph, "rd", [128, 8], F32)
            oacc = sb(ph, "oacc", [128, 8, 129], F32)
            for i in range(NBIS + 2):
                MEMSET("pool", pw[:, i:i + 1], 2.0 ** (-(i + 1)), ["pw"])
            banks_i = [(PA, 0), (PA, 1), (PB, 0), (PB, 1)]

            def key_tiles(nb_):
                tl = []
                for r in range(8):
                    s0 = 0
                    while s0 < nb_:
                        n = min(4, nb_ - s0)
                        tl.append((r, s0, n))
                        s0 += n
                return tl

            cnt_i = [0]

            def gen_indexer(j):
                nb_ = j + 1
                qb = qiT[j % 2]
                qk = ("qiT", j % 2)
                DMA("sp", "ld", qb[:], qiT_scr[j].rearrange("d (h t) -> d h t", h=16), ["qiT_scr"], [qk])
                DMA("sp", "ld", bg2[j % 2][:], proj[j, :, 5008:6032], PROJ_ALL(j), [("bg", j % 2)])
                tiles = key_tiles(nb_)
                for h in range(16):
                    for (r, s0, n) in tiles:
                        it = cnt_i[0]
                        cnt_i[0] += 1
                        ak = ("acc", r, s0)
                        pst, hf = banks_i[it % 4]
                        po = pst[:, hf * 512: hf * 512 + n * 128]
                        MM(po, qb[:, h, :], kidxT[:, r, s0 * 128:(s0 + n) * 128], True, True, [qk, "kidxT"], [(id(pst), hf)])
                        rb = rbuf[it % 4]
                        rk = ("rbuf", it % 4)
                        ACT(rb[:, 0:n * 128], po, AF.Relu, [(id(pst), hf), "wabs"], [rk], scale=wabs[:, j, h:h + 1])
                        av = acc[:, r, s0:s0 + n, :]
                        rv = rb[:, 0:n * 128].rearrange("p (n t) -> p n t", n=n)
                        if h == 0:
                            TS(av, rv, wsgn[:, j, h:h + 1], ALU.mult, [rk, "wsgn"], [ak])
                        else:
                            STT(av, rv, wsgn[:, j, h:h + 1], av, ALU.mult, ALU.add, [rk, "wsgn", ak], [ak])
                        yield

            def gen_bisect(j):
                nb_ = j + 1
                bs = bs2[j % 2]
                halfs = halfs2[j % 2]
                bk = "bs%d" % (j % 2)
                AK = [("acc", r, s0) for (r, s0, n) in key_tiles(nb_)]
                av = acc[:, :, 0:nb_, :]
                RED(bs[:, 0:1], av, ALU.max, AK, [bk + "0"], ax=AX.XYZ)
                yield
                RED(bs[:, 1:2], av, ALU.min, AK, [bk + "1"], ax=AX.XYZ)
                yield
                TT(acc[:, :, j, :], acc[:, :, j, :], mb[:], ALU.add, AK + ["mb"], AK + ["accm"])
                yield
                AKM = AK + ["accm"]
                TT(bs[:, 2:3], bs[:, 0:1], bs[:, 1:2], ALU.subtract, [bk + "0", bk + "1"], [bk + "w"])
                yield
                TS(bs[:, 2:3], bs[:, 2:3], 1.0001, ALU.mult, [bk + "w"], [bk + "w"], s2=1e-6, op1=ALU.add)
                yield
                TS(halfs[:], pw[:], bs[:, 2:3], ALU.mult, ["pw", bk + "w"], [bk + "h"])
                yield
                TS(bs[:, 3:4], bs[:, 1:2], halfs[:, 0:1], ALU.add, [bk + "1", bk + "h"], [bk + "thr"], s2=-1.0, op1=ALU.mult)
                yield
                cthr = float(2 * int(TOPK) - nb_ * 1024)
                for i in range(NBIS):
                    ACT(selb[:, :, 0:nb_, :], av, AF.Sign, AKM + [bk + "thr"], ["selb", bk + "cnt"],
                        bias=bs[:, 3:4], scale=1.0, accum_out=bs[:, 4:5])
                    yield
                    TT(bs[:, 7:8], bs[:, 3:4], halfs[:, i + 1:i + 2], ALU.add, [bk + "thr", bk + "h"], [bk + "tm"])
                    yield
                    STT(bs[:, 5:6], bs[:, 4:5], cthr, halfs[:, i:i + 1], ALU.is_ge, ALU.mult, [bk + "cnt", bk + "h"], [bk + "g"])
                    yield
                    TT(bs[:, 3:4], bs[:, 7:8], bs[:, 5:6], ALU.subtract, [bk + "g", bk + "tm"], [bk + "thr"])
                    yield
                TS(bs[:, 6:7], bs[:, 3:4], -1.0, ALU.mult, [bk + "thr", bk + "h"], [bk + "lo"], s2=halfs[:, NBIS:NBIS + 1], op1=ALU.subtract)
                yield
                TS(selb[:, :, 0:nb_, :], av, bs[:, 6:7], ALU.is_ge, AKM + [bk + "lo"], ["selb"])
                yield
                if dbg and j == 1:
                    DMA("sp", "o", dbg_out["d_acc"].rearrange("p (r t) -> p r t", r=8), acc[:, :, 1, :], AKM, [])
                    CP("pool", att[:].rearrange("p (r t) -> p r t", r=8), selb[:, :, 1, :], ["selb"], ["att"])
                    DMA("sp", "o", dbg_out["d_sel"], att[:], ["att"], [])
                    yield

            def gen_transposes(j):
                nb_ = j + 1
                selT = selT2[j % 2]
                it = 0
                for r in range(8):
                    s0 = 0
                    while s0 < nb_:
                        n = min(8, nb_ - s0)
                        pt = PT0 if it % 2 == 0 else PT1
                        pk = "PT%d" % (it % 2)
                        for q in range(n):
                            TR(pt[:, q * 128:(q + 1) * 128], selb[:, r, s0 + q, :], ["selb"], [pk])
                        CP("act", selT[:, r, s0:s0 + n, :],
                           pt[:, 0:n * 128].rearrange("p (n t) -> p n t", n=n), [pk], [("selT", j % 2, r)])
                        s0 += n
                        it += 1
                        yield

            cnt_a = [0]

            def gen_attention(j):
                nb_ = j + 1
                selT = selT2[j % 2]
                bg = bg2[j % 2]
                pieces = key_tiles(nb_)
                npc = len(pieces)
                for pi, (r, s0, n) in enumerate(pieces):
                    kb_ = KTp[pi % 2]
                    vb_ = Vp[pi % 2]
                    kk = ("KTp", pi % 2)
                    vk = ("Vp", pi % 2)
                    DMA("pool", "kv", kb_[:, :, 0:n * 128],
                        gA[r * 128:(r + 1) * 128, 0:8192].rearrange("p (h t) -> p h t", h=8)[:, :, s0 * 128:(s0 + n) * 128],
                        ["gathA"], [kk])
                    DMA("pool", "kv", vb_[:, 0:n, :],
                        gB[r * 1024 + s0 * 128: r * 1024 + (s0 + n) * 128, 1032:2064].rearrange("(n p) c -> p n c", p=128),
                        ["gathB"], [vk])
                    for h in range(8):
                        ia = cnt_a[0]
                        cnt_a[0] += 1
                        lk = (id(PC), ia % 2)
                        pl = PC[:, (ia % 2) * 512:(ia % 2) * 512 + n * 128]
                        for q in range(n):
                            MM(PC[:, (ia % 2) * 512 + q * 128:(ia % 2) * 512 + (q + 1) * 128],
                               kb_[:, h, q * 128:(q + 1) * 128], qT_all[:, h, j * 128:(j + 1) * 128], True, True,
                               [kk, "qT_all"], [lk])
                        eb = et[ia % 3]
                        ek = ("et", ia % 3)
                        ACT(eb[:, 0:n * 128], pl, AF.Exp, [lk], [ek])
                        pb_ = ptt[ia % 3]
                        pk = ("ptt", ia % 3)
                        TT(pb_[:, 0:n, :], eb[:, 0:n * 128].rearrange("p (n t) -> p n t", n=n), selT[:, r, s0:s0 + n, :],
                           ALU.mult, [ek, ("selT", j % 2, r)], [pk])
                        for q in range(n):
                            jl = s0 + q
                            kbw = None
                            if jl == j:
                                kbw = 1 + r
                            elif jl == j - 1 and r == 7:
                                kbw = 0
                            if kbw is not None:
                                TT(pb_[:, q, :], pb_[:, q, :], EB[:, kbw, h, :], ALU.mult, [pk, "EB"], [pk])
                        po_t = PA if h < 4 else PB
                        ok_ = ("PO", h // 4)
                        for q in range(n):
                            MM(po_t[:, (h % 4) * 256:(h % 4) * 256 + 129], pb_[:, q, :], vb_[:, q, h * 129:(h + 1) * 129],
                               q == 0, q == n - 1, [pk, vk], [ok_, (id(po_t), 0), (id(po_t), 1)])
                        yield
                    for g2, po_t in enumerate((PA, PB)):
                        pv = po_t[:].rearrange("p (h c) -> p h c", h=4)[:, :, 0:129]
                        ov = oacc[:, g2 * 4:(g2 + 1) * 4, :]
                        if pi == 0:
                            CP("dve", ov, pv, [("PO", g2), (id(po_t), 0), (id(po_t), 1)], [("oacc", g2)])
                        else:
                            TT(ov, pv, ov, ALU.add, [("PO", g2), (id(po_t), 0), (id(po_t), 1), ("oacc", g2)], [("oacc", g2)])
                        yield
                for h in range(8):
                    ok_ = ("oacc", h // 4)
                    P.op("dve", lambda e, o=rd[:, h:h + 1], i_=oacc[:, h, 128:129]: e.reciprocal(out=o, in_=i_), [ok_], [("rd", h)])
                    TS(att[:, h * 128:(h + 1) * 128], oacc[:, h, 0:128], rd[:, h:h + 1], ALU.mult, [ok_, ("rd", h)], ["att"])
                    yield
                if dbg and j == 1:
                    DMA("sp", "o", dbg_out["d_attn"], att[:], ["att"], [])
                if dbg and j == 0:
                    DMA("sp", "o", dbg_out["d_attn0"], att[:], ["att"], [])
                ACT(bg[:], bg[:], AF.Silu, [("bg", j % 2)], [("bg", j % 2)])
                TT(ybb[:], att[:], bg[:], ALU.mult, ["att", ("bg", j % 2)], ["ybb"])
                yield
                for h in range(8):
                    TR(PT0[:, h * 128:(h + 1) * 128], ybb[:, h * 128:(h + 1) * 128], ["ybb"], ["PT0"])
                CP("act", ybT[:], PT0[:].rearrange("p (h t) -> p h t", h=8), ["PT0"], ["ybT"])
                DMA("sp", "st", ybT_scr.rearrange("p (k t) -> p k t", k=8)[:, :, j * 128:(j + 1) * 128], ybT[:],
                    ["ybT"], ["ybT_scr"])
                yield

            def run(g):
                for _ in g:
                    pass

            def interleave(ga, gb, ra, rb_):
                da = db = False
                while not (da and db):
                    for _ in range(ra):
                        if not da:
                            try:
                                next(ga)
                            except StopIteration:
                                da = True
                    for _ in range(rb_):
                        if not db:
                            try:
                                next(gb)
                            except StopIteration:
                                db = True

            run(gen_indexer(0))
            for j in range(NB):
                if j >= 1:
                    interleave(gen_bisect(j), gen_attention(j - 1), 1, 1)
                else:
                    run(gen_bisect(j))
                run(gen_transposes(j))
                if j + 1 < NB:
                    run(gen_indexer(j + 1))
            run(gen_attention(NB - 1))
            P.barrier()

        with ExitStack() as ph7:
            mT_all = sb(ph7, "mT_all", [128, 16, 1024], BF16)
            with ExitStack() as ph:
                wpa = sb(ph, "wpa", [128, 8, DM], BF16)
                wpb = sb(ph, "wpb", [128, 8, DM], BF16)
                yaT7 = [sb(ph, "yaT7_%d" % i, [128, 8, 128], BF16) for i in range(2)]
                ybT7 = [sb(ph, "ybT7_%d" % i, [128, 8, 128], BF16) for i in range(2)]
                mg = sb(ph, "mg", [128, 4096], F32)
                mrg = sb(ph, "mrg", [128, DM], F32)
                tmp7 = sb(ph, "tmp7", [128, DM], F32)
                mrb = sb(ph, "mrb", [128, DM], BF16)
                DMA("pool", "cp", wpa[:], w_pa.rearrange("(k p) c -> p k c", p=128), [], ["wpa"])
                DMA("pool", "cp", wpb[:], w_pb.rearrange("(k p) c -> p k c", p=128), [], ["wpb"])
                P.alias_last(["wpa", "wpb"])
                for j in range(NB):
                    b = j % 2
                    DMA("sp", "ld", yaT7[b][:], yaT_scr.rearrange("p (k t) -> p k t", k=8)[:, :, j * 128:(j + 1) * 128],
                        ["yaT_scr"], [("yaT7", b)])
                    DMA("sp", "ld", ybT7[b][:], ybT_scr.rearrange("p (k t) -> p k t", k=8)[:, :, j * 128:(j + 1) * 128],
                        ["ybT_scr"], [("ybT7", b)])
                    DMA("sp", "ld", mg[:], proj[j, :, 6032:10128], PROJ_ALL(j), ["mg"])
                    ACT(mg[:], mg[:], AF.Sigmoid, ["mg"], ["mg"])
                    for (wt, yT, wkey, ykey, goff, first) in [(wpa, yaT7[b], "wpa", ("yaT7", b), 0, True),
                                                              (wpb, ybT7[b], "wpb", ("ybT7", b), 2048, False)]:
                        for q4 in range(4):
                            pst, hf = [(PA, 0), (PA, 1), (PB, 0), (PB, 1)][q4]
                            po = pst[:, hf * 512:(hf + 1) * 512]
                            for k in range(8):
                                MM(po, yT[:, k, :], wt[:, k, q4 * 512:(q4 + 1) * 512], k == 0, k == 7,
                                   [wkey, ykey], [(id(pst), hf)])
                            cs = slice(q4 * 512, (q4 + 1) * 512)
                            gs = slice(goff + q4 * 512, goff + (q4 + 1) * 512)
                            if first:
                                TT(mrg[:, cs], po, mg[:, gs], ALU.mult, [(id(pst), hf), "mg"], [("mrg", q4)])
                            else:
                                TT(tmp7[:, cs], po, mg[:, gs], ALU.mult, [(id(pst), hf), "mg"], [("tmp7", q4)])
                                TT(mrb[:, cs], mrg[:, cs], tmp7[:, cs], ALU.add, [("mrg", q4), ("tmp7", q4)], [("mrb", q4)])
                    for half in range(2):
                        pt = PT0 if half == 0 else PT1
                        pk = "PT%d" % half
                        for k in range(8):
                            kc = half * 8 + k
                            TR(pt[:, k * 128:(k + 1) * 128], mrb[:, kc * 128:(kc + 1) * 128], [("mrb", kc // 4)], [pk])
                        CP("act" if half == 0 else "dve", mT_all[:, half * 8:(half + 1) * 8, j * 128:(j + 1) * 128],
                           pt[:].rearrange("p (k t) -> p k t", k=8), [pk], [("mT", j)])
                P.barrier()
            with ExitStack() as ph:
                wo = sb(ph, "wo", [128, 16, DM], BF16)
                fn_bc = sb(ph, "fn_bc", [128, DM], F32)
                xr = [sb(ph, "xr%d" % i, [128, DM], F32) for i in range(2)]
                tmp8 = sb(ph, "tmp8", [128, DM], F32)
                jk8 = sb(ph, "jk8", [128, DM], F32)
                s7 = sb(ph, "s7", [128, 8], F32)
                for k4 in range(4):
                    DMA("pool", "cp", wo[:, k4 * 4:(k4 + 1) * 4, :],
                        w_out[k4 * 512:(k4 + 1) * 512, :].rearrange("(k p) c -> p k c", p=128), [], ["wo"])
                DMA("sp", "c", fn_bc[:], fnw.partition_broadcast(128), [], ["fn_bc"])
                for j in range(NB):
                    b = j % 2
                    DMA("sp", "x", xr[b][:], x_in[j], [], [("xr", b)])
                    for q4 in range(4):
                        pst, hf = [(PC, 0), (PC, 1), (PA, 0), (PA, 1)][q4]
                        po = pst[:, hf * 512:(hf + 1) * 512]
                        for k in range(16):
                            MM(po, mT_all[:, k, j * 128:(j + 1) * 128], wo[:, k, q4 * 512:(q4 + 1) * 512], k == 0, k == 15,
                               [("mT", j), "wo"], [(id(pst), hf)])
                        cs = slice(q4 * 512, (q4 + 1) * 512)
                        TT(tmp8[:, cs], po, xr[b][:, cs], ALU.add, [(id(pst), hf), ("xr", b)], [("tmp8", q4)])
                    T7 = [("tmp8", q) for q in range(4)]
                    ACT(jk8[:], tmp8[:], AF.Square, T7, ["jk8", "s7a"], accum_out=s7[:, 0:1])
                    rstd_from_ss(s7[:, 2:3], s7[:, 0:1], s7[:, 1:2], DM, ["s7a"], ["s7c"], "s7b")
                    STT(xr[b][:], tmp8[:], s7[:, 2:3], fn_bc[:], ALU.mult, ALU.mult, T7 + ["s7c", "fn_bc", ("xr", b)], [("xr", b)])
                    DMA("sp", "o", y_out[j], xr[b][:], [("xr", b)], [("y", j)])
                P.barrier()

        P.finalize()
        for en, meth in [("pe", block.tensor), ("act", block.scalar), ("dve", block.vector),
                         ("pool", block.gpsimd), ("sp", block.sync)]:
            meth(lambda e, en=en: P.emit(en, e))
    return nc, P


def _consts(c):
    ident = np.eye(128, dtype=np.float32)
    s = np.arange(128)[:, None]
    t = np.arange(128)[None, :]
    LT = ((s <= t).astype(np.float32) - (s <= 63).astype(np.float32))
    ind2 = np.stack([(np.arange(128) <= 63).astype(np.float32), np.ones(128, np.float32)], axis=1)
    cmask = (s <= t).astype(np.float32)
    J = np.zeros((128, 128), np.float32)
    J[np.arange(128), 127 - np.arange(128)] = 1.0
    i = np.arange(128)[:, None, None]
    r = np.arange(8)[None, :, None]
    ip = np.arange(128)[None, None, :]
    vis = (2 * r + ip // 64) <= (2 * c + i // 64)
    mbias = np.where(vis, 0.0, NEG).astype(np.float32).reshape(128, 1024)
    v = np.arange(1280)
    bk = _t5_bucket_np(v - 255 - 128 * c)
    OH = np.zeros((32, 1280), np.float32)
    OH[bk, v] = 1.0
    OH[15, :] -= 1.0
    sel = np.zeros((128, 64), np.float32)
    sel[:, np.arange(64) % 8 == c] = 1.0
    return dict(c_ident=ident, c_LT=LT, c_ind2=ind2, c_cmask=cmask, c_J=J, c_mbias=mbias, c_OH=OH, c_sel=sel)


_DBG = False


def kernel(x, norm_w, w_in, lb_table, gnorm_a, q_norm_w, kv_norm_w, w_uq, w_qidx, w_ukv,
           kidx_norm_w, kidx_norm_b, w_pa, w_pb, w_out, rel_bias, final_norm_w):
    f = lambda a: np.ascontiguousarray(np.asarray(a, dtype=np.float32))
    x = f(x)
    xb = x.reshape(8, 8, 128, DM)
    shared = dict(
        w_in=f(w_in)[0], normw=f(norm_w).reshape(1, DM), lbt=f(lb_table), gnorm=f(gnorm_a).reshape(1, 128),
        qnw=f(q_norm_w).reshape(1, 512), kvnw=f(kv_norm_w).reshape(1, 256), w_uq=f(w_uq)[0], w_qidx=f(w_qidx)[0],
        w_ukv=f(w_ukv)[0], kiw=f(kidx_norm_w).reshape(1, 128), kib=f(kidx_norm_b).reshape(1, 128),
        w_pa=f(w_pa)[0], w_pb=f(w_pb)[0], w_out=f(w_out)[0], relb=f(rel_bias), fnw=f(final_norm_w).reshape(1, DM))
    in_maps = []
    for c in range(NCORE):
        m = dict(shared)
        m["x"] = np.ascontiguousarray(xb[:, c])
        m.update(_consts(c))
        in_maps.append(m)
    nc, _ = build_nc(_DBG)
    res = run_bass_kernel_spmd(nc, in_maps, core_ids=list(range(NCORE)))
    out = np.zeros((8, 8, 128, DM), np.float32)
    for c in range(NCORE):
        out[:, c] = res.results[c]["y"]
    if _DBG:
        kernel.dbg = res.results
    return out.reshape(1, 8192, DM)
```

```python
import math
from contextlib import ExitStack
import numpy as np
import ml_dtypes
import jax
import jax.numpy as jnp
import concourse.bass as bass
import concourse.mybir as mybir
from concourse.bass_utils import run_bass_kernel_spmd

F32 = mybir.dt.float32
BF16 = mybir.dt.bfloat16
ALU = mybir.AluOpType
AF = mybir.ActivationFunctionType
AX = mybir.AxisListType

NCORE = 8
NB = 8
T = 128
DM = 2048
INW = 10128
EPS = 1e-6
SCALE_A = 128 ** -0.5
SCALE_B = 128 ** -0.5
WIDX_C = 16 ** -0.5 * 128 ** -0.5
NEG = -1.0e30
NBIS = 18
TOPK = 256.0


class Prog:
    def __init__(self, engines):
        self.sem = {}
        self.nins = {n: 0 for n in engines}
        self.known = {n: {} for n in engines}
        self.lastw = {}
        self.reads = {}
        self.dsem = {}
        self.dcnt = {}
        self.dnext = {}
        self.acts = {n: [] for n in engines}
        self.needed = {n: set() for n in engines}
        self.n_inst = 0

    def finalize(self):
        self.val = {}
        for en, need in self.needed.items():
            v = 0
            for idx in sorted(need):
                v += 1
                self.val[(en, idx)] = v

    def emit(self, en, e):
        for a in self.acts[en]:
            if a[0] == "w":
                ev = a[1]
                if ev[0] == "e":
                    e.wait_ge(self.sem[ev[1]], self.val[(ev[1], ev[2])])
                else:
                    e.wait_ge(ev[1], ev[2])
            elif a[0] == "c":
                a[1](e).then_inc(a[2])
            elif a[0] == "d":
                a[1](e).then_inc(a[2], 16)
            else:
                inst = a[1](e)
                if a[2] in self.needed[en]:
                    inst.then_inc(self.sem[en], 1)

    def add_engine_sem(self, name, sem):
        self.sem[name] = sem

    def add_dma_sems(self, name, sems):
        self.dsem[name] = list(sems)
        self.dcnt[name] = [0] * len(sems)
        self.dnext[name] = 0

    def _deps(self, reads, writes):
        deps = []
        for k in list(reads) + list(writes):
            ev = self.lastw.get(k)
            if ev is not None:
                deps.append(ev)
        for k in writes:
            deps.extend(self.reads.get(k, []))
        return deps

    @staticmethod
    def _kv(ev):
        if ev[0] == "e":
            return ("e", ev[1]), ev[2] + 1
        return ("s", id(ev[1])), ev[2]

    def _wait(self, en, deps):
        best = {}
        for ev in deps:
            if ev[0] == "e" and ev[1] == en and en == "pe":
                continue
            k, v = self._kv(ev)
            if v > best.get(k, (0, None))[0]:
                best[k] = (v, ev)
        for k, (v, ev) in best.items():
            if self.known[en].get(k, 0) >= v:
                continue
            self.acts[en].append(("w", ev))
            if ev[0] == "e":
                self.needed[ev[1]].add(ev[2])
            self.known[en][k] = v

    def _commit(self, ev, reads, writes):
        for k in writes:
            self.lastw[k] = ev
            self.reads[k] = []
        for k in reads:
            if k in writes:
                continue
            lst = self.reads.setdefault(k, [])
            lst.append(ev)
            if len(lst) > 48:
                best = {}
                for e2 in lst:
                    kk, v = self._kv(e2)
                    if v > best.get(kk, (0, None))[0]:
                        best[kk] = (v, e2)
                self.reads[k] = [x[1] for x in best.values()]

    def op(self, en, fn, reads=(), writes=()):
        self._wait(en, self._deps(reads, writes))
        idx = self.nins[en]
        self.nins[en] += 1
        self.acts[en].append(("i", fn, idx))
        ev = ("e", en, idx)
        self._commit(ev, reads, writes)
        self.n_inst += 1
        return ev

    def dma(self, en, stream, out, in_, reads=(), writes=()):
        i = self.dnext[stream]
        self.dnext[stream] = (i + 1) % len(self.dsem[stream])
        sem = self.dsem[stream][i]
        deps = self._deps(reads, writes)
        if self.dcnt[stream][i] > 0 and en != "pool":
            deps.append(("d", sem, self.dcnt[stream][i]))
        self._wait(en, deps)
        self.dcnt[stream][i] += 16
        self.acts[en].append(("d", (lambda e, out=out, in_=in_: e.dma_start(out=out, in_=in_)), sem))
        ev = ("d", sem, self.dcnt[stream][i])
        self._commit(ev, reads, writes)
        self.n_inst += 1
        return ev

    def alias_last(self, keys):
        evs = [self.lastw[k] for k in keys]
        newest = max(evs, key=lambda ev: self._kv(ev)[1])
        for k in keys:
            self.lastw[k] = newest

    def collective(self, fn, sem, reads=(), writes=()):
        en = "pool"
        self._wait(en, self._deps(reads, writes))
        self.acts[en].append(("c", fn, sem))
        ev = ("d", sem, 1)
        self.acts[en].append(("w", ev))
        self.known[en][("s", id(sem))] = 1
        self._commit(ev, reads, writes)

    def wait_all(self, en):
        deps = []
        for n in self.nins:
            if self.nins[n] > 0:
                deps.append(("e", n, self.nins[n] - 1))
        for n, ss in self.dsem.items():
            for s_, c in zip(ss, self.dcnt[n]):
                if c > 0:
                    deps.append(("d", s_, c))
        self._wait(en, deps)

    def barrier(self):
        for en in self.acts:
            self.wait_all(en)


def _t5_bucket_np(rel):
    rel = jnp.asarray(rel, jnp.int32)
    half = 16
    max_exact = 8
    base = jnp.where(rel > 0, half, 0)
    n = jnp.abs(rel)
    large = max_exact + (jnp.log(jnp.maximum(n, 1).astype(jnp.float32) / max_exact)
                         / math.log(128 / max_exact) * (half - max_exact)).astype(jnp.int32)
    large = jnp.minimum(large, half - 1)
    return np.asarray(base + jnp.where(n < max_exact, n, large))


def build_nc(dbg=False):
    nc = bass.Bass("TRN2", target_bir_lowering=False)

    def din(name, shape, dt=F32):
        return nc.dram_tensor(name, shape, dt, kind="ExternalInput").ap()

    x_in = din("x", [NB, T, DM])
    w_in = din("w_in", [DM, INW])
    normw = din("normw", [1, DM])
    lbt = din("lbt", [2, 1024])
    gnorm = din("gnorm", [1, 128])
    qnw = din("qnw", [1, 512])
    kvnw = din("kvnw", [1, 256])
    w_uq = din("w_uq", [512, 1024])
    w_qidx = din("w_qidx", [512, 2048])
    w_ukv = din("w_ukv", [256, 2048])
    kiw = din("kiw", [1, 128])
    kib = din("kib", [1, 128])
    w_pa = din("w_pa", [1024, DM])
    w_pb = din("w_pb", [1024, DM])
    w_out = din("w_out", [DM, DM])
    relb = din("relb", [32, 8])
    fnw = din("fnw", [1, DM])
    c_ident = din("c_ident", [128, 128])
    c_LT = din("c_LT", [128, 128])
    c_ind2 = din("c_ind2", [128, 2])
    c_cmask = din("c_cmask", [128, 128])
    c_J = din("c_J", [128, 128])
    c_mbias = din("c_mbias", [128, 1024])
    c_OH = din("c_OH", [32, 1280])
    c_sel = din("c_sel", [128, 64])
    y_out = nc.dram_tensor("y", [NB, T, DM], F32, kind="ExternalOutput").ap()

    proj = nc.dram_tensor("proj", [NB, T, INW], F32).ap()
    oint = nc.dram_tensor("oint", [NB, T, 1024], F32).ap()
    sendA = nc.dram_tensor("sendA", [128, 9216], BF16)
    gathA = nc.dram_tensor("gathA", [NCORE * 128, 9216], BF16)
    sendB = nc.dram_tensor("sendB", [1024, 2064], BF16)
    gathB = nc.dram_tensor("gathB", [NCORE * 1024, 2064], BF16)
    bvec = nc.dram_tensor("bvec", [8, 1280], F32)
    fl_in = nc.dram_tensor("fl_in", [128, 64], BF16)
    fl_out = nc.dram_tensor("fl_out", [NCORE * 128, 64], BF16)
    fl_out2 = nc.dram_tensor("fl_out2", [NCORE * 128, 64], BF16)
    qiT_scr = nc.dram_tensor("qiT_scr", [NB, 128, 2048], BF16).ap()
    yaT_scr = nc.dram_tensor("yaT_scr", [128, 8192], BF16).ap()
    ybT_scr = nc.dram_tensor("ybT_scr", [128, 8192], BF16).ap()
    dbg_out = {}
    if dbg:
        for nm, shp in [("d_proj0", [128, 512]), ("d_ya", [128, 1024]), ("d_sel", [128, 1024]),
                        ("d_attn", [128, 1024]), ("d_acc", [128, 1024]), ("d_S", [128, 1024]),
                        ("d_S0", [128, 1024]), ("d_ya0", [128, 1024]), ("d_attn0", [128, 1024]), ("d_sel0", [128, 1024]),
                        ("d_oi0", [128, 1024]), ("d_den0", [128, 8])]:
            dbg_out[nm] = nc.dram_tensor(nm, shp, F32, kind="ExternalOutput").ap()

    ENG = ["pe", "act", "dve", "pool", "sp"]
    P = Prog(ENG)

    with ExitStack() as es:
        sems = {n: es.enter_context(nc.semaphore("s_" + n)) for n in ENG}
        for n in ENG:
            P.add_engine_sem(n, sems[n])
        for nm, k in [("c", 1), ("cp", 1), ("x", 2), ("w", 2), ("st", 3), ("ld", 3), ("g", 4), ("kv", 4), ("o", 1)]:
            P.add_dma_sems(nm, [es.enter_context(nc.semaphore("d_%s%d" % (nm, i))) for i in range(k)])
        cc1 = es.enter_context(nc.semaphore("cc1"))
        cc2 = es.enter_context(nc.semaphore("cc2"))
        cc3 = es.enter_context(nc.semaphore("cc3"))
        cc4 = es.enter_context(nc.semaphore("cc4"))
        PA = es.enter_context(nc.psum_tensor("PA", [128, 1024], F32))
        PB = es.enter_context(nc.psum_tensor("PB", [128, 1024], F32))
        PC = es.enter_context(nc.psum_tensor("PC", [128, 1024], F32))
        PT0 = es.enter_context(nc.psum_tensor("PT0", [128, 1024], BF16))
        PT1 = es.enter_context(nc.psum_tensor("PT1", [128, 1024], BF16))
        block = es.enter_context(nc.Block())

        def MM(out, lhsT, rhs, st, sp, r, w):
            P.op("pe", lambda e: e.matmul(out, lhsT=lhsT, rhs=rhs, start=st, stop=sp), r, w)

        def ACT(out, in_, func, r, w, **kw):
            P.op("act", lambda e: e.activation(out=out, in_=in_, func=func, **kw), r, w)

        def TT(out, a, b, op, r, w, en="dve"):
            P.op(en, lambda e: e.tensor_tensor(out=out, in0=a, in1=b, op=op), r, w)

        def TS(out, a, s1, op0, r, w, s2=None, op1=None, en="dve", acc=None):
            if op1 is None:
                P.op(en, lambda e: e.tensor_scalar(out=out, in0=a, scalar1=s1, scalar2=None, op0=op0), r, w)
            elif acc is None:
                P.op(en, lambda e: e.tensor_scalar(out=out, in0=a, scalar1=s1, scalar2=s2, op0=op0, op1=op1), r, w)
            else:
                P.op(en, lambda e: e.tensor_scalar(out=out, in0=a, scalar1=s1, scalar2=s2, op0=op0, op1=op1, accum_out=acc), r, w)

        def STT(out, a, s, b, op0, op1, r, w):
            P.op("dve", lambda e: e.scalar_tensor_tensor(out=out, in0=a, scalar=s, in1=b, op0=op0, op1=op1), r, w)

        def CP(en, out, in_, r, w):
            if en == "act":
                P.op("act", lambda e: e.copy(out=out, in_=in_), r, w)
            else:
                P.op(en, lambda e: e.tensor_copy(out=out, in_=in_), r, w)

        def RED(out, in_, op, r, w, ax=AX.X):
            P.op("dve", lambda e: e.tensor_reduce(out=out, in_=in_, axis=ax, op=op), r, w)

        def MEMSET(en, ap, v, w):
            P.op(en, lambda e: e.memset(ap, v), [], w)

        def DMA(en, stream, out, in_, r, w):
            P.dma(en, stream, out, in_, r, w)

        def sb(stack, name, shape, dt):
            return stack.enter_context(nc.sbuf_tensor(name, shape, dt))

        def rstd_from_ss(out, ss, tmp, n, r, w, tk):
            ACT(tmp, ss, AF.Ln, r, [tk], scale=1.0 / n, bias=EPS)
            ACT(out, tmp, AF.Exp, [tk], w, scale=-0.5)

        identf = sb(es, "identf", [128, 128], F32)
        identb = sb(es, "identb", [128, 128], BF16)
        DMA("sp", "c", identf[:], c_ident, [], ["identf"])
        CP("dve", identb[:], identf[:], ["identf"], ["identb"])

        qT_all = sb(es, "qT_all", [128, 8, 1024], BF16)
        wabs = sb(es, "wabs", [128, NB, 16], F32)
        wsgn = sb(es, "wsgn", [128, NB, 16], F32)
        EB = sb(es, "EB", [128, 9, 8, 128], BF16)

        def TR(out, in_, r, w):
            P.op("pe", lambda e: e.transpose(out=out, in_=in_, identity=identb[:]), list(r) + ["identb"], w)

        with ExitStack() as ph:
            hT = sb(ph, "hT", [128, 16, 1024], BF16)
            normw_bc = sb(ph, "normw_bc", [128, DM], F32)
            xt = [sb(ph, "xt%d" % i, [128, DM], F32) for i in range(2)]
            hb = [sb(ph, "hb%d" % i, [128, DM], BF16) for i in range(2)]
            junk = sb(ph, "junk1", [128, DM], F32)
            st1 = sb(ph, "st1", [128, 8], F32)
            DMA("sp", "c", normw_bc[:], normw.partition_broadcast(128), [], ["normw_bc"])
            for j in range(NB):
                b = j % 2
                DMA("sp", "x", xt[b][:], x_in[j], [], [("xt", b)])
                ACT(junk[:], xt[b][:], AF.Square, [("xt", b)], ["junk1", "st1a"], accum_out=st1[:, 0:1])
                rstd_from_ss(st1[:, 2:3], st1[:, 0:1], st1[:, 1:2], DM, ["st1a"], ["st1c"], "st1b")
                STT(hb[b][:], xt[b][:], st1[:, 2:3], normw_bc[:], ALU.mult, ALU.mult,
                    [("xt", b), "st1c", "normw_bc"], [("hb", b)])
                for half in range(2):
                    pt = PT0 if half == 0 else PT1
                    pk = "PT%d" % half
                    for k in range(8):
                        kc = half * 8 + k
                        TR(pt[:, k * 128:(k + 1) * 128], hb[b][:, kc * 128:(kc + 1) * 128], [("hb", b)], [pk])
                    CP("act" if half == 0 else "dve",
                       hT[:, half * 8:(half + 1) * 8, j * 128:(j + 1) * 128],
                       pt[:].rearrange("p (k t) -> p k t", k=8), [pk], [("hT", j)])
            wb = [sb(ph, "wb%d" % i, [128, 16, 512], BF16) for i in range(2)]
            stg = [sb(ph, "stg%d" % i, [128, 512], F32) for i in range(4)]
            banks = [(PA, 0), (PA, 1), (PB, 0), (PB, 1), (PC, 0), (PC, 1)]
            nct = (INW + 511) // 512
            it = 0
            for ct in range(nct):
                c0 = ct * 512
                cw = min(512, INW - c0)
                wbuf = wb[ct % 2]
                wk = ("wb", ct % 2)
                DMA("pool", "w", wbuf[:, :, 0:cw], w_in[:, c0:c0 + cw].rearrange("(k p) c -> p k c", p=128), [], [wk])
                for j in range(NB):
                    pst, hf = banks[it % 6]
                    pk = (id(pst), hf)
                    po = pst[:, hf * 512: hf * 512 + cw]
                    for k in range(16):
                        MM(po, hT[:, k, j * 128:(j + 1) * 128], wbuf[:, k, 0:cw], k == 0, k == 15,
                           [("hT", j), wk], [pk])
                    sg = stg[it % 4]
                    sk = ("stg", it % 4)
                    CP("act" if it % 2 == 0 else "dve", sg[:, 0:cw], po, [pk], [sk])
                    DMA("sp", "st", proj[j, :, c0:c0 + cw], sg[:, 0:cw], [sk], [("proj", j, ct)])
                    if dbg and ct == 0 and j == 0:
                        DMA("sp", "o", dbg_out["d_proj0"], sg[:, 0:512], [sk], [])
                    it += 1
            P.barrier()
        PROJ_ALL = lambda j: [("proj", j, ct) for ct in range(nct)]

        with ExitStack() as ph3:
            qeT_all = sb(ph3, "qeT_all", [128, NB, 8, 128], BF16)
            emid_all = sb(ph3, "emid_all", [128, NB, 8], F32)
            with ExitStack() as ph:
                LT = sb(ph, "LT", [128, 128], F32)
                ind2 = sb(ph, "ind2", [128, 2], F32)
                cmask8 = sb(ph, "cmask8", [128, 8, 128], F32)
                lb_bc = sb(ph, "lb_bc", [128, 1024], F32)
                oml_bc = sb(ph, "oml_bc", [128, 1024], F32)
                lt1 = sb(ph, "lt1", [128, 1024], F32)
                kvn_bc = sb(ph, "kvn_bc", [128, 256], F32)
                kiw_bc = sb(ph, "kiw_bc", [128, 128], F32)
                kib_bc = sb(ph, "kib_bc", [128, 128], F32)
                wukv = sb(ph, "wukv", [128, 2, 2048], BF16)
                DMA("sp", "c", LT[:], c_LT, [], ["LT"])
                DMA("sp", "c", ind2[:], c_ind2, [], ["ind2"])
                for h in range(8):
                    DMA("sp", "c", cmask8[:, h, :], c_cmask, [], ["cmask8"])
                DMA("sp", "c", lb_bc[:], lbt[0:1, :].partition_broadcast(128), [], ["lb_bc"])
                DMA("sp", "c", lt1[:], lbt[1:2, :].partition_broadcast(128), [], ["lt1"])
                DMA("sp", "c", kvn_bc[:], kvnw.partition_broadcast(128), [], ["kvn_bc"])
                DMA("sp", "c", kiw_bc[:], kiw.partition_broadcast(128), [], ["kiw_bc"])
                DMA("sp", "c", kib_bc[:], kib.partition_broadcast(128), [], ["kib_bc"])
                DMA("pool", "cp", wukv[:], w_ukv.rearrange("(k p) c -> p k c", p=128), [], ["wukv"])
                TT(lt1[:], lb_bc[:], lt1[:], ALU.subtract, ["lb_bc", "lt1"], ["lt1"])
                ACT(lb_bc[:], lt1[:], AF.Sigmoid, ["lt1"], ["lb_bc"])
                ACT(oml_bc[:], lt1[:], AF.Sigmoid, ["lt1"], ["oml_bc"], scale=-1.0)

                a3 = [sb(ph, "a3_%d" % i, [128, 3072], F32) for i in range(2)]
                ck = [sb(ph, "ck_%d" % i, [128, 400], F32) for i in range(2)]
                t1 = sb(ph, "t1", [128, 1024], F32)
                gt = sb(ph, "gt", [128, 1024], F32)
                e1 = sb(ph, "e1", [128, 1024], F32)
                e2 = sb(ph, "e2", [128, 1024], F32)
                t2 = sb(ph, "t2", [128, 1024], F32)
                qe_b = sb(ph, "qe_b", [128, 1024], BF16)
                ke_b = sb(ph, "ke_b", [128, 1024], BF16)
                v_b = sb(ph, "v_b", [128, 1024], BF16)
                keT = sb(ph, "keT", [128, 8, 128], BF16)
                AT_b = sb(ph, "AT_b", [128, 8, 128], BF16)
                oi = sb(ph, "oi", [128, 1024], F32)
                sm = sb(ph, "sm", [128, 64], F32)
                sendb = sb(ph, "sendb", [128, 2064], BF16)
                sendk = sb(ph, "sendk", [128, 8, 128], BF16)
                sendi = sb(ph, "sendi", [128, 128], BF16)
                cn = sb(ph, "cn", [128, 256], BF16)
                ckvT = sb(ph, "ckvT", [128, 2, 128], BF16)
                kn = sb(ph, "kn", [128, 128], F32)
                knb = sb(ph, "knb", [128, 128], BF16)
                junk3 = sb(ph, "junk3", [128, 256], F32)
                MEMSET("pool", sendb[:, 1032:2064], 1.0, ["sendV"])
                for j in range(NB):
                    b = j % 2
                    DMA("sp", "ld", a3[b][:], proj[j, :, 0:3072], PROJ_ALL(j), [("a3", b)])
                    DMA("sp", "ld", ck[b][:], proj[j, :, 4608:5008], PROJ_ALL(j), [("ck", b)])
                    aq = a3[b][:, 0:1024]
                    af = a3[b][:, 1024:2048]
                    ai = a3[b][:, 2048:3072]
                    ACT(t1[:], af, AF.Sigmoid, [("a3", b)], ["t1"])
                    TT(t1[:], t1[:], oml_bc[:], ALU.mult, ["t1", "oml_bc"], ["t1"])
                    TT(t1[:], t1[:], lb_bc[:], ALU.add, ["t1", "lb_bc"], ["t1"])
                    ACT(gt[:], t1[:], AF.Ln, ["t1"], ["gt"])
                    TS(t1[:], t1[:], -1.0, ALU.mult, ["t1"], ["t1"], s2=1.0, op1=ALU.add)
                    for hf in range(2):
                        MM(PA[:, hf * 512:(hf + 1) * 512], LT[:], gt[:, hf * 512:(hf + 1) * 512], True, True,
                           ["LT", "gt"], [(id(PA), hf)])
                    for h in range(8):
                        MM(PB[:, 2 * h:2 * h + 2], gt[:, h * 128:(h + 1) * 128], ind2[:], True, True,
                           ["gt", "ind2"], [(id(PB), 0)])
                    CP("dve", sm[:, 0:16], PB[:, 0:16], [(id(PB), 0)], ["sm"])
                    smv = sm[:, 0:16].rearrange("p (h two) -> p h two", two=2)
                    TT(sm[:, 16:24], smv[:, :, 1], smv[:, :, 0], ALU.subtract, ["sm"], ["sm2"])
                    ACT(sm[:, 24:32], sm[:, 16:24], AF.Exp, ["sm2"], ["elm"])
                    ACT(emid_all[:, j, :], smv[:, :, 0], AF.Exp, ["sm"], [("emid", j)])
                    ACT(sendb[:, 1024:1032], smv[:, :, 1], AF.Exp, ["sm"], ["sendD"])
                    for hf in range(2):
                        sl = slice(hf * 512, (hf + 1) * 512)
                        ACT(e1[:, sl], PA[:, sl], AF.Exp, [(id(PA), hf)], [("e1", hf)])
                        ACT(e2[:, sl], PA[:, sl], AF.Exp, [(id(PA), hf)], [("e2", hf)], scale=-1.0)
                    ACT(t2[:], aq, AF.Silu, [("a3", b)], ["t2"])
                    STT(qe_b[:], t2[:], SCALE_A, e1[:], ALU.mult, ALU.mult, ["t2", ("e1", 0), ("e1", 1)], ["qe_b"])
                    TT(ke_b[:], t1[:], e2[:], ALU.mult, ["t1", ("e2", 0), ("e2", 1)], ["ke_b"])
                    CP("pool", v_b[:], ai, [("a3", b)], ["v_b"])
                    for h in range(8):
                        TR(PT0[:, h * 128:(h + 1) * 128], qe_b[:, h * 128:(h + 1) * 128], ["qe_b"], ["PT0"])
                    for h in range(8):
                        TR(PT1[:, h * 128:(h + 1) * 128], ke_b[:, h * 128:(h + 1) * 128], ["ke_b"], ["PT1"])
                    CP("act", qeT_all[:, j, :, :], PT0[:].rearrange("p (h t) -> p h t", h=8), ["PT0"], [("qeT", j)])
                    CP("dve", keT[:], PT1[:].rearrange("p (h t) -> p h t", h=8), ["PT1"], ["keT"])
                    for h in range(8):
                        MM(PB[:, h * 128:(h + 1) * 128], keT[:, h, :], qeT_all[:, j, h, :], True, True,
                           ["keT", ("qeT", j)], [(id(PB), h // 4)])
                    for hf in range(2):
                        TT(AT_b[:, hf * 4:(hf + 1) * 4, :],
                           PB[:, hf * 512:(hf + 1) * 512].rearrange("p (h t) -> p h t", h=4),
                           cmask8[:, hf * 4:(hf + 1) * 4, :], ALU.mult, [(id(PB), hf), "cmask8"], [("AT", hf)])
                    for h in range(8):
                        MM(PA[:, h * 128:(h + 1) * 128], AT_b[:, h, :], v_b[:, h * 128:(h + 1) * 128], True, True,
                           [("AT", h // 4), "v_b"], [(id(PA), h // 4)])
                    for hf in range(2):
                        sl = slice(hf * 512, (hf + 1) * 512)
                        CP("act", oi[:, sl], PA[:, sl], [(id(PA), hf)], ["oi"])
                    DMA("sp", "st", oint[j], oi[:], ["oi"], [("oint", j)])
                    for h in range(8):
                        MM(PC[:, h * 128:(h + 1) * 128], ke_b[:, h * 128:(h + 1) * 128], v_b[:, h * 128:(h + 1) * 128],
                           True, True, ["ke_b", "v_b"], [(id(PC), h // 4)])
                    for h in range(8):
                        TS(sendb[:, h * 128:(h + 1) * 128], PC[:, h * 128:(h + 1) * 128], sm[:, 24 + h:25 + h], ALU.mult,
                           [(id(PC), h // 4), "elm"], ["sendS"])
                    ckv = ck[b][:, 0:256]
                    kraw = ck[b][:, 256:384]
                    ACT(junk3[:], ckv, AF.Square, [("ck", b)], ["junk3", "sm3a"], accum_out=sm[:, 32:33])
                    rstd_from_ss(sm[:, 34:35], sm[:, 32:33], sm[:, 33:34], 256, ["sm3a"], ["sm3c"], "sm3b")
                    STT(cn[:], ckv, sm[:, 34:35], kvn_bc[:], ALU.mult, ALU.mult, [("ck", b), "sm3c", "kvn_bc"], ["cn"])
                    for k in range(2):
                        TR(PT0[:, k * 128:(k + 1) * 128], cn[:, k * 128:(k + 1) * 128], ["cn"], ["PT0"])
                    CP("dve", ckvT[:], PT0[:, 0:256].rearrange("p (k t) -> p k t", k=2), ["PT0"], ["ckvT"])
                    wv = wukv[:].rearrange("p k (h c) -> p k h c", h=8)
                    for h in range(8):
                        for k in range(2):
                            MM(PB[:, h * 128:(h + 1) * 128], wv[:, k, h, 0:128], ckvT[:, k, :], k == 0, k == 1,
                               ["wukv", "ckvT"], [(id(PB), h // 4)])
                    for hf in range(2):
                        CP("act", sendk[:, hf * 4:(hf + 1) * 4, :],
                           PB[:, hf * 512:(hf + 1) * 512].rearrange("p (h t) -> p h t", h=4), [(id(PB), hf)], ["sendk"])
                    for hf in range(2):
                        for k in range(2):
                            MM(PC[:, hf * 512:(hf + 1) * 512], ckvT[:, k, :], wv[:, k, hf * 4:(hf + 1) * 4, 128:256],
                               k == 0, k == 1, ["wukv", "ckvT"], [(id(PC), hf)])
                    sv = sendb[:, 1032:2064].rearrange("p (h c) -> p h c", c=129)
                    for hf in range(2):
                        CP("dve", sv[:, hf * 4:(hf + 1) * 4, 0:128],
                           PC[:, hf * 512:(hf + 1) * 512].rearrange("p (h c) -> p h c", h=4), [(id(PC), hf)], ["sendV"])
                    RED(sm[:, 36:37], kraw, ALU.add, [("ck", b)], ["sm4a"])
                    TS(sm[:, 37:38], sm[:, 36:37], -1.0 / 128, ALU.mult, ["sm4a"], ["sm4b"])
                    TS(kn[:], kraw, sm[:, 37:38], ALU.add, [("ck", b), "sm4b"], ["kn"])
                    ACT(junk3[:, 0:128], kn[:], AF.Square, ["kn"], ["junk3", "sm4c"], accum_out=sm[:, 38:39])
                    rstd_from_ss(sm[:, 40:41], sm[:, 38:39], sm[:, 39:40], 128, ["sm4c"], ["sm4e"], "sm4d")
                    STT(kn[:], kn[:], sm[:, 40:41], kiw_bc[:], ALU.mult, ALU.mult, ["kn", "sm4e", "kiw_bc"], ["kn"])
                    TT(knb[:], kn[:], kib_bc[:], ALU.add, ["kn", "kib_bc"], ["knb"])
                    TR(PT1[:, 0:128], knb[:], ["knb"], ["PT1"])
                    CP("act", sendi[:], PT1[:, 0:128], ["PT1"], ["sendi"])
                    sA = sendA.ap()
                    DMA("sp", "st", sendB.ap()[j * 128:(j + 1) * 128, :], sendb[:], ["sendS", "sendD", "sendV"], ["sendB"])
                    DMA("sp", "st", sA[:, 0:8192].rearrange("p (h j t) -> p h j t", h=8, j=8)[:, :, j, :], sendk[:],
                        ["sendk"], ["sendA"])
                    DMA("sp", "st", sA[:, 8192 + j * 128: 8192 + (j + 1) * 128], sendi[:], ["sendi"], ["sendA"])
                P.barrier()
            P.collective(lambda e: e.collective_compute("AllGather", ALU.bypass, replica_groups=[list(range(NCORE))],
                                                        ins=[sendA.ap().opt()], outs=[gathA.ap().opt()]),
                         cc1, ["sendA"], ["gathA"])
            P.collective(lambda e: e.collective_compute("AllGather", ALU.bypass, replica_groups=[list(range(NCORE))],
                                                        ins=[sendB.ap().opt()], outs=[gathB.ap().opt()]),
                         cc2, ["sendB"], ["gathB"])
            gA = gathA.ap()
            gB = gathB.ap()
            with ExitStack() as pp:
                relb_s = sb(pp, "relb_s", [32, 8], F32)
                oh_s = sb(pp, "oh_s", [32, 1280], F32)
                bv_s = sb(pp, "bv_s", [8, 1280], F32)
                Jm = sb(pp, "Jm", [128, 128], F32)
                hk = [sb(pp, "hk%d" % i, [128, 128], F32) for i in range(4)]
                DMA("sp", "c", relb_s[:], relb, [], ["relb_s"])
                DMA("sp", "c", oh_s[:], c_OH, [], ["oh_s"])
                DMA("sp", "c", Jm[:], c_J, [], ["Jm"])
                for i, (v0, vw) in enumerate([(0, 512), (512, 512), (1024, 256)]):
                    MM(PA[0:8, 0:vw], relb_s[:], oh_s[:, v0:v0 + vw], True, True, ["relb_s", "oh_s"], [(id(PA), 0)])
                    CP("act", bv_s[:, v0:v0 + vw], PA[0:8, 0:vw], [(id(PA), 0)], ["bv_s"])
                DMA("sp", "st", bvec.ap(), bv_s[:], ["bv_s"], ["bvec"])
                it = 0
                for kb in range(9):
                    for h in range(8):
                        hb_ = hk[it % 4]
                        hkk = ("hk", it % 4)
                        DMA("sp", "ld", hb_[:], bass.AP(bvec, h * 1280 + kb * 128, [[1, 128], [1, 128]]), ["bvec"], [hkk])
                        pst, hf = [(PB, 0), (PB, 1), (PC, 0), (PC, 1)][it % 4]
                        MM(pst[:, hf * 512: hf * 512 + 128], hb_[:], Jm[:], True, True, [hkk, "Jm"], [(id(pst), hf)])
                        ACT(EB[:, kb, h, :], pst[:, hf * 512: hf * 512 + 128], AF.Exp, [(id(pst), hf)], ["EB"])
                        it += 1
                qn_bc = sb(pp, "qn_bc", [128, 512], F32)
                wuq = sb(pp, "wuq", [128, 4, 1024], BF16)
                wqi = sb(pp, "wqi", [128, 4, 2048], BF16)
                cqT = sb(pp, "cqT", [128, 4, 1024], BF16)
                cq = [sb(pp, "cq%d" % i, [128, 912], F32) for i in range(2)]
                cqn = sb(pp, "cqn", [128, 512], BF16)
                j6 = sb(pp, "j6", [128, 512], F32)
                s6 = sb(pp, "s6", [128, 8], F32)
                qiS = [sb(pp, "qiS%d" % i, [128, 1024], BF16) for i in range(2)]
                DMA("sp", "c", qn_bc[:], qnw.partition_broadcast(128), [], ["qn_bc"])
                DMA("pool", "cp", wuq[:], w_uq.rearrange("(k p) c -> p k c", p=128), [], ["wuq"])
                DMA("pool", "cp", wqi[:], w_qidx.rearrange("(k p) c -> p k c", p=128), [], ["wqi"])
                P.alias_last(["wuq", "wqi"])
                for j in range(NB):
                    b = j % 2
                    DMA("sp", "ld", cq[b][:], proj[j, :, 4096:5008], PROJ_ALL(j), [("cq", b)])
                    ACT(j6[:], cq[b][:, 0:512], AF.Square, [("cq", b)], ["j6", "s6a"], accum_out=s6[:, 0:1])
                    rstd_from_ss(s6[:, 2:3], s6[:, 0:1], s6[:, 1:2], 512, ["s6a"], ["s6c"], "s6b")
                    STT(cqn[:], cq[b][:, 0:512], s6[:, 2:3], qn_bc[:], ALU.mult, ALU.mult, [("cq", b), "s6c", "qn_bc"], ["cqn"])
                    for k in range(4):
                        TR(PT0[:, k * 128:(k + 1) * 128], cqn[:, k * 128:(k + 1) * 128], ["cqn"], ["PT0"])
                    CP("act", cqT[:, :, j * 128:(j + 1) * 128], PT0[:, 0:512].rearrange("p (k t) -> p k t", k=4), ["PT0"], ["cqT"])
                    ACT(wabs[:, j, :], cq[b][:, 896:912], AF.Abs, [("cq", b)], ["wabs"], scale=WIDX_C)
                    ACT(wsgn[:, j, :], cq[b][:, 896:912], AF.Sign, [("cq", b)], ["wsgn"])
                banks = [(PA, 0), (PA, 1), (PB, 0), (PB, 1), (PC, 0), (PC, 1)]
                it = 0
                for h in range(8):
                    for half in range(2):
                        pst, hf = banks[it % 6]
                        po = pst[:, hf * 512:(hf + 1) * 512]
                        for k in range(4):
                            MM(po, wuq[:, k, h * 128:(h + 1) * 128], cqT[:, k, half * 512:(half + 1) * 512], k == 0, k == 3,
                               ["wuq", "cqT"], [(id(pst), hf)])
                        ACT(qT_all[:, h, half * 512:(half + 1) * 512], po, AF.Copy, [(id(pst), hf)], ["qT_all"], scale=SCALE_B)
                        it += 1
                for h in range(16):
                    qb = qiS[h % 2]
                    qk = ("qiS", h % 2)
                    for half in range(2):
                        pst, hf = banks[it % 6]
                        po = pst[:, hf * 512:(hf + 1) * 512]
                        for k in range(4):
                            MM(po, wqi[:, k, h * 128:(h + 1) * 128], cqT[:, k, half * 512:(half + 1) * 512], k == 0, k == 3,
                               ["wqi", "cqT"], [(id(pst), hf)])
                        CP("act" if it % 2 == 0 else "dve", qb[:, half * 512:(half + 1) * 512], po, [(id(pst), hf)], [qk])
                        it += 1
                    DMA("sp", "st", qiT_scr.rearrange("j d (h t) -> d h j t", h=16)[:, h, :, :],
                        qb[:].rearrange("p (j t) -> p j t", j=8), [qk], ["qiT_scr"])
                P.barrier()


            with ExitStack() as ph:
                S = sb(ph, "S", [128, 1024], F32)
                save = sb(ph, "save", [128, NB, 1024], F32)
                selc = sb(ph, "selc", [128, 64], F32)
                slb = [sb(ph, "slb%d" % i, [128, 1032], BF16) for i in range(4)]
                dfl = [sb(ph, "dfl%d" % i, [128, 8], F32) for i in range(2)]
                DMA("sp", "c", selc[:], c_sel, [], ["selc"])
                MEMSET("dve", S[:], 0.0, ["S"])
                for bb in range(64):
                    r, jj = bb % 8, bb // 8
                    sl = slb[bb % 4]
                    slk = ("slb", bb % 4)
                    row0 = r * 1024 + jj * 128
                    DMA("pool", "g", sl[:], gB[row0:row0 + 128, 0:1032], ["gathB"], [slk])
                    if r == 0:
                        TS(save[:, jj, :], S[:], selc[:, bb:bb + 1], ALU.mult, ["S", "selc"], [("save", jj)])
                    else:
                        STT(save[:, jj, :], S[:], selc[:, bb:bb + 1], save[:, jj, :], ALU.mult, ALU.add,
                            ["S", "selc", ("save", jj)], [("save", jj)])
                    df = dfl[bb % 2]
                    CP("act", df[:], sl[:, 1024:1032], [slk], [("dfl", bb % 2)])
                    for h in range(8):
                        hs = slice(h * 128, (h + 1) * 128)
                        STT(S[:, hs], S[:, hs], df[:, h:h + 1], sl[:, hs], ALU.mult, ALU.add,
                            ["S", ("dfl", bb % 2), slk], ["S"])
                if dbg:
                    DMA("sp", "o", dbg_out["d_S"], save[:, 1, :], [("save", 1)], [])
                    DMA("sp", "o", dbg_out["d_S0"], save[:, 0, :], [("save", 0)], [])

                gn_bc = sb(ph, "gn_bc", [128, 128], F32)
                sinb = sb(ph, "sinb", [128, 1024], BF16)
                oi2 = [sb(ph, "oi2_%d" % i, [128, 1024], F32) for i in range(2)]
                ag2 = [sb(ph, "ag2_%d" % i, [128, 1024], F32) for i in range(2)]
                o5 = sb(ph, "o5", [128, 1024], F32)
                j5 = sb(ph, "j5", [128, 1024], F32)
                sg5 = sb(ph, "sg5", [128, 1024], F32)
                y5 = sb(ph, "y5", [128, 1024], F32)
                yab = sb(ph, "yab", [128, 1024], BF16)
                yaT = sb(ph, "yaT", [128, 8, 128], BF16)
                s5 = sb(ph, "s5", [128, 32], F32)
                DMA("sp", "c", gn_bc[:], gnorm.partition_broadcast(128), [], ["gn_bc"])
                for j in range(NB):
                    b = j % 2
                    DMA("sp", "ld", oi2[b][:], oint[j], [("oint", j)], [("oi2", b)])
                    DMA("sp", "ld", ag2[b][:], proj[j, :, 3072:4096], PROJ_ALL(j), [("ag2", b)])
                    for h in range(8):
                        hs = slice(h * 128, (h + 1) * 128)
                        TS(sinb[:, hs], save[:, j, hs], emid_all[:, j, h:h + 1], ALU.mult,
                           [("save", j), ("emid", j)], ["sinb"])
                    for h in range(8):
                        hs = slice(h * 128, (h + 1) * 128)
                        MM(PA[:, hs], qeT_all[:, j, h, :], sinb[:, hs], True, True, [("qeT", j), "sinb"], [(id(PA), h // 4)])
                    for hf in range(2):
                        sl_ = slice(hf * 512, (hf + 1) * 512)
                        TT(o5[:, sl_], PA[:, sl_], oi2[b][:, sl_], ALU.add, [(id(PA), hf), ("oi2", b)], ["o5"])
                    ACT(j5[:], o5[:], AF.Square, ["o5"], ["j5"])
                    RED(s5[:, 0:8], j5[:].rearrange("p (h v) -> p h v", h=8), ALU.add, ["j5"], ["s5a"])
                    rstd_from_ss(s5[:, 16:24], s5[:, 0:8], s5[:, 8:16], 128, ["s5a"], ["s5c"], "s5b")
                    ACT(sg5[:], ag2[b][:], AF.Silu, [("ag2", b)], ["sg5"])
                    for h in range(8):
                        hs = slice(h * 128, (h + 1) * 128)
                        STT(y5[:, hs], o5[:, hs], s5[:, 16 + h:17 + h], gn_bc[:], ALU.mult, ALU.mult,
                            ["o5", "s5c", "gn_bc"], ["y5"])
                    TT(yab[:], y5[:], sg5[:], ALU.mult, ["y5", "sg5"], ["yab"])
                    if dbg and j == 1:
                        DMA("sp", "o", dbg_out["d_ya"], y5[:], ["y5"], [])
                    if dbg and j == 0:
                        DMA("sp", "o", dbg_out["d_ya0"], y5[:], ["y5"], [])
                        DMA("sp", "o", dbg_out["d_oi0"], oi2[b][:], [("oi2", b)], [])
                    for h in range(8):
                        TR(PT0[:, h * 128:(h + 1) * 128], yab[:, h * 128:(h + 1) * 128], ["yab"], ["PT0"])
                    CP("act", yaT[:], PT0[:].rearrange("p (h t) -> p h t", h=8), ["PT0"], ["yaT"])
                    DMA("sp", "st", yaT_scr.rearrange("p (k t) -> p k t", k=8)[:, :, j * 128:(j + 1) * 128], yaT[:],
                        ["yaT"], ["yaT_scr"])
                P.barrier()

        with ExitStack() as ph:
            kidxT = sb(ph, "kidxT", [128, 8, 1024], BF16)
            mb = sb(ph, "mb", [128, 8, 128], F32)
            DMA("pool", "cp", kidxT[:], gA[:, 8192:9216].rearrange("(r p) t -> p r t", p=128), ["gathA"], ["kidxT"])
            DMA("sp", "c", mb[:], c_mbias.rearrange("p (r t) -> p r t", r=8), [], ["mb"])
            acc = sb(ph, "acc", [128, 8, 8, 128], F32)
            selb = sb(ph, "selb", [128, 8, 8, 128], BF16)
            selT2 = [sb(ph, "selT%d" % i, [128, 8, 8, 128], BF16) for i in range(2)]
            qiT = [sb(ph, "qiT%d" % i, [128, 16, 128], BF16) for i in range(2)]
            rbuf = [sb(ph, "rbuf%d" % i, [128, 512], BF16) for i in range(4)]
            bs2 = [sb(ph, "bs%d" % i, [128, 16], F32) for i in range(2)]
            halfs2 = [sb(ph, "halfs%d" % i, [128, NBIS + 2], F32) for i in range(2)]
            pw = sb(ph, "pw", [128, NBIS + 2], F32)
            KTp = [sb(ph, "KTp%d" % i, [128, 8, 512], BF16) for i in range(2)]
            Vp = [sb(ph, "Vp%d" % i, [128, 4, 1032], BF16) for i in range(2)]
            et = [sb(ph, "et%d" % i, [128, 512], BF16) for i in range(3)]
            ptt = [sb(ph, "ptt%d" % i, [128, 4, 128], BF16) for i in range(3)]
            att = sb(ph, "att", [128, 1024], F32)
            bg2 = [sb(ph, "bg%d" % i, [128, 1024], F32) for i in range(2)]
            ybb = sb(ph, "ybb", [128, 1024], BF16)
            ybT = sb(ph, "ybT", [128, 8, 128], BF16)
            rd = sb(ph, "rd", [128, 8], F32)
            oacc = sb(ph, "oacc", [128, 8, 129], F32)
            for i in range(NBIS + 2):
                MEMSET("pool", pw[:, i:i + 1], 2.0 ** (-(i + 1)), ["pw"])
            banks_i = [(PA, 0), (PA, 1), (PB, 0), (PB, 1)]

            def key_tiles(nb_):
                tl = []
                for r in range(8):
                    s0 = 0
                    while s0 < nb_:
                        n = min(4, nb_ - s0)
                        tl.append((r, s0, n))
                        s0 += n
                return tl

            cnt_i = [0]

            def gen_indexer(j):
                nb_ = j + 1
                qb = qiT[j % 2]
                qk = ("qiT", j % 2)
                DMA("sp", "ld", qb[:], qiT_scr[j].rearrange("d (h t) -> d h t", h=16), ["qiT_scr"], [qk])
                DMA("sp", "ld", bg2[j % 2][:], proj[j, :, 5008:6032], PROJ_ALL(j), [("bg", j % 2)])
                tiles = key_tiles(nb_)
                for h in range(16):
                    for (r, s0, n) in tiles:
                        it = cnt_i[0]
                        cnt_i[0] += 1
                        ak = ("acc", r, s0)
                        pst, hf = banks_i[it % 4]
                        po = pst[:, hf * 512: hf * 512 + n * 128]
                        MM(po, qb[:, h, :], kidxT[:, r, s0 * 128:(s0 + n) * 128], True, True, [qk, "kidxT"], [(id(pst), hf)])
                        rb = rbuf[it % 4]
                        rk = ("rbuf", it % 4)
                        ACT(rb[:, 0:n * 128], po, AF.Relu, [(id(pst), hf), "wabs"], [rk], scale=wabs[:, j, h:h + 1])
                        av = acc[:, r, s0:s0 + n, :]
                        rv = rb[:, 0:n * 128].rearrange("p (n t) -> p n t", n=n)
                        if h == 0:
                            TS(av, rv, wsgn[:, j, h:h + 1], ALU.mult, [rk, "wsgn"], [ak])
                        else:
                            STT(av, rv, wsgn[:, j, h:h + 1], av, ALU.mult, ALU.add, [rk, "wsgn", ak], [ak])
                        yield

            def gen_bisect(j):
                nb_ = j + 1
                bs = bs2[j % 2]
                halfs = halfs2[j % 2]
                bk = "bs%d" % (j % 2)
                AK = [("acc", r, s0) for (r, s0, n) in key_tiles(nb_)]
                av = acc[:, :, 0:nb_, :]
                RED(bs[:, 0:1], av, ALU.max, AK, [bk + "0"], ax=AX.XYZ)
                yield
                RED(bs[:, 1:2], av, ALU.min, AK, [bk + "1"], ax=AX.XYZ)
                yield
                TT(acc[:, :, j, :], acc[:, :, j, :], mb[:], ALU.add, AK + ["mb"], AK + ["accm"])
                yield
                AKM = AK + ["accm"]
                TT(bs[:, 2:3], bs[:, 0:1], bs[:, 1:2], ALU.subtract, [bk + "0", bk + "1"], [bk + "w"])
                yield
                TS(bs[:, 2:3], bs[:, 2:3], 1.0001, ALU.mult, [bk + "w"], [bk + "w"], s2=1e-6, op1=ALU.add)
                yield
                TS(halfs[:], pw[:], bs[:, 2:3], ALU.mult, ["pw", bk + "w"], [bk + "h"])
                yield
                TS(bs[:, 3:4], bs[:, 1:2], halfs[:, 0:1], ALU.add, [bk + "1", bk + "h"], [bk + "thr"], s2=-1.0, op1=ALU.mult)
                yield
                cthr = float(2 * int(TOPK) - nb_ * 1024)
                for i in range(NBIS):
                    ACT(selb[:, :, 0:nb_, :], av, AF.Sign, AKM + [bk + "thr"], ["selb", bk + "cnt"],
                        bias=bs[:, 3:4], scale=1.0, accum_out=bs[:, 4:5])
                    yield
                    TT(bs[:, 7:8], bs[:, 3:4], halfs[:, i + 1:i + 2], ALU.add, [bk + "thr", bk + "h"], [bk + "tm"])
                    yield
                    STT(bs[:, 5:6], bs[:, 4:5], cthr, halfs[:, i:i + 1], ALU.is_ge, ALU.mult, [bk + "cnt", bk + "h"], [bk + "g"])
                    yield
                    TT(bs[:, 3:4], bs[:, 7:8], bs[:, 5:6], ALU.subtract, [bk + "g", bk + "tm"], [bk + "thr"])
                    yield
                TS(bs[:, 6:7], bs[:, 3:4], -1.0, ALU.mult, [bk + "thr", bk + "h"], [bk + "lo"], s2=halfs[:, NBIS:NBIS + 1], op1=ALU.subtract)
                yield
                TS(selb[:, :, 0:nb_, :], av, bs[:, 6:7], ALU.is_ge, AKM + [bk + "lo"], ["selb"])
                yield
                if dbg and j == 1:
                    DMA("sp", "o", dbg_out["d_acc"].rearrange("p (r t) -> p r t", r=8), acc[:, :, 1, :], AKM, [])
                    CP("pool", att[:].rearrange("p (r t) -> p r t", r=8), selb[:, :, 1, :], ["selb"], ["att"])
                    DMA("sp", "o", dbg_out["d_sel"], att[:], ["att"], [])
                    yield

            def gen_transposes(j):
                nb_ = j + 1
                selT = selT2[j % 2]
                it = 0
                for r in range(8):
                    s0 = 0
                    while s0 < nb_:
                        n = min(8, nb_ - s0)
                        pt = PT0 if it % 2 == 0 else PT1
                        pk = "PT%d" % (it % 2)
                        for q in range(n):
                            TR(pt[:, q * 128:(q + 1) * 128], selb[:, r, s0 + q, :], ["selb"], [pk])
                        CP("act", selT[:, r, s0:s0 + n, :],
                           pt[:, 0:n * 128].rearrange("p (n t) -> p n t", n=n), [pk], [("selT", j % 2, r)])
                        s0 += n
                        it += 1
                        yield

            cnt_a = [0]

            def gen_attention(j):
                nb_ = j + 1
                selT = selT2[j % 2]
                bg = bg2[j % 2]
                pieces = key_tiles(nb_)
                npc = len(pieces)
                for pi, (r, s0, n) in enumerate(pieces):
                    kb_ = KTp[pi % 2]
                    vb_ = Vp[pi % 2]
                    kk = ("KTp", pi % 2)
                    vk = ("Vp", pi % 2)
                    DMA("pool", "kv", kb_[:, :, 0:n * 128],
                        gA[r * 128:(r + 1) * 128, 0:8192].rearrange("p (h t) -> p h t", h=8)[:, :, s0 * 128:(s0 + n) * 128],
                        ["gathA"], [kk])
                    DMA("pool", "kv", vb_[:, 0:n, :],
                        gB[r * 1024 + s0 * 128: r * 1024 + (s0 + n) * 128, 1032:2064].rearrange("(n p) c -> p n c", p=128),
                        ["gathB"], [vk])
                    for h in range(8):
                        ia = cnt_a[0]
                        cnt_a[0] += 1
                        lk = (id(PC), ia % 2)
                        pl = PC[:, (ia % 2) * 512:(ia % 2) * 512 + n * 128]
                        for q in range(n):
                            MM(PC[:, (ia % 2) * 512 + q * 128:(ia % 2) * 512 + (q + 1) * 128],
                               kb_[:, h, q * 128:(q + 1) * 128], qT_all[:, h, j * 128:(j + 1) * 128], True, True,
                               [kk, "qT_all"], [lk])
                        eb = et[ia % 3]
                        ek = ("et", ia % 3)
                        ACT(eb[:, 0:n * 128], pl, AF.Exp, [lk], [ek])
                        pb_ = ptt[ia % 3]
                        pk = ("ptt", ia % 3)
                        TT(pb_[:, 0:n, :], eb[:, 0:n * 128].rearrange("p (n t) -> p n t", n=n), selT[:, r, s0:s0 + n, :],
                           ALU.mult, [ek, ("selT", j % 2, r)], [pk])
                        for q in range(n):
                            jl = s0 + q
                            kbw = None
                            if jl == j:
                                kbw = 1 + r
                            elif jl == j - 1 and r == 7:
                                kbw = 0
                            if kbw is not None:
                                TT(pb_[:, q, :], pb_[:, q, :], EB[:, kbw, h, :], ALU.mult, [pk, "EB"], [pk])
                        po_t = PA if h < 4 else PB
                        ok_ = ("PO", h // 4)
                        for q in range(n):
                            MM(po_t[:, (h % 4) * 256:(h % 4) * 256 + 129], pb_[:, q, :], vb_[:, q, h * 129:(h + 1) * 129],
                               q == 0, q == n - 1, [pk, vk], [ok_, (id(po_t), 0), (id(po_t), 1)])
                        yield
                    for g2, po_t in enumerate((PA, PB)):
                        pv = po_t[:].rearrange("p (h c) -> p h c", h=4)[:, :, 0:129]
                        ov = oacc[:, g2 * 4:(g2 + 1) * 4, :]
                        if pi == 0:
                            CP("dve", ov, pv, [("PO", g2), (id(po_t), 0), (id(po_t), 1)], [("oacc", g2)])
                        else:
                            TT(ov, pv, ov, ALU.add, [("PO", g2), (id(po_t), 0), (id(po_t), 1), ("oacc", g2)], [("oacc", g2)])
                        yield
                for h in range(8):
                    ok_ = ("oacc", h // 4)
                    P.op("dve", lambda e, o=rd[:, h:h + 1], i_=oacc[:, h, 128:129]: e.reciprocal(out=o, in_=i_), [ok_], [("rd", h)])
                    TS(att[:, h * 128:(h + 1) * 128], oacc[:, h, 0:128], rd[:, h:h + 1], ALU.mult, [ok_, ("rd", h)], ["att"])
                    yield
                if dbg and j == 1:
                    DMA("sp", "o", dbg_out["d_attn"], att[:], ["att"], [])
                if dbg and j == 0:
                    DMA("sp", "o", dbg_out["d_attn0"], att[:], ["att"], [])
                ACT(bg[:], bg[:], AF.Silu, [("bg", j % 2)], [("bg", j % 2)])
                TT(ybb[:], att[:], bg[:], ALU.mult, ["att", ("bg", j % 2)], ["ybb"])
                yield
                for h in range(8):
                    TR(PT0[:, h * 128:(h + 1) * 128], ybb[:, h * 128:(h + 1) * 128], ["ybb"], ["PT0"])
                CP("act", ybT[:], PT0[:].rearrange("p (h t) -> p h t", h=8), ["PT0"], ["ybT"])
                DMA("sp", "st", ybT_scr.rearrange("p (k t) -> p k t", k=8)[:, :, j * 128:(j + 1) * 128], ybT[:],
                    ["ybT"], ["ybT_scr"])
                yield

            def run(g):
                for _ in g:
                    pass

            def interleave(ga, gb, ra, rb_):
                da = db = False
                while not (da and db):
                    for _ in range(ra):
                        if not da:
                            try:
                                next(ga)
                            except StopIteration:
                                da = True
                    for _ in range(rb_):
                        if not db:
                            try:
                                next(gb)
                            except StopIteration:
                                db = True

            run(gen_indexer(0))
            for j in range(NB):
                if j >= 1:
                    interleave(gen_bisect(j), gen_attention(j - 1), 1, 1)
                else:
                    run(gen_bisect(j))
                run(gen_transposes(j))
                if j + 1 < NB:
                    run(gen_indexer(j + 1))
            run(gen_attention(NB - 1))
            P.barrier()

        with ExitStack() as ph7:
            mT_all = sb(ph7, "mT_all", [128, 16, 1024], BF16)
            with ExitStack() as ph:
                wpa = sb(ph, "wpa", [128, 8, DM], BF16)
                wpb = sb(ph, "wpb", [128, 8, DM], BF16)
                yaT7 = [sb(ph, "yaT7_%d" % i, [128, 8, 128], BF16) for i in range(2)]
                ybT7 = [sb(ph, "ybT7_%d" % i, [128, 8, 128], BF16) for i in range(2)]
                mg = sb(ph, "mg", [128, 4096], F32)
                mrg = sb(ph, "mrg", [128, DM], F32)
                tmp7 = sb(ph, "tmp7", [128, DM], F32)
                mrb = sb(ph, "mrb", [128, DM], BF16)
                DMA("pool", "cp", wpa[:], w_pa.rearrange("(k p) c -> p k c", p=128), [], ["wpa"])
                DMA("pool", "cp", wpb[:], w_pb.rearrange("(k p) c -> p k c", p=128), [], ["wpb"])
                P.alias_last(["wpa", "wpb"])
                for j in range(NB):
                    b = j % 2
                    DMA("sp", "ld", yaT7[b][:], yaT_scr.rearrange("p (k t) -> p k t", k=8)[:, :, j * 128:(j + 1) * 128],
                        ["yaT_scr"], [("yaT7", b)])
                    DMA("sp", "ld", ybT7[b][:], ybT_scr.rearrange("p (k t) -> p k t", k=8)[:, :, j * 128:(j + 1) * 128],
                        ["ybT_scr"], [("ybT7", b)])
                    DMA("sp", "ld", mg[:], proj[j, :, 6032:10128], PROJ_ALL(j), ["mg"])
                    ACT(mg[:], mg[:], AF.Sigmoid, ["mg"], ["mg"])
                    for (wt, yT, wkey, ykey, goff, first) in [(wpa, yaT7[b], "wpa", ("yaT7", b), 0, True),
                                                              (wpb, ybT7[b], "wpb", ("ybT7", b), 2048, False)]:
                        for q4 in range(4):
                            pst, hf = [(PA, 0), (PA, 1), (PB, 0), (PB, 1)][q4]
                            po = pst[:, hf * 512:(hf + 1) * 512]
                            for k in range(8):
                                MM(po, yT[:, k, :], wt[:, k, q4 * 512:(q4 + 1) * 512], k == 0, k == 7,
                                   [wkey, ykey], [(id(pst), hf)])
                            cs = slice(q4 * 512, (q4 + 1) * 512)
                            gs = slice(goff + q4 * 512, goff + (q4 + 1) * 512)
                            if first:
                                TT(mrg[:, cs], po, mg[:, gs], ALU.mult, [(id(pst), hf), "mg"], [("mrg", q4)])
                            else:
                                TT(tmp7[:, cs], po, mg[:, gs], ALU.mult, [(id(pst), hf), "mg"], [("tmp7", q4)])
                                TT(mrb[:, cs], mrg[:, cs], tmp7[:, cs], ALU.add, [("mrg", q4), ("tmp7", q4)], [("mrb", q4)])
                    for half in range(2):
                        pt = PT0 if half == 0 else PT1
                        pk = "PT%d" % half
                        for k in range(8):
                            kc = half * 8 + k
                            TR(pt[:, k * 128:(k + 1) * 128], mrb[:, kc * 128:(kc + 1) * 128], [("mrb", kc // 4)], [pk])
                        CP("act" if half == 0 else "dve", mT_all[:, half * 8:(half + 1) * 8, j * 128:(j + 1) * 128],
                           pt[:].rearrange("p (k t) -> p k t", k=8), [pk], [("mT", j)])
                P.barrier()
            with ExitStack() as ph:
                wo = sb(ph, "wo", [128, 16, DM], BF16)
                fn_bc = sb(ph, "fn_bc", [128, DM], F32)
                xr = [sb(ph, "xr%d" % i, [128, DM], F32) for i in range(2)]
                tmp8 = sb(ph, "tmp8", [128, DM], F32)
                jk8 = sb(ph, "jk8", [128, DM], F32)
                s7 = sb(ph, "s7", [128, 8], F32)
                for k4 in range(4):
                    DMA("pool", "cp", wo[:, k4 * 4:(k4 + 1) * 4, :],
                        w_out[k4 * 512:(k4 + 1) * 512, :].rearrange("(k p) c -> p k c", p=128), [], ["wo"])
                DMA("sp", "c", fn_bc[:], fnw.partition_broadcast(128), [], ["fn_bc"])
                for j in range(NB):
                    b = j % 2
                    DMA("sp", "x", xr[b][:], x_in[j], [], [("xr", b)])
                    for q4 in range(4):
                        pst, hf = [(PC, 0), (PC, 1), (PA, 0), (PA, 1)][q4]
                        po = pst[:, hf * 512:(hf + 1) * 512]
                        for k in range(16):
                            MM(po, mT_all[:, k, j * 128:(j + 1) * 128], wo[:, k, q4 * 512:(q4 + 1) * 512], k == 0, k == 15,
                               [("mT", j), "wo"], [(id(pst), hf)])
                        cs = slice(q4 * 512, (q4 + 1) * 512)
                        TT(tmp8[:, cs], po, xr[b][:, cs], ALU.add, [(id(pst), hf), ("xr", b)], [("tmp8", q4)])
                    T7 = [("tmp8", q) for q in range(4)]
                    ACT(jk8[:], tmp8[:], AF.Square, T7, ["jk8", "s7a"], accum_out=s7[:, 0:1])
                    rstd_from_ss(s7[:, 2:3], s7[:, 0:1], s7[:, 1:2], DM, ["s7a"], ["s7c"], "s7b")
                    STT(xr[b][:], tmp8[:], s7[:, 2:3], fn_bc[:], ALU.mult, ALU.mult, T7 + ["s7c", "fn_bc", ("xr", b)], [("xr", b)])
                    DMA("sp", "o", y_out[j], xr[b][:], [("xr", b)], [("y", j)])
                P.barrier()

        P.finalize()
        for en, meth in [("pe", block.tensor), ("act", block.scalar), ("dve", block.vector),
                         ("pool", block.gpsimd), ("sp", block.sync)]:
            meth(lambda e, en=en: P.emit(en, e))
    return nc, P


def _consts(c):
    ident = np.eye(128, dtype=np.float32)
    s = np.arange(128)[:, None]
    t = np.arange(128)[None, :]
    LT = ((s <= t).astype(np.float32) - (s <= 63).astype(np.float32))
    ind2 = np.stack([(np.arange(128) <= 63).astype(np.float32), np.ones(128, np.float32)], axis=1)
    cmask = (s <= t).astype(np.float32)
    J = np.zeros((128, 128), np.float32)
    J[np.arange(128), 127 - np.arange(128)] = 1.0
    i = np.arange(128)[:, None, None]
    r = np.arange(8)[None, :, None]
    ip = np.arange(128)[None, None, :]
    vis = (2 * r + ip // 64) <= (2 * c + i // 64)
    mbias = np.where(vis, 0.0, NEG).astype(np.float32).reshape(128, 1024)
    v = np.arange(1280)
    bk = _t5_bucket_np(v - 255 - 128 * c)
    OH = np.zeros((32, 1280), np.float32)
    OH[bk, v] = 1.0
    OH[15, :] -= 1.0
    sel = np.zeros((128, 64), np.float32)
    sel[:, np.arange(64) % 8 == c] = 1.0
    return dict(c_ident=ident, c_LT=LT, c_ind2=ind2, c_cmask=cmask, c_J=J, c_mbias=mbias, c_OH=OH, c_sel=sel)


_DBG = False


def kernel(x, norm_w, w_in, lb_table, gnorm_a, q_norm_w, kv_norm_w, w_uq, w_qidx, w_ukv,
           kidx_norm_w, kidx_norm_b, w_pa, w_pb, w_out, rel_bias, final_norm_w):
    f = lambda a: np.ascontiguousarray(np.asarray(a, dtype=np.float32))
    x = f(x)
    xb = x.reshape(8, 8, 128, DM)
    shared = dict(
        w_in=f(w_in)[0], normw=f(norm_w).reshape(1, DM), lbt=f(lb_table), gnorm=f(gnorm_a).reshape(1, 128),
        qnw=f(q_norm_w).reshape(1, 512), kvnw=f(kv_norm_w).reshape(1, 256), w_uq=f(w_uq)[0], w_qidx=f(w_qidx)[0],
        w_ukv=f(w_ukv)[0], kiw=f(kidx_norm_w).reshape(1, 128), kib=f(kidx_norm_b).reshape(1, 128),
        w_pa=f(w_pa)[0], w_pb=f(w_pb)[0], w_out=f(w_out)[0], relb=f(rel_bias), fnw=f(final_norm_w).reshape(1, DM))
    in_maps = []
    for c in range(NCORE):
        m = dict(shared)
        m["x"] = np.ascontiguousarray(xb[:, c])
        m.update(_consts(c))
        in_maps.append(m)
    nc, _ = build_nc(_DBG)
    res = run_bass_kernel_spmd(nc, in_maps, core_ids=list(range(NCORE)))
    out = np.zeros((8, 8, 128, DM), np.float32)
    for c in range(NCORE):
        out[:, c] = res.results[c]["y"]
    if _DBG:
        kernel.dbg = res.results
    return out.reshape(1, 8192, DM)
```
